# Optimizing a Trainium2 kernel written in Bass

```python
import math
import jax, jax.numpy as jnp
from jax import lax
import numpy as np

D_MODEL = 1024
BATCH = 8
SEQ = 4096
DEPTH = 2

N_EVEN = (DEPTH + 1) // 2
N_ODD = DEPTH // 2

MIX_WIDTH = D_MODEL
A_WIDTH = MIX_WIDTH // 2
A_GROUPS = 4
A_GROUP_DIM = A_WIDTH // A_GROUPS
CHUNK = 128
B_WIDTH = MIX_WIDTH - A_WIDTH
POOL_WINDOWS = (2, 4, 8, 16)
B_GROUPS = len(POOL_WINDOWS)
B_GROUP_DIM = B_WIDTH // B_GROUPS
C_HEADS = 4
C_QK_DIM = 64
C_V_DIM = 2 * C_QK_DIM
C_WIDTH = C_HEADS * C_V_DIM
C_QK_WIDTH = C_HEADS * 2 * C_QK_DIM
Q_BLOCK = 128
D_WIDTH = MIX_WIDTH - C_WIDTH
CONV_WIDTH = 3
REL_BUCKETS = 32
REL_MAX_DIST = 128
D_FF = 4 * D_MODEL
EPS = 1e-6

EVEN_IN = 2 * A_WIDTH + B_WIDTH
ODD_IN = 2 * C_QK_WIDTH + C_WIDTH + 3 * D_WIDTH

kernel_name = "hybrid_gmlp_pool_diffattn_shortconv"


def rms_norm(x, g):
    xf = x.astype(jnp.float32)
    y = xf * lax.rsqrt(jnp.mean(xf * xf, axis=-1, keepdims=True) + EPS)
    return (y * g.astype(jnp.float32)).astype(x.dtype)


def layer_norm(x, g, b):
    xf = x.astype(jnp.float32)
    mu = jnp.mean(xf, axis=-1, keepdims=True)
    xc = xf - mu
    y = xc * lax.rsqrt(jnp.mean(xc * xc, axis=-1, keepdims=True) + EPS)
    return (y * g.astype(jnp.float32) + b.astype(jnp.float32)).astype(x.dtype)


def chunked_gmlp(z, ln_g, ln_b, w_s, b_s):
    bsz, s, _ = z.shape
    z = jax.nn.gelu(z)
    u, v = jnp.split(z, 2, axis=-1)
    v = layer_norm(v, ln_g, ln_b)
    v = v.reshape(bsz, s // CHUNK, CHUNK, A_GROUPS, A_GROUP_DIM)
    causal = jnp.tril(jnp.ones((CHUNK, CHUNK), dtype=bool))
    w = jnp.where(causal[None], w_s, 0).astype(v.dtype)
    mixed = jnp.einsum('gts,bnsgc->bntgc', w, v) + b_s.T[None, None, :, :, None].astype(v.dtype)
    return u * mixed.reshape(bsz, s, A_WIDTH)


def multiscale_pool(p, pool_w, pool_scale):
    bsz, s, _ = p.shape
    pos = jnp.arange(s)
    outs = []
    for g_in, w in zip(jnp.split(p, B_GROUPS, axis=-1), POOL_WINDOWS):
        gf = g_in.astype(jnp.float32)
        c = jnp.cumsum(gf, axis=1)
        shifted = jnp.pad(c[:, :s - w], ((0, 0), (w, 0), (0, 0)))
        count = jnp.minimum(pos + 1, w).astype(jnp.float32)[None, :, None]
        outs.append(((c - shifted) / count - gf).astype(p.dtype))
    pooled = jnp.stack(outs, axis=2)
    y = jnp.einsum('bsgc,gcd->bsgd', pooled, pool_w).reshape(bsz, s, B_WIDTH)
    return y * pool_scale


def t5_bucket(dist):
    max_exact = REL_BUCKETS // 2
    nf = jnp.maximum(dist, 1).astype(jnp.float32)
    large = max_exact + (jnp.log(nf / max_exact) / math.log(REL_MAX_DIST / max_exact)
                         * (REL_BUCKETS - max_exact)).astype(jnp.int32)
    large = jnp.minimum(large, REL_BUCKETS - 1)
    return jnp.where(dist < max_exact, dist, large)


def diff_attention(q, k, v, rel_table, lam, lambda_init, subln_g):
    bsz, s = q.shape[0], q.shape[1]
    q = q.transpose(0, 2, 3, 1, 4) * (C_QK_DIM ** -0.5)
    k = k.transpose(0, 2, 3, 1, 4)
    v = v.transpose(0, 2, 1, 3)
    outs = []
    for blk in range(s // Q_BLOCK):
        q0 = blk * Q_BLOCK
        end = q0 + Q_BLOCK
        logits = jnp.einsum('bhmqd,bhmkd->bhmqk', q[:, :, :, q0:end], k[:, :, :, :end]).astype(jnp.float32)
        dist = jnp.arange(q0, end)[:, None] - jnp.arange(end)[None, :]
        bias = rel_table[t5_bucket(jnp.maximum(dist, 0))].astype(jnp.float32)
        logits = logits + bias.transpose(2, 0, 1)[None, :, None]
        logits = jnp.where(dist >= 0, logits, -jnp.inf)
        p = jax.nn.softmax(logits, axis=-1)
        a = p[:, :, 0] - lam * p[:, :, 1]
        outs.append(jnp.einsum('bhqk,bhkd->bhqd', a.astype(v.dtype), v[:, :, :end]))
    o = jnp.concatenate(outs, axis=2)
    o = rms_norm(o, subln_g) * (1.0 - lambda_init)
    return o.transpose(0, 2, 1, 3).reshape(bsz, s, C_WIDTH)


def short_gated_conv(bg, cg, xin, conv_w):
    z = cg * xin
    s = z.shape[1]
    y = conv_w[CONV_WIDTH - 1] * z
    for j in range(CONV_WIDTH - 1):
        shift = CONV_WIDTH - 1 - j
        y = y + conv_w[j] * jnp.pad(z[:, :s - shift], ((0, 0), (shift, 0), (0, 0)))
    return bg * y


def even_mixer(h, w_in, ln_g, ln_b, w_s, b_s, pool_w, pool_scale, w_out):
    proj = h @ w_in
    ya = chunked_gmlp(proj[..., :2 * A_WIDTH], ln_g, ln_b, w_s, b_s)
    yb = multiscale_pool(proj[..., 2 * A_WIDTH:], pool_w, pool_scale)
    return jnp.concatenate([ya, yb], axis=-1) @ w_out


def odd_mixer(h, w_in, rel_table, lam_params, subln_g, conv_w, w_out, lambda_init):
    bsz, s, _ = h.shape
    proj = h @ w_in
    splits = np.cumsum([C_QK_WIDTH, C_QK_WIDTH, C_WIDTH, D_WIDTH, D_WIDTH]).tolist()
    q, k, v, bg, cg, xin = jnp.split(proj, splits, axis=-1)
    q = q.reshape(bsz, s, C_HEADS, 2, C_QK_DIM)
    k = k.reshape(bsz, s, C_HEADS, 2, C_QK_DIM)
    v = v.reshape(bsz, s, C_HEADS, C_V_DIM)
    lp = lam_params.astype(jnp.float32)
    lam = jnp.exp(jnp.sum(lp[0] * lp[1])) - jnp.exp(jnp.sum(lp[2] * lp[3])) + lambda_init
    yc = diff_attention(q, k, v, rel_table, lam, lambda_init, subln_g)
    yd = short_gated_conv(bg, cg, xin, conv_w)
    return jnp.concatenate([yc, yd], axis=-1) @ w_out


def squared_relu_mlp(h, w1, w2):
    return jnp.square(jax.nn.relu(h @ w1)) @ w2


def setup_inputs(seed: int = 0) -> dict:
    key = jax.random.key(seed)
    ks = jax.random.split(key, 20)
    nrm = lambda k, shape, scale: jax.random.normal(k, shape, jnp.float32) * scale
    gain = lambda k, shape: 1.0 + 0.05 * jax.random.normal(k, shape, jnp.float32)
    return {
        "x": nrm(ks[0], (BATCH, SEQ, D_MODEL), 1.0),
        "rel_bias_table": nrm(ks[1], (REL_BUCKETS, C_HEADS), 0.5),
        "norm_g": gain(ks[2], (DEPTH, 4, D_MODEL)),
        "even_w_in": nrm(ks[3], (N_EVEN, D_MODEL, EVEN_IN), D_MODEL ** -0.5),
        "even_ln_g": gain(ks[4], (N_EVEN, A_WIDTH)),
        "even_ln_b": nrm(ks[5], (N_EVEN, A_WIDTH), 0.02),
        "even_spatial_w": nrm(ks[6], (N_EVEN, A_GROUPS, CHUNK, CHUNK), CHUNK ** -0.5),
        "even_spatial_b": gain(ks[7], (N_EVEN, A_GROUPS, CHUNK)),
        "even_pool_w": nrm(ks[8], (N_EVEN, B_GROUPS, B_GROUP_DIM, B_GROUP_DIM), B_GROUP_DIM ** -0.5),
        "even_pool_scale": gain(ks[9], (N_EVEN, B_WIDTH)),
        "even_w_out": nrm(ks[10], (N_EVEN, MIX_WIDTH, D_MODEL), MIX_WIDTH ** -0.5),
        "odd_w_in": nrm(ks[11], (N_ODD, D_MODEL, ODD_IN), D_MODEL ** -0.5),
        "odd_lambda": nrm(ks[12], (N_ODD, 4, C_QK_DIM), 0.1),
        "odd_subln_g": gain(ks[13], (N_ODD, C_V_DIM)),
        "odd_conv_w": nrm(ks[14], (N_ODD, CONV_WIDTH, D_WIDTH), CONV_WIDTH ** -0.5),
        "odd_w_out": nrm(ks[15], (N_ODD, MIX_WIDTH, D_MODEL), MIX_WIDTH ** -0.5),
        "ffn_w1": nrm(ks[16], (DEPTH, D_MODEL, D_FF), D_MODEL ** -0.5),
        "ffn_w2": nrm(ks[17], (DEPTH, D_FF, D_MODEL), D_FF ** -0.5),
    }


def reference(x, rel_bias_table, norm_g, even_w_in, even_ln_g, even_ln_b, even_spatial_w,
              even_spatial_b, even_pool_w, even_pool_scale, even_w_out, odd_w_in, odd_lambda,
              odd_subln_g, odd_conv_w, odd_w_out, ffn_w1, ffn_w2):
    h = x
    for layer in range(DEPTH):
        g = norm_g[layer]
        hn = rms_norm(h, g[0])
        if layer % 2 == 0:
            e = layer // 2
            y = even_mixer(hn, even_w_in[e], even_ln_g[e], even_ln_b[e], even_spatial_w[e],
                           even_spatial_b[e], even_pool_w[e], even_pool_scale[e], even_w_out[e])
        else:
            o = layer // 2
            lambda_init = 0.8 - 0.6 * math.exp(-0.3 * layer)
            y = odd_mixer(hn, odd_w_in[o], rel_bias_table, odd_lambda[o], odd_subln_g[o],
                          odd_conv_w[o], odd_w_out[o], lambda_init)
        h = h + rms_norm(y, g[1])
        y = squared_relu_mlp(rms_norm(h, g[2]), ffn_w1[layer], ffn_w2[layer])
        h = h + rms_norm(y, g[3])
    return h
```

```python
import math
import contextlib
import numpy as np
import concourse.bass as bass
import concourse.mybir as mybir
from concourse.bass_utils import run_bass_kernel_spmd

F32 = mybir.dt.float32
BF16 = mybir.dt.bfloat16
AF = mybir.ActivationFunctionType
ALU = mybir.AluOpType
AX = mybir.AxisListType

D = 1024
SEQ = 4096
T = 512
P = 128
DC = 8
TC = 4
DFF = 4096
EPS = 1e-6
NEG = -30000.0
LAMBDA_INIT = 0.8 - 0.6 * math.exp(-0.3 * 1)
RING = 4

ENGS = ("pe", "act", "dve", "pool", "sp")
STRICT_OWN = True


class Sched:
    def __init__(self, nc, serial=False):
        self.nc = nc
        self.serial = serial
        self.ops = {e: [] for e in ENGS}
        self.cnt = {e: 0 for e in ENGS}
        self.dcnt = {}
        self.known = {e: {} for e in ENGS}
        self.tok_w = {}
        self.tok_r = {}
        self.last = None
        self.tag = ''
        self.tags = {e: [] for e in ENGS}

    def _collect(self, reads, writes):
        deps = {}

        def add(k, v):
            if deps.get(k, -1) < v:
                deps[k] = v
        for t in reads:
            for k, v in self.tok_w.get(t, {}).items():
                add(k, v)
        for t in writes:
            for k, v in self.tok_w.get(t, {}).items():
                add(k, v)
            for k, v in self.tok_r.get(t, {}).items():
                add(k, v)
        if self.serial and self.last is not None:
            add(*self.last)
        return deps

    def _commit(self, me, reads, writes):
        for t in writes:
            self.tok_w[t] = {me[0]: me[1]}
            self.tok_r[t] = {}
        for t in reads:
            r = self.tok_r.setdefault(t, {})
            if r.get(me[0], -1) < me[1]:
                r[me[0]] = me[1]
        self.last = me

    def _waits(self, eng, deps):
        waits = []
        kn = self.known[eng]
        for k, v in deps.items():
            if k == ("e", eng):
                if eng == "pe" or (v < self.cnt[eng] and not STRICT_OWN):
                    continue
            if kn.get(k, -1) >= v:
                continue
            kn[k] = v
            waits.append((k, v))
        return waits

    def op(self, eng, fn, reads=(), writes=(), inc=True):
        psr = [t for t in reads if t.startswith("ps")]
        if psr:
            reads = [t for t in reads if not t.startswith("ps")]
            writes = list(writes) + [t for t in psr if t not in writes]
        deps = self._collect(reads, writes)
        waits = self._waits(eng, deps)
        idx = self.cnt[eng] + 1
        if inc:
            self.cnt[eng] = idx
        me = (("e", eng), idx)
        self.ops[eng].append((waits, fn, ("e", eng) if inc else None))
        self.tags[eng].append(self.tag)
        self._commit(me, reads, writes)

    def dma(self, eng, sem, fn, reads=(), writes=()):
        deps = self._collect(reads, writes)
        waits = self._waits(eng, deps)
        v = self.dcnt.get(sem, 0) + 16
        self.dcnt[sem] = v
        me = (("d", sem), v)
        self.ops[eng].append((waits, fn, ("d", sem)))
        self._commit(me, reads, writes)

    def fence(self, tokens, sem):
        for t in tokens:
            self.tok_w[t] = {("d", sem): self.dcnt[sem]}

    def alias(self, tokens):
        dep = {("e", e): self.cnt[e] for e in ENGS if e != "sp" and self.cnt[e] > 0}
        for t in tokens:
            self.tok_w[t] = dict(dep)
            self.tok_r[t] = {}

    def wait_all(self, eng):
        deps = {}
        for e in ENGS:
            if e != "sp" and e != eng and self.cnt[e] > 0:
                deps[("e", e)] = self.cnt[e]
        for s, v in self.dcnt.items():
            deps[("d", s)] = v
        self.ops[eng].append((list(deps.items()), None, None))

    def emit(self):
        nc = self.nc
        with contextlib.ExitStack() as st:
            sems = {}
            for e in ENGS:
                if e != "sp":
                    sems[("e", e)] = st.enter_context(nc.semaphore("P_" + e))
            for s in self.dcnt:
                sems[("d", s)] = st.enter_context(nc.semaphore("D_" + s))
            block = st.enter_context(nc.Block())

            def run(engname, engobj):
                for waits, fn, incspec in self.ops[engname]:
                    for k, v in waits:
                        engobj.wait_ge(sems[k], v)
                    if fn is None:
                        continue
                    ins = fn(engobj)
                    if incspec is not None:
                        ins.then_inc(sems[incspec], 16 if incspec[0] == "d" else 1)

            @block.tensor
            def _(e):
                run("pe", e)

            @block.scalar
            def _(e):
                run("act", e)

            @block.vector
            def _(e):
                run("dve", e)

            @block.gpsimd
            def _(e):
                run("pool", e)

            @block.sync
            def _(e):
                run("sp", e)


def _t5_bucket_np(dist):
    max_exact = 16
    nf = np.maximum(dist, 1).astype(np.float32)
    large = max_exact + (np.log(nf / max_exact) / math.log(128 / max_exact) * (32 - max_exact)).astype(np.int32)
    large = np.minimum(large, 31)
    return np.where(dist < max_exact, dist, large)


def host_consts():
    c = {}
    c["c_ident"] = np.eye(128, dtype=np.float32)
    c["c_J"] = np.ascontiguousarray(np.eye(128, dtype=np.float32)[::-1])
    c["c_tri"] = np.tril(np.ones((128, 128), dtype=np.float32))
    u = np.arange(383)
    dist = u - 127
    bucket = _t5_bucket_np(np.maximum(dist, 0))
    oh = np.zeros((32, 383), dtype=np.float32)
    for j in range(383):
        if dist[j] >= 0:
            oh[bucket[j], j] = 1.0
    c["c_oh"] = oh
    ic = np.zeros((128, 4, 16), dtype=np.float32)
    for g, w in enumerate((2, 4, 8, 16)):
        ic[:, g, :] = float(w) / np.minimum(np.arange(16) + 1, w).astype(np.float32)
    c["c_invcnt"] = ic.reshape(128, 64)
    return c


def MM(out, lhsT, rhs, start=True, stop=True, **kw):
    return lambda e: e.matmul(out, lhsT=lhsT, rhs=rhs, start=start, stop=stop, **kw)


def TR(out, in_, ident):
    return lambda e: e.transpose(out, in_, ident)


def ACT(out, in_, func, **kw):
    return lambda e: e.activation(out=out, in_=in_, func=func, **kw)


def TT(out, in0, in1, op):
    return lambda e: e.tensor_tensor(out=out, in0=in0, in1=in1, op=op)


def TS(out, in0, s1, s2, op0, op1=None):
    if op1 is None:
        return lambda e: e.tensor_scalar(out=out, in0=in0, scalar1=s1, scalar2=None, op0=op0)
    return lambda e: e.tensor_scalar(out=out, in0=in0, scalar1=s1, scalar2=s2, op0=op0, op1=op1)


def STT(out, in0, scalar, in1, op0, op1):
    return lambda e: e.scalar_tensor_tensor(out=out, in0=in0, scalar=scalar, in1=in1, op0=op0, op1=op1)


def CP(out, in_):
    return lambda e: e.tensor_copy(out=out, in_=in_)


def MS(ap, val):
    return lambda e: e.memset(ap, val)


def RCP(out, in_):
    return lambda e: e.reciprocal(out=out, in_=in_)


def RSUM(out, in_):
    return lambda e: e.reduce_sum(out=out, in_=in_, axis=AX.X)


def DMA(out, in_):
    return lambda e: e.dma_start(out=out, in_=in_)


def build(NT=SEQ // T, serial=False, stop_after=4):
    nc = bass.Bass("TRN2", target_bir_lowering=False)
    SEQL = NT * T
    NKB = NT * TC

    def din(name, shape, dt=F32):
        return nc.dram_tensor(name, shape, dt, kind="ExternalInput").ap()

    x = din("x", [SEQL, D])
    out = nc.dram_tensor("out", [SEQL, D], F32, kind="ExternalOutput").ap()
    w_in0 = din("even_w_in", [D, 1536])
    w_out0 = din("even_w_out", [D, D])
    w_in1 = din("odd_w_in", [D, 3072])
    w_out1 = din("odd_w_out", [D, D])
    w1 = [din("ffn_w1_%d" % l, [D, DFF]) for l in range(2)]
    w2 = [din("ffn_w2_%d" % l, [DFF, D]) for l in range(2)]
    norm_g = din("norm_g", [64, 128])
    ln_g = din("even_ln_g", [1, 512])
    ln_b = din("even_ln_b", [1, 512])
    sp_w = din("even_spatial_w", [4, 128, 128])
    sp_b = din("even_spatial_b", [1, 512])
    pool_w = din("even_pool_w", [4, 128, 128])
    pool_s = din("even_pool_scale", [4, 128])
    lam_in = din("odd_lambda", [1, 256])
    subln = din("odd_subln_g", [1, 128])
    conv_w = din("odd_conv_w", [12, 128])
    rel_tab = din("rel_bias_table", [32, 4])
    c_ident = din("c_ident", [128, 128])
    c_J = din("c_J", [128, 128])
    c_tri = din("c_tri", [128, 128])
    c_oh = din("c_oh", [32, 383])
    c_invcnt = din("c_invcnt", [128, 64])

    def dscr(name, shape, dt=BF16):
        return nc.dram_tensor(name, shape, dt).ap()

    s_in0 = dscr("s_in0", [D, 1536])
    s_out0 = dscr("s_out0", [D, D])
    s_in1 = dscr("s_in1", [D, 3072])
    s_out1 = dscr("s_out1", [D, D])
    s_w1 = [dscr("s_w1_%d" % l, [D, DFF]) for l in range(2)]
    s_w2 = [dscr("s_w2_%d" % l, [DFF, D]) for l in range(2)]
    rline = dscr("rline", [4, 384], F32)

    S = Sched(nc, serial=serial)

    with contextlib.ExitStack() as st:
        def sb(name, shape, dt):
            return st.enter_context(nc.sbuf_tensor(name, shape, dt))

        hT = sb("hT", [P, DC, T], F32)
        hm = sb("hm", [P, 2, DC, T], BF16)
        hn = hm[:, 0]
        yst = sb("yst", [P, DC, T], F32)
        mix = hm[:, 1]
        rstd = sb("rstd", [P, T], F32)
        sqb = sb("sqb", [P, T], F32)
        UB = 37920
        U = sb("U", [P, UB // 2], BF16)
        KT = sb("KT", [P, 4, SEQL], BF16)
        V = sb("V", [P, NKB, 4, 130], BF16)
        BT = sb("BT", [P, 4, 256], F32)
        wring = sb("wring", [P, RING, DC, 512], BF16)
        onesD = sb("onesD", [P, P], BF16)
        ident_f = sb("ident_f", [P, P], F32)
        ident_b = sb("ident_b", [P, P], BF16)
        Jt = sb("Jt", [P, P], F32)
        tri = sb("tri", [P, P], F32)
        stage = sb("stage", [P, P], F32)
        cols = sb("cols", [P, 89], F32)
        onesDV = sb("onesDV", [P, P], BF16)
        ones_b1 = sb("ones_b1", [P, P], BF16)
        lnsc = sb("lnsc", [P, 32], F32)
        bsb = sb("bsb", [P, 512], F32)
        sublng = sb("sublng", [P, P], F32)
        lamraw = sb("lamraw", [P, 256], F32)
        lamt = sb("lamt", [P, 8], F32)
        tab = sb("tab", [32, 4], F32)
        oh = sb("oh", [32, 384], F32)
        line = sb("line", [4, 384], F32)
        Wh = sb("Wh", [P, 256], F32)
        spw = sb("spw", [P, 4, P], F32)
        WsT = sb("WsT", [P, 4, P], BF16)
        poolw = sb("poolw", [P, 4, P], BF16)
        invcnt = sb("invcnt", [P, 4, 16], F32)
        phalo = sb("phalo", [P, 4, 16], F32)
        zhalo = sb("zhalo", [P, 4, 2], F32)
        small = sb("small", [P, 16], F32)

        PS = [st.enter_context(nc.psum_tensor("ps%d" % b, [P, 512], F32)) for b in range(8)]
        PSB = PS[7][:].bitcast(BF16)

        def uview(off, nbytes, dt):
            assert off % 4 == 0 and off + nbytes <= UB
            v = U[:, off // 2:(off + nbytes) // 2]
            if dt == F32:
                v = v.bitcast(F32)
            return v

        r4 = lambda v: v.rearrange("p (a b) -> p a b", a=4)
        uT = r4(uview(0, 8192, F32))
        vtm = [uview(8192 + 2048 * i, 2048, F32) for i in range(2)]
        vn = r4(uview(12288, 4096, BF16))
        pT = r4(uview(16384, 8448, F32))
        wsA = uview(24832, 2112, F32)
        wsB = uview(26944, 2112, F32)
        pooled = r4(uview(29056, 4096, BF16))
        hid = uview(0, 32768, BF16).rearrange("p (a b) -> p a b", a=32)
        QT = r4(uview(0, 4096, BF16))
        cgT = r4(uview(4096, 8192, F32))
        zT = r4(uview(12288, 8224, F32))
        Et = [[uview(20512 + 1024 * (2 * m + b), 1024, BF16) for b in range(2)] for m in range(2)]
        accS = [uview(24608 + 2048 * m, 2048, F32) for m in range(2)]
        rcT = [uview(32800 + 2048 * m, 2048, F32) for m in range(2)]
        sqT = uview(36896, 1024, BF16)
        rcs = sb("rcs", [P, 16], F32)
        sm = lambda i: small[:, i:i + 1]
        ALLY = ["yst%d" % c for c in range(DC)]

        def cload(dst, src, tok):
            S.dma("sp", "c", DMA(dst, src), writes=[tok])

        cload(ident_f[:], c_ident[:, :], "ident_f")
        cload(Jt[:], c_J[:, :], "Jt")
        cload(tri[:], c_tri[:, :], "tri")
        cload(oh[:, 0:383], c_oh[:, :], "oh")
        cload(invcnt[:].rearrange("p a b -> p (a b)"), c_invcnt[:, :], "invcnt")
        cload(stage[0:64, :], norm_g[:, :], "stage")
        cload(stage[64:68, :], pool_s[:, :], "stage")
        cload(stage[68:80, :], conv_w[:, :], "stage")
        cload(stage[80:84, :], ln_g.rearrange("o (g c) -> (o g) c", g=4), "stage")
        cload(stage[84:88, :], ln_b.rearrange("o (g c) -> (o g) c", g=4), "stage")
        cload(stage[88:89, :], subln[:, :], "stage")
        cload(bsb[:], sp_b.partition_broadcast(P), "bsb")
        cload(sublng[:], subln.partition_broadcast(P), "sublng")
        cload(lamraw[:], lam_in.partition_broadcast(P), "lamraw")
        cload(tab[:], rel_tab[:, :], "tab")
        cload(spw[:], sp_w.rearrange("g t s -> t g s"), "spw")
        S.fence(["ident_f", "Jt", "tri", "oh", "invcnt", "stage", "bsb", "sublng", "lamraw", "tab", "spw"], "c")
        S.dma("pool", "c2", DMA(poolw[:], pool_w.rearrange("g c d -> c g d")), writes=["poolw"])

        tile_blocks = []
        for c0 in (1024, 512, 0):
            tile_blocks.append((w_in0[:, c0:c0 + 512], s_in0[:, c0:c0 + 512]))
        for c0 in (0, 512):
            tile_blocks.append((w_out0[:, c0:c0 + 512], s_out0[:, c0:c0 + 512]))
        for b in range(8):
            tile_blocks.append((w1[0][:, b * 512:(b + 1) * 512], s_w1[0][:, b * 512:(b + 1) * 512]))
        for half in range(2):
            for kg in range(4):
                tile_blocks.append((w2[0][kg * 1024:(kg + 1) * 1024, half * 512:(half + 1) * 512],
                                    s_w2[0][kg * 1024:(kg + 1) * 1024, half * 512:(half + 1) * 512]))
        for c0 in (2048, 2560, 0, 512, 1536, 1024):
            tile_blocks.append((w_in1[:, c0:c0 + 512], s_in1[:, c0:c0 + 512]))
        for c0 in (0, 512):
            tile_blocks.append((w_out1[:, c0:c0 + 512], s_out1[:, c0:c0 + 512]))
        for b in range(8):
            tile_blocks.append((w1[1][:, b * 512:(b + 1) * 512], s_w1[1][:, b * 512:(b + 1) * 512]))
        for half in range(2):
            for kg in range(4):
                tile_blocks.append((w2[1][kg * 1024:(kg + 1) * 1024, half * 512:(half + 1) * 512],
                                    s_w2[1][kg * 1024:(kg + 1) * 1024, half * 512:(half + 1) * 512]))
        NBLK = len(tile_blocks)
        S.op("pool", MS(onesD[:], 1.0 / D), writes=["onesD"])
        S.op("pool", MS(onesDV[:], 1.0 / 128), writes=["onesDV"])
        S.op("pool", MS(ones_b1[:], 1.0), writes=["ones_b1"])
        S.op("pool", MS(V[:, :, :, 128:130], 1.0), writes=["Vones"])
        S.op("pool", MS(phalo[:], 0.0), writes=["phalo"])
        S.op("pool", MS(zhalo[:], 0.0), writes=["zhalo"])
        S.op("pool", MS(line[:], NEG), writes=["line"])
        S.op("pool", MS(small[:, 4:5], EPS), writes=["epsc"])
        cstate = {"n": 0}

        def cast_issue_upto(n):
            while cstate["n"] < min(n, NBLK):
                bi = cstate["n"]
                src, dst = tile_blocks[bi]
                S.dma("pool", "k%d" % bi, DMA(dst, src), writes=["S%d" % bi])
                cstate["n"] += 1


        S.op("dve", CP(ident_b[:], ident_f[:]), reads=["ident_f"], writes=["ident_b"])
        S.op("pe", TR(PS[7][:, 0:89], stage[0:89, :], ident_f[0:89, 0:89]), reads=["stage", "ident_f"], writes=["ps7"])
        S.op("dve", CP(cols[:], PS[7][:, 0:89]), reads=["ps7"], writes=["cols0"])
        S.op("dve", TS(cols[:, 88:89], cols[:, 88:89], 1.0 - LAMBDA_INIT, None, ALU.mult), reads=["cols0"], writes=["cols"])
        gcol = lambda l, n, c: cols[:, (l * 4 + n) * 8 + c:(l * 4 + n) * 8 + c + 1]
        pscol = lambda j: cols[:, 64 + j:65 + j]
        cwcol = lambda k, j: cols[:, 68 + k * 4 + j:69 + k * 4 + j]
        lngcol = lambda g: cols[:, 80 + g:81 + g]
        for g in range(4):
            S.op("dve", TT(spw[:, g, :], spw[:, g, :], tri[:], ALU.mult), reads=["spw", "tri"], writes=["spw"])
        for g in range(4):
            S.op("pe", TR(PS[6][:, g * P:(g + 1) * P], spw[:, g, :], ident_f[:]), reads=["spw", "ident_f"], writes=["ps6"])
        S.op("dve", CP(WsT[:].rearrange("p a b -> p (a b)"), PS[6][:]), reads=["ps6"], writes=["WsT"])
        S.op("dve", TS(lnsc[:, 24:28], cols[:, 84:88], float(D), None, ALU.mult), reads=["cols"], writes=["lnb1k"])
        for g in range(4):
            S.op("pe", MM(PS[6][:, g * P:(g + 1) * P], onesD[:], WsT[:, g, :]), reads=["onesD", "WsT"], writes=["ps6"], inc=(g == 3))
        for g in range(4):
            S.op("dve", STT(bsb[:, g * P:(g + 1) * P], PS[6][:, g * P:(g + 1) * P], lnsc[:, 24 + g:25 + g], bsb[:, g * P:(g + 1) * P], ALU.mult, ALU.add),
                 reads=["ps6", "lnb1k", "bsb"], writes=["bsb"])
        S.op("pe", MM(PS[5][0:4, 0:383], tab[:, :], oh[:, 0:383]), reads=["tab", "oh"], writes=["ps5"])
        S.op("dve", CP(small[0:4, 0:1], PS[5][0:4, 382:383]), reads=["ps5"], writes=["small0"])
        S.op("dve", TS(line[:, 127:383], PS[5][0:4, 127:383], small[0:4, 0:1], 8.0, ALU.subtract, ALU.mult),
             reads=["ps5", "small0", "line"], writes=["line"])
        S.dma("sp", "c3", DMA(rline[:, :], line[:]), reads=["line"], writes=["rline"])
        for h in range(4):
            src = bass.AP(rline.tensor, h * 384, [[1, P], [1, 256]])
            S.dma("sp", "c4", DMA(BT[:, h, :], src), reads=["rline"], writes=["BTraw"])
        S.fence(["BTraw"], "c4")

        def finish_bias_tiles():
            for h in range(4):
                S.op("pe", MM(PS[7][:, 0:256], Jt[:], BT[:, h, :]), reads=["Jt", "BTraw", "BT"], writes=["ps7"])
                S.op("dve", CP(BT[:, h, :], PS[7][:, 0:256]), reads=["ps7"], writes=["BT", "BTraw"])

        S.op("dve", TT(lamraw[:, 0:64], lamraw[:, 0:64], lamraw[:, 64:128], ALU.mult), reads=["lamraw"], writes=["lamraw"])
        S.op("dve", TT(lamraw[:, 128:192], lamraw[:, 128:192], lamraw[:, 192:256], ALU.mult), reads=["lamraw"], writes=["lamraw"])
        S.op("dve", RSUM(lamt[:, 0:1], lamraw[:, 0:64]), reads=["lamraw"], writes=["lamt"])
        S.op("dve", RSUM(lamt[:, 1:2], lamraw[:, 128:192]), reads=["lamraw", "lamt"], writes=["lamt"])
        S.op("act", ACT(lamt[:, 2:4], lamt[:, 0:2], AF.Exp), reads=["lamt"], writes=["lamt"])
        S.op("dve", TT(lamt[:, 4:5], lamt[:, 3:4], lamt[:, 2:3], ALU.subtract), reads=["lamt"], writes=["lamt"])
        S.op("dve", TS(small[:, 2:3], lamt[:, 4:5], -LAMBDA_INIT, None, ALU.add), reads=["lamt"], writes=["nlam"])
        S.op("dve", TS(sublng[:], sublng[:], 1.0 - LAMBDA_INIT, None, ALU.mult), reads=["sublng"], writes=["sublng"])
        nlam = small[:, 2:3]

        blocks = []
        for i in range(NT):
            for bi, (src, dst) in enumerate(tile_blocks):
                blocks.append((dst, "S%d" % bi))
        wstate = {"issued": 0, "used": 0}

        def w_issue_upto(n):
            while wstate["issued"] < min(n, len(blocks)):
                k = wstate["issued"]
                slot = k % RING
                src, tok = blocks[k]
                if k < NBLK:
                    f32src, scr = tile_blocks[k]
                    S.dma("pool", "wc%d" % slot, DMA(wring[:, slot], f32src.rearrange("(c p) n -> p c n", p=P)), writes=["w%d" % slot])
                    S.dma("sp", "k%d" % k, DMA(scr.rearrange("(c p) n -> p c n", p=P), wring[:, slot]), reads=["w%d" % slot], writes=["S%d" % k])
                else:
                    sv = src.rearrange("(c p) n -> p c n", p=P)
                    S.dma("sp", "w%d" % slot, DMA(wring[:, slot], sv), reads=[tok], writes=["w%d" % slot])
                wstate["issued"] += 1

        def w_next():
            k = wstate["used"]
            w_issue_upto(k + RING)
            wstate["used"] += 1
            slot = k % RING
            return wring[:, slot], "w%d" % slot

        rot = {"i": 0}

        def nextbank(n=6):
            b = rot["i"] % n
            rot["i"] += 1
            return b

        def stats_and_rstd():
            for c in range(DC):
                S.op("pe", MM(PS[6][:], onesD[:], hn[:, c, :], start=(c == 0), stop=(c == DC - 1)),
                     reads=["onesD", "hn%d" % c], writes=["ps6"], inc=(c == DC - 1))
            S.op("act", ACT(sqb[:], PS[6][:], AF.Ln, bias=EPS, scale=1.0), reads=["ps6"], writes=["sqb"])
            S.op("act", ACT(rstd[:], sqb[:], AF.Exp, scale=-0.5), reads=["sqb"], writes=["rstd"])

        def prenorm(l, n, have_squares=False):
            for c in range(DC):
                if not have_squares:
                    S.op("act", ACT(hn[:, c, :], hT[:, c, :], AF.Square), reads=["hT%d" % c], writes=["hn%d" % c])
            stats_and_rstd()
            for c in range(DC):
                S.op("dve", STT(hn[:, c, :], hT[:, c, :], gcol(l, n, c), rstd[:], ALU.mult, ALU.mult),
                     reads=["hT%d" % c, "cols", "rstd"], writes=["hn%d" % c])

        def evac_post(b, c, l, n):
            S.op("act", ACT(hn[:, c, :], PS[b][:], AF.Square), reads=["ps%d" % b], writes=["hn%d" % c])
            S.op("act", ACT(yst[:, c, :], PS[b][:], AF.Copy, scale=gcol(l, n, c)), reads=["ps%d" % b, "cols"], writes=["yst%d" % c])

        def postnorm(l, n):
            stats_and_rstd()
            for c in range(DC):
                S.op("dve", TT(yst[:, c, :], yst[:, c, :], rstd[:], ALU.mult), reads=["yst%d" % c, "rstd"], writes=["yst%d" % c])
                S.op("dve", TT(hT[:, c, :], hT[:, c, :], yst[:, c, :], ALU.add),
                     reads=["yst%d" % c, "hT%d" % c], writes=["hT%d" % c])

        def mm_fm(wblk, wtok, src, src_tok, j, b, nk=DC, start=True, stop=True, koff=0):
            for k in range(nk):
                S.op("pe", MM(PS[b][:], wblk[:, k, j * P:(j + 1) * P], src[:, koff + k, :],
                              start=(start and k == 0), stop=(stop and k == nk - 1)),
                     reads=[wtok, "%s%d" % (src_tok, koff + k)], writes=["ps%d" % b], inc=(k == nk - 1))

        def mm_fm_kouter(wblk, wtok, src, src_tok, banks, korder=None, before_last=None):
            korder = list(range(DC)) if korder is None else korder
            for i, k in enumerate(korder):
                if i == DC - 1 and before_last is not None:
                    before_last()
                for j, b in enumerate(banks):
                    S.op("pe", MM(PS[b][:], wblk[:, k, j * P:(j + 1) * P], src[:, k, :], start=(i == 0), stop=(i == DC - 1)),
                         reads=[wtok, "%s%d" % (src_tok, k)], writes=["ps%d" % b], inc=(i == DC - 1))

        def mm_tm(wblk, wtok, tc, b):
            for k in range(DC):
                S.op("pe", MM(PS[b][:], hn[:, k, tc * P:(tc + 1) * P], wblk[:, k, :], start=(k == 0), stop=(k == DC - 1)),
                     reads=[wtok, "hn%d" % k], writes=["ps%d" % b], inc=(k == DC - 1))

        def wout_and_post(l, korder, before_last=None):
            for half in range(2):
                wblk, wtok = w_next()
                if half == 0:
                    banks = [nextbank() for j in range(4)]
                    mm_fm_kouter(wblk, wtok, mix, "mix", banks, korder, before_last)
                    for j in range(4):
                        evac_post(banks[j], j, l, 1)
                else:
                    for j in range(4):
                        b = nextbank()
                        mm_fm(wblk, wtok, mix, "mix", j, b)
                        evac_post(b, 4 + j, l, 1)
            postnorm(l, 1)

        def ffn(l, after_w2=None):
            for c in range(DC):
                S.op("act", ACT(hn[:, c, :], hT[:, c, :], AF.Copy, scale=gcol(l, 2, c)), reads=["hT%d" % c, "cols"], writes=["hn%d" % c])
                S.op("act", ACT(mix[:, c, :], hT[:, c, :], AF.Square), reads=["hT%d" % c], writes=["mix%d" % c])
            S.alias(["hid%d" % j for j in range(32)])
            for blk in range(8):
                wblk, wtok = w_next()
                banks = [nextbank() for j in range(4)]
                if blk == 0:
                    mm_fm_kouter(wblk, wtok, hn, "hn", banks)
                for j in range(4):
                    hc = blk * 4 + j
                    b = banks[j]
                    if blk != 0:
                        mm_fm(wblk, wtok, hn, "hn", j, b)
                    r = hc % DC
                    S.op("act", ACT(yst[:, r, :], PS[b][:], AF.Relu), reads=["ps%d" % b], writes=["yst%d" % r])
                    eng = "pool" if hc % 4 == 3 else "dve"
                    S.op(eng, TT(hid[:, hc, :], yst[:, r, :], yst[:, r, :], ALU.mult), reads=["yst%d" % r], writes=["hid%d" % hc])
                if blk == 1:
                    for c in range(DC):
                        S.op("pe", MM(PS[6][:], onesD[:], mix[:, c, :], start=(c == 0), stop=(c == DC - 1)),
                             reads=["onesD", "mix%d" % c], writes=["ps6"], inc=(c == DC - 1))
                    S.op("act", ACT(rstd[:], PS[6][:], AF.Square, bias=1e-3 * EPS, scale=1e-3), reads=["ps6"], writes=["rstd"])
            for half in range(2):
                banks = [half * 4 + j for j in range(4)]
                for kg in range(4):
                    wblk, wtok = w_next()
                    for j in range(4):
                        mm_fm(wblk, wtok, hid, "hid", j, banks[j], nk=8, start=(kg == 0), stop=(kg == 3), koff=kg * 8)
                if half == 1 and after_w2 is not None:
                    after_w2()
                for j in range(4):
                    evac_post(banks[j], half * 4 + j, l, 3)
            for c in range(DC):
                S.op("pe", MM(PS[6][:], onesD[:], hn[:, c, :], start=(c == 0), stop=(c == DC - 1)),
                     reads=["onesD", "hn%d" % c], writes=["ps6"], inc=(c == DC - 1))
            S.op("dve", TT(sqb[:], PS[6][:], rstd[:], ALU.add), reads=["ps6", "rstd"], writes=["sqb"])
            S.op("act", ACT(sqb[:], sqb[:], AF.Ln), reads=["sqb"], writes=["sqb"])
            S.op("act", ACT(rstd[:], sqb[:], AF.Exp, scale=-0.5), reads=["sqb"], writes=["rstd"])
            for c in range(DC):
                S.op("dve", TT(yst[:, c, :], yst[:, c, :], rstd[:], ALU.mult), reads=["yst%d" % c, "rstd"], writes=["yst%d" % c])
                S.op("dve", TT(hT[:, c, :], hT[:, c, :], yst[:, c, :], ALU.add),
                     reads=["yst%d" % c, "hT%d" % c], writes=["hT%d" % c])

        ALLHM = ["hn%d" % c for c in range(DC)] + ["mix%d" % c for c in range(DC)]
        xv = U[:, 0:8192].bitcast(F32).rearrange("p (a d) -> p a d", a=4)
        ov = hm[:].rearrange("p a c t -> p (a c t)").bitcast(F32).rearrange("p (a d) -> p a d", a=4)

        def xload(it):
            t0 = it * T
            S.alias(["xbuf"])
            S.dma("sp", "x", DMA(xv, x[t0:t0 + T, :].rearrange("(a p) d -> p a d", p=P)), writes=["xbuf"])

        S.tag = 'xload'
        xload(0)
        for it in range(NT):
            t0 = it * T
            S.tag = 'xload'
            for c in range(DC):
                b = nextbank()
                for a in range(4):
                    S.op("pe", TR(PS[b][:, a * P:(a + 1) * P], xv[:, a, c * P:(c + 1) * P], ident_f[:]),
                         reads=["xbuf", "ident_f"], writes=["ps%d" % b], inc=(a == 3))
                if c % 2 == 0:
                    S.op("dve", CP(hT[:, c, :], PS[b][:]), reads=["ps%d" % b], writes=["hT%d" % c])
                else:
                    S.op("act", ACT(hT[:, c, :], PS[b][:], AF.Copy), reads=["ps%d" % b], writes=["hT%d" % c])

            S.tag = 'L0.pre'
            prenorm(0, 0)
            S.tag = 'L0.mix'
            S.alias(["uT", "vtm0", "vtm1", "vn0", "vn1", "vn2", "vn3", "pT", "wsA", "wsB", "pooled"])
            wblk, wtok = w_next()
            S.op("pool", CP(pT[:, :, 0:16], phalo[:]), reads=["phalo"], writes=["pT"])
            banks = [nextbank() for j in range(4)]
            mm_fm_kouter(wblk, wtok, hn, "hn", banks)
            for j in range(4):
                S.op("act", ACT(pT[:, j, 16:16 + T], PS[banks[j]][:], AF.Copy), reads=["ps%d" % banks[j]], writes=["pT"])
            S.op("pool", CP(phalo[:], pT[:, :, T:T + 16]), reads=["pT"], writes=["phalo"])
            for j in range(4):
                W_ = 2 ** (j + 1)
                cur, curtok = pT[:, j, :], "pT"
                bufs = [(wsA, "wsA"), (wsB, "wsB")]
                sh, lvl = 1, 0
                while sh < W_:
                    dst, dtok = bufs[lvl % 2]
                    lo = 2 * sh - 1
                    S.op("pool", TT(dst[:, lo:16 + T], cur[:, lo:16 + T], cur[:, lo - sh:16 + T - sh], ALU.add), reads=[curtok], writes=[dtok])
                    cur, curtok = dst, dtok
                    sh *= 2
                    lvl += 1
                if it == 0:
                    S.op("pool", TT(cur[:, 16:32], cur[:, 16:32], invcnt[:, j, :], ALU.mult), reads=[curtok, "invcnt"], writes=[curtok])
                if j < 3:
                    S.op("dve", STT(pooled[:, j, :], cur[:, 16:16 + T], 1.0 / W_, pT[:, j, 16:16 + T], ALU.mult, ALU.subtract), reads=[curtok, "pT"], writes=["pooled"])
                else:
                    pool_last = (cur, curtok, W_)
            wblk, wtok = w_next()
            vbanks = [nextbank() for tc in range(TC)]
            for k in range(DC):
                for tc in range(TC):
                    S.op("pe", MM(PS[vbanks[tc]][:], hn[:, k, tc * P:(tc + 1) * P], wblk[:, k, :], start=(k == 0), stop=(k == DC - 1)),
                         reads=[wtok, "hn%d" % k], writes=["ps%d" % vbanks[tc]], inc=(k == DC - 1))
            for tc in range(TC):
                b = vbanks[tc]
                S.op("act", ACT(yst[:, tc, :], PS[b][:], AF.Gelu_apprx_tanh, accum_out=lnsc[:, tc:tc + 1]), reads=["ps%d" % b], writes=["yst%d" % tc, "lns1_%d" % tc])
                S.op("act", ACT(vn[:, tc, :], yst[:, tc, :], AF.Square, accum_out=lnsc[:, 4 + tc:5 + tc]), reads=["yst%d" % tc], writes=["vn%d" % tc, "lns2_%d" % tc])
            LNS = ["lns1_%d" % t for t in range(4)] + ["lns2_%d" % t for t in range(4)]
            S.op("dve", TS(lnsc[:, 8:12], lnsc[:, 0:4], 1.0 / 512, None, ALU.mult), reads=LNS, writes=["lnmu"])
            S.op("dve", TT(lnsc[:, 12:16], lnsc[:, 8:12], lnsc[:, 8:12], ALU.mult), reads=["lnmu"], writes=["lnvar"])
            S.op("dve", STT(lnsc[:, 12:16], lnsc[:, 4:8], 1.0 / 512, lnsc[:, 12:16], ALU.mult, ALU.subtract), reads=LNS + ["lnvar"], writes=["lnvar"])
            S.op("act", ACT(lnsc[:, 20:24], lnsc[:, 12:16], AF.Ln, bias=EPS, scale=1.0), reads=["lnvar"], writes=["lnln"])
            S.op("act", ACT(lnsc[:, 12:16], lnsc[:, 20:24], AF.Exp, scale=-0.5), reads=["lnln"], writes=["lnvar"])
            S.op("dve", STT(lnsc[:, 16:20], lnsc[:, 8:12], -1.0, lnsc[:, 12:16], ALU.mult, ALU.mult), reads=["lnmu", "lnvar"], writes=["lnnmr"])
            for tc in range(TC):
                S.op("dve", TS(vn[:, tc, :], yst[:, tc, :], lnsc[:, 12 + tc:13 + tc], lnsc[:, 16 + tc:17 + tc], ALU.mult, ALU.add),
                     reads=["yst%d" % tc, "lnvar", "lnnmr"], writes=["vn%d" % tc])
            wblk, wtok = w_next()
            for j in range(4):
                b = nextbank()
                mm_fm(wblk, wtok, hn, "hn", j, b)
                S.op("act", ACT(uT[:, j, :], PS[b][:], AF.Gelu_apprx_tanh), reads=["ps%d" % b], writes=["uT"])
            for tc in range(TC):
                gb = 6 + (tc % 2)
                for g in range(4):
                    S.op("pe", MM(PS[gb][:, g * P:(g + 1) * P], vn[:, tc, g * P:(g + 1) * P], WsT[:, g, :]),
                         reads=["vn%d" % tc, "WsT"], writes=["ps%d" % gb], inc=(g == 3))
                vt = vtm[tc % 2]
                for g in range(4):
                    S.op("dve", STT(vt[:, g * P:(g + 1) * P], PS[gb][:, g * P:(g + 1) * P], lngcol(g), bsb[:, g * P:(g + 1) * P], ALU.mult, ALU.add),
                         reads=["ps%d" % gb, "bsb", "cols"], writes=["vtm%d" % (tc % 2)])
                S.op("dve", TT(mix[:, 0:4, tc * P:(tc + 1) * P], uT[:, :, tc * P:(tc + 1) * P], r4(vt), ALU.mult),
                     reads=["vtm%d" % (tc % 2), "uT"], writes=["mix%d" % c for c in range(4)])
            cur, curtok, W_ = pool_last
            S.op("dve", STT(pooled[:, 3, :], cur[:, 16:16 + T], 1.0 / W_, pT[:, 3, 16:16 + T], ALU.mult, ALU.subtract), reads=[curtok, "pT"], writes=["pooled"])
            for j in range(4):
                b = nextbank()
                S.op("pe", MM(PS[b][:], poolw[:, j, :], pooled[:, j, :]), reads=["poolw", "pooled"], writes=["ps%d" % b])
                S.op("act", ACT(mix[:, 4 + j, :], PS[b][:], AF.Copy, scale=pscol(j)), reads=["ps%d" % b, "cols"], writes=["mix%d" % (4 + j)])
            S.tag = 'L0.wout'
            wout_and_post(0, [0, 1, 2, 3, 4, 5, 6, 7])
            S.tag = 'F0'
            if stop_after >= 2:
                ffn(0)
            else:
                [w_next() for _ in range(16)]

            if stop_after >= 3:
                S.tag = 'L1.pre'
                prenorm(1, 0)
                S.tag = 'L1.proj'
                S.alias(["QT%d" % h for h in range(4)] + ["cgT", "zT", "E00", "E01", "E10", "E11", "accS0", "accS1", "rcT0", "rcT1", "sqT"])
                wblk, wtok = w_next()
                banks = [nextbank(4) for j in range(4)]
                mm_fm_kouter(wblk, wtok, hn, "hn", banks)
                for j in range(4):
                    S.op("act", ACT(cgT[:, j, :], PS[banks[j]][:], AF.Copy), reads=["ps%d" % banks[j]], writes=["cgT"])
                wblk, wtok = w_next()
                S.op("pool", CP(zT[:, :, 0:2], zhalo[:]), reads=["zhalo"], writes=["zT"])
                for j in range(4):
                    b = nextbank(4)
                    mm_fm(wblk, wtok, hn, "hn", j, b)
                    S.op("dve", TT(zT[:, j, 2:2 + T], PS[b][:], cgT[:, j, :], ALU.mult), reads=["ps%d" % b, "cgT"], writes=["zT"])
                S.op("pool", CP(zhalo[:], zT[:, :, T:T + 2]), reads=["zT"], writes=["zhalo"])
                for j in range(4):
                    S.op("dve", TS(cgT[:, j, :], zT[:, j, 2:2 + T], cwcol(2, j), None, ALU.mult), reads=["zT", "cols"], writes=["cgT"])
                    S.op("dve", STT(cgT[:, j, :], zT[:, j, 1:1 + T], cwcol(1, j), cgT[:, j, :], ALU.mult, ALU.add), reads=["zT", "cols", "cgT"], writes=["cgT"])
                    S.op("dve", STT(cgT[:, j, :], zT[:, j, 0:T], cwcol(0, j), cgT[:, j, :], ALU.mult, ALU.add), reads=["zT", "cols", "cgT"], writes=["cgT"])
                wblk, wtok = w_next()
                for h in range(4):
                    b = nextbank(4)
                    mm_fm(wblk, wtok, hn, "hn", h, b)
                    S.op("act", ACT(QT[:, h, :], PS[b][:], AF.Copy), reads=["ps%d" % b], writes=["QT%d" % h])
                wblk, wtok = w_next()
                for h in range(4):
                    b = nextbank(4)
                    mm_fm(wblk, wtok, hn, "hn", h, b)
                    S.op("act", ACT(KT[:, h, t0:t0 + T], PS[b][:], AF.Copy), reads=["ps%d" % b], writes=["KT%d_%d" % (it, h)])
                wblk_bg, wtok_bg = w_next()
                for j in range(4):
                    b = nextbank(4)
                    mm_fm(wblk_bg, wtok_bg, hn, "hn", j, b)
                    S.op("dve", TT(mix[:, 4 + j, :], PS[b][:], cgT[:, j, :], ALU.mult), reads=["ps%d" % b, "cgT"], writes=["mix%d" % (4 + j)])
                wblk, wtok = w_next()
                for tc in range(TC):
                    b = nextbank(4)
                    mm_tm(wblk, wtok, tc, b)
                    kb = it * TC + tc
                    S.op("act", ACT(V[:, kb, :, 0:128], r4(PS[b][:]), AF.Copy), reads=["ps%d" % b, "Vones"], writes=["V%d" % kb])

                S.tag = 'L1.attn'
                if it == 0:
                    finish_bias_tiles()
                nkb = (it + 1) * TC

                est = {"n": 0}
                sublncol = cols[:, 88:89]

                def scores(h, kb):
                    r = kb - it * TC
                    q0 = max(r, 0)
                    nq = TC - q0
                    ncol = nq * P
                    eb = est["n"] % 2
                    est["n"] += 1
                    for m in range(2):
                        sbk = 2 * m + eb
                        S.op("pe", MM(PS[sbk][:, 0:ncol], KT[m * 64:(m + 1) * 64, h, kb * P:(kb + 1) * P],
                                      QT[m * 64:(m + 1) * 64, h, q0 * P:q0 * P + ncol]),
                             reads=["KT%d_%d" % (kb // TC, h), "QT%d" % h], writes=["ps%d" % sbk])
                        for a in range(nq):
                            rel = it * TC + q0 + a - kb
                            if rel in (0, 1):
                                S.op("dve", TT(PS[sbk][:, a * P:(a + 1) * P], PS[sbk][:, a * P:(a + 1) * P], BT[:, h, rel * P:(rel + 1) * P], ALU.add),
                                     reads=["BT", "ps%d" % sbk], writes=["ps%d" % sbk])
                        S.op("act", ACT(Et[m][eb][:, 0:ncol], PS[sbk][:, 0:ncol], AF.Exp, scale=0.125), reads=["ps%d" % sbk], writes=["E%d%d" % (m, eb)])
                    return eb, q0, nq

                def av(h, kb, eb, q0, nq):
                    ncol = nq * P
                    for m in range(2):
                        S.op("pe", MM(PS[4 + m][:, q0 * P:T], V[:, kb, h, 0:128], Et[m][eb][:, 0:ncol],
                                      start=(kb == 0), stop=(kb == nkb - 1), skip_group_check=True),
                             reads=["E%d%d" % (m, eb), "V%d" % kb], writes=["ps%d" % (4 + m)], inc=(m == 1))
                    for m in range(2):
                        S.op("pe", MM(PS[6 + m][:, q0 * P:T], ones_b1[:], Et[m][eb][:, 0:ncol],
                                      start=(kb == 0), stop=(kb == nkb - 1), skip_group_check=True),
                             reads=["E%d%d" % (m, eb), "ones_b1"], writes=["ps%d" % (6 + m)], inc=(m == 1))

                def head_end(h):
                    S.op("dve", CP(accS[0][:, :], PS[4][:]), reads=["ps4"], writes=["accS0"])
                    S.op("act", ACT(accS[1][:, :], PS[5][:], AF.Copy), reads=["ps5"], writes=["accS1"])
                    S.op("dve", CP(rcT[0][:, :], PS[6][:]), reads=["ps6"], writes=["rcT0"])
                    S.op("act", ACT(rcT[1][:, :], PS[7][:], AF.Copy), reads=["ps7"], writes=["rcT1"])

                def fin_a(h):
                    for m in range(2):
                        S.op("act", ACT(rcT[m][:, :], rcT[m][:, :], AF.Ln), reads=["rcT%d" % m], writes=["rcT%d" % m])
                        S.op("act", ACT(rcT[m][:, :], rcT[m][:, :], AF.Exp, scale=-1.0), reads=["rcT%d" % m], writes=["rcT%d" % m])

                def fin_b(h):
                    S.op("dve", TT(accS[0][:, :], accS[0][:, :], rcT[0][:, :], ALU.mult), reads=["accS0", "rcT0"], writes=["accS0"])
                    S.op("dve", TT(accS[1][:, :], accS[1][:, :], rcT[1][:, :], ALU.mult), reads=["accS1", "rcT1"], writes=["accS1"])
                    S.op("dve", STT(accS[0][:, :], accS[1][:, :], nlam, accS[0][:, :], ALU.mult, ALU.add), reads=["accS0", "accS1", "nlam"], writes=["accS0"])
                    S.op("dve", TT(sqT[:, :], accS[0][:, :], accS[0][:, :], ALU.mult), reads=["accS0"], writes=["sqT"])

                def fin_c(h):
                    S.op("pe", MM(PS[0][:], onesDV[:], sqT[:, :]), reads=["onesDV", "sqT"], writes=["ps0"])
                    S.op("act", ACT(rcT[0][:, :], PS[0][:], AF.Ln, bias=EPS, scale=1.0), reads=["ps0"], writes=["rcT0"])
                    S.op("act", ACT(rcT[0][:, :], rcT[0][:, :], AF.Exp, scale=-0.5), reads=["rcT0"], writes=["rcT0"])
                    S.op("dve", STT(mix[:, h, :], accS[0][:, :], sublncol, rcT[0][:, :], ALU.mult, ALU.mult), reads=["accS0", "cols", "rcT0"], writes=["mix%d" % h])

                for h in range(4):
                    cur = scores(h, 0)
                    for kb in range(nkb):
                        nxt = scores(h, kb + 1) if kb + 1 < nkb else None
                        av(h, kb, *cur)
                        cur = nxt
                        if h > 0 and kb == 0:
                            fin_a(h - 1)
                        if h > 0 and kb == 1:
                            fin_b(h - 1)
                        if h > 0 and kb == 2:
                            fin_c(h - 1)
                    head_end(h)
                fin_a(3)
                fin_b(3)
                fin_c(3)
                S.tag = 'L1.wout'
                wout_and_post(1, [4, 5, 6, 7, 0, 1, 2, 3])
            else:
                [w_next() for _ in range(8)]
            S.tag = 'F1'
            if stop_after >= 4:
                ffn(1, after_w2=(lambda: xload(it + 1)) if it + 1 < NT else None)
            else:
                [w_next() for _ in range(16)]

            S.tag = 'store'
            for c in range(DC):
                b = nextbank()
                for a in range(4):
                    S.op("pe", TR(PS[b][:, a * P:(a + 1) * P], hT[:, c, a * P:(a + 1) * P], ident_f[:]),
                         reads=["hT%d" % c, "ident_f"], writes=["ps%d" % b], inc=(a == 3))
                if c % 2 == 0:
                    S.op("dve", CP(ov[:, :, c * P:(c + 1) * P], r4(PS[b][:])), reads=["ps%d" % b], writes=ALLHM)
                else:
                    S.op("act", ACT(ov[:, :, c * P:(c + 1) * P], r4(PS[b][:]), AF.Copy), reads=["ps%d" % b], writes=ALLHM)
            S.dma("sp", "o", DMA(out[t0:t0 + T, :].rearrange("(a p) d -> p a d", p=P), ov), reads=ALLHM)

        S.wait_all("sp")
        S.emit()
    return nc, S


_CACHE = {}


def kernel(**inputs):
    n = 8
    x = np.ascontiguousarray(inputs["x"], dtype=np.float32)
    common = dict(host_consts())
    f = lambda k: np.ascontiguousarray(inputs[k], dtype=np.float32)
    common["even_w_in"] = f("even_w_in")[0]
    common["even_w_out"] = f("even_w_out")[0]
    common["odd_w_in"] = f("odd_w_in")[0]
    common["odd_w_out"] = f("odd_w_out")[0]
    for l in range(2):
        common["ffn_w1_%d" % l] = f("ffn_w1")[l]
        common["ffn_w2_%d" % l] = f("ffn_w2")[l]
    common["norm_g"] = f("norm_g").reshape(64, 128)
    common["even_ln_g"] = f("even_ln_g").reshape(1, 512)
    common["even_ln_b"] = f("even_ln_b").reshape(1, 512)
    common["even_spatial_w"] = f("even_spatial_w")[0]
    common["even_spatial_b"] = f("even_spatial_b").reshape(1, 512)
    common["even_pool_w"] = f("even_pool_w")[0]
    common["even_pool_scale"] = f("even_pool_scale").reshape(4, 128)
    common["odd_lambda"] = f("odd_lambda").reshape(1, 256)
    common["odd_subln_g"] = f("odd_subln_g").reshape(1, 128)
    common["odd_conv_w"] = f("odd_conv_w").reshape(12, 128)
    common["rel_bias_table"] = f("rel_bias_table")
    if "nc" not in _CACHE:
        _CACHE["nc"] = build()[0]
    nc = _CACHE["nc"]
    in_maps = []
    for b in range(n):
        m = dict(common)
        m["x"] = x[b]
        in_maps.append(m)
    res = run_bass_kernel_spmd(nc, in_maps, core_ids=list(range(n)))
    return np.stack([np.asarray(r["out"], dtype=np.float32) for r in res.results], axis=0)
```

```python
import math
import contextlib
import numpy as np
import concourse.bass as bass
import concourse.mybir as mybir
from concourse.bass_utils import run_bass_kernel_spmd

F32 = mybir.dt.float32
BF16 = mybir.dt.bfloat16
AF = mybir.ActivationFunctionType
ALU = mybir.AluOpType
AX = mybir.AxisListType

D = 1024
SEQ = 4096
T = 512
P = 128
DC = 8
TC = 4
DFF = 4096
EPS = 1e-6
NEG = -30000.0
LAMBDA_INIT = 0.8 - 0.6 * math.exp(-0.3 * 1)
RING = 4
NFILL = 4

ENGS = ("pe", "act", "dve", "pool", "sp")
STRICT_OWN = True


class Sched:
    def __init__(self, nc, serial=False):
        self.nc = nc
        self.serial = serial
        self.ops = {e: [] for e in ENGS}
        self.cnt = {e: 0 for e in ENGS}
        self.dcnt = {}
        self.known = {e: {} for e in ENGS}
        self.tok_w = {}
        self.tok_r = {}
        self.last = None
        self.tag = ''
        self.tags = {e: [] for e in ENGS}

    def _collect(self, reads, writes):
        deps = {}

        def add(k, v):
            if deps.get(k, -1) < v:
                deps[k] = v
        for t in reads:
            for k, v in self.tok_w.get(t, {}).items():
                add(k, v)
        for t in writes:
            for k, v in self.tok_w.get(t, {}).items():
                add(k, v)
            for k, v in self.tok_r.get(t, {}).items():
                add(k, v)
        if self.serial and self.last is not None:
            add(*self.last)
        return deps

    def _commit(self, me, reads, writes):
        for t in writes:
            self.tok_w[t] = {me[0]: me[1]}
            self.tok_r[t] = {}
        for t in reads:
            r = self.tok_r.setdefault(t, {})
            if r.get(me[0], -1) < me[1]:
                r[me[0]] = me[1]
        self.last = me

    def _waits(self, eng, deps):
        waits = []
        kn = self.known[eng]
        for k, v in deps.items():
            if k == ("e", eng):
                if eng == "pe" or (v < self.cnt[eng] and not STRICT_OWN):
                    continue
            if kn.get(k, -1) >= v:
                continue
            kn[k] = v
            waits.append((k, v))
        return waits

    def op(self, eng, fn, reads=(), writes=(), inc=True):
        psr = [t for t in reads if t.startswith("ps")]
        if psr:
            reads = [t for t in reads if not t.startswith("ps")]
            writes = list(writes) + [t for t in psr if t not in writes]
        deps = self._collect(reads, writes)
        waits = self._waits(eng, deps)
        idx = self.cnt[eng] + 1
        if inc:
            self.cnt[eng] = idx
        me = (("e", eng), idx)
        self.ops[eng].append((waits, fn, ("e", eng) if inc else None))
        self.tags[eng].append(self.tag)
        self._commit(me, reads, writes)

    def dma(self, eng, sem, fn, reads=(), writes=()):
        deps = self._collect(reads, writes)
        waits = self._waits(eng, deps)
        v = self.dcnt.get(sem, 0) + 16
        self.dcnt[sem] = v
        me = (("d", sem), v)
        self.ops[eng].append((waits, fn, ("d", sem)))
        self._commit(me, reads, writes)

    def fence(self, tokens, sem):
        for t in tokens:
            self.tok_w[t] = {("d", sem): self.dcnt[sem]}

    def alias(self, tokens):
        dep = {("e", e): self.cnt[e] for e in ENGS if e != "sp" and self.cnt[e] > 0}
        for t in tokens:
            self.tok_w[t] = dict(dep)
            self.tok_r[t] = {}

    def wait_all(self, eng):
        deps = {}
        for e in ENGS:
            if e != "sp" and e != eng and self.cnt[e] > 0:
                deps[("e", e)] = self.cnt[e]
        for s, v in self.dcnt.items():
            deps[("d", s)] = v
        self.ops[eng].append((list(deps.items()), None, None))

    def emit(self):
        nc = self.nc
        with contextlib.ExitStack() as st:
            sems = {}
            for e in ENGS:
                if e != "sp":
                    sems[("e", e)] = st.enter_context(nc.semaphore("P_" + e))
            for s in self.dcnt:
                sems[("d", s)] = st.enter_context(nc.semaphore("D_" + s))
            block = st.enter_context(nc.Block())

            def run(engname, engobj):
                for waits, fn, incspec in self.ops[engname]:
                    for k, v in waits:
                        engobj.wait_ge(sems[k], v)
                    if fn is None:
                        continue
                    ins = fn(engobj)
                    if incspec is not None:
                        ins.then_inc(sems[incspec], 16 if incspec[0] == "d" else 1)

            @block.tensor
            def _(e):
                run("pe", e)

            @block.scalar
            def _(e):
                run("act", e)

            @block.vector
            def _(e):
                run("dve", e)

            @block.gpsimd
            def _(e):
                run("pool", e)

            @block.sync
            def _(e):
                run("sp", e)


def _t5_bucket_np(dist):
    max_exact = 16
    nf = np.maximum(dist, 1).astype(np.float32)
    large = max_exact + (np.log(nf / max_exact) / math.log(128 / max_exact) * (32 - max_exact)).astype(np.int32)
    large = np.minimum(large, 31)
    return np.where(dist < max_exact, dist, large)


def host_consts():
    c = {}
    c["c_ident"] = np.eye(128, dtype=np.float32)
    c["c_J"] = np.ascontiguousarray(np.eye(128, dtype=np.float32)[::-1])
    c["c_tri"] = np.tril(np.ones((128, 128), dtype=np.float32))
    u = np.arange(383)
    dist = u - 127
    bucket = _t5_bucket_np(np.maximum(dist, 0))
    oh = np.zeros((32, 383), dtype=np.float32)
    for j in range(383):
        if dist[j] >= 0:
            oh[bucket[j], j] = 1.0
    c["c_oh"] = oh
    ic = np.zeros((128, 4, 16), dtype=np.float32)
    for g, w in enumerate((2, 4, 8, 16)):
        ic[:, g, :] = float(w) / np.minimum(np.arange(16) + 1, w).astype(np.float32)
    c["c_invcnt"] = ic.reshape(128, 64)
    return c


def MM(out, lhsT, rhs, start=True, stop=True, **kw):
    return lambda e: e.matmul(out, lhsT=lhsT, rhs=rhs, start=start, stop=stop, **kw)


def TR(out, in_, ident):
    return lambda e: e.transpose(out, in_, ident)


def ACT(out, in_, func, **kw):
    return lambda e: e.activation(out=out, in_=in_, func=func, **kw)


def TT(out, in0, in1, op):
    return lambda e: e.tensor_tensor(out=out, in0=in0, in1=in1, op=op)


def TS(out, in0, s1, s2, op0, op1=None):
    if op1 is None:
        return lambda e: e.tensor_scalar(out=out, in0=in0, scalar1=s1, scalar2=None, op0=op0)
    return lambda e: e.tensor_scalar(out=out, in0=in0, scalar1=s1, scalar2=s2, op0=op0, op1=op1)


def STT(out, in0, scalar, in1, op0, op1):
    return lambda e: e.scalar_tensor_tensor(out=out, in0=in0, scalar=scalar, in1=in1, op0=op0, op1=op1)


def CP(out, in_):
    return lambda e: e.tensor_copy(out=out, in_=in_)


def MS(ap, val):
    return lambda e: e.memset(ap, val)


def RCP(out, in_):
    return lambda e: e.reciprocal(out=out, in_=in_)


def RSUM(out, in_):
    return lambda e: e.reduce_sum(out=out, in_=in_, axis=AX.X)


def DMA(out, in_):
    return lambda e: e.dma_start(out=out, in_=in_)


def build(NT=SEQ // T, serial=False, stop_after=4):
    nc = bass.Bass("TRN2", target_bir_lowering=False)
    SEQL = NT * T
    NKB = NT * TC

    def din(name, shape, dt=F32):
        return nc.dram_tensor(name, shape, dt, kind="ExternalInput").ap()

    x = din("x", [SEQL, D])
    out = nc.dram_tensor("out", [SEQL, D], F32, kind="ExternalOutput").ap()
    w_in0 = din("even_w_in", [D, 1536])
    w_out0 = din("even_w_out", [D, D])
    w_in1 = din("odd_w_in", [D, 3072])
    w_out1 = din("odd_w_out", [D, D])
    w1 = [din("ffn_w1_%d" % l, [D, DFF]) for l in range(2)]
    w2 = [din("ffn_w2_%d" % l, [DFF, D]) for l in range(2)]
    norm_g = din("norm_g", [64, 128])
    ln_g = din("even_ln_g", [1, 512])
    ln_b = din("even_ln_b", [1, 512])
    sp_w = din("even_spatial_w", [4, 128, 128])
    sp_b = din("even_spatial_b", [1, 512])
    pool_w = din("even_pool_w", [4, 128, 128])
    pool_s = din("even_pool_scale", [4, 128])
    lam_in = din("odd_lambda", [1, 256])
    subln = din("odd_subln_g", [1, 128])
    conv_w = din("odd_conv_w", [12, 128])
    rel_tab = din("rel_bias_table", [32, 4])
    c_ident = din("c_ident", [128, 128])
    c_J = din("c_J", [128, 128])
    c_tri = din("c_tri", [128, 128])
    c_oh = din("c_oh", [32, 383])
    c_invcnt = din("c_invcnt", [128, 64])

    def dscr(name, shape, dt=BF16):
        return nc.dram_tensor(name, shape, dt).ap()

    s_in0 = dscr("s_in0", [D, 1536])
    s_out0 = dscr("s_out0", [D, D])
    s_in1 = dscr("s_in1", [D, 3072])
    s_out1 = dscr("s_out1", [D, D])
    s_w1 = [dscr("s_w1_%d" % l, [D, DFF]) for l in range(2)]
    s_w2 = [dscr("s_w2_%d" % l, [DFF, D]) for l in range(2)]
    rline = dscr("rline", [4, 384], F32)

    S = Sched(nc, serial=serial)

    with contextlib.ExitStack() as st:
        def sb(name, shape, dt):
            return st.enter_context(nc.sbuf_tensor(name, shape, dt))

        hT = sb("hT", [P, DC, T], F32)
        hm = sb("hm", [P, 2, DC, T], BF16)
        hn = hm[:, 0]
        yst = sb("yst", [P, DC, T], F32)
        mix = hm[:, 1]
        rstd = sb("rstd", [P, T], F32)
        sqb = sb("sqb", [P, T], F32)
        UB = 35328
        U = sb("U", [P, UB // 2], BF16)
        KT = sb("KT", [P, 4, SEQL], BF16)
        V = sb("V", [P, NKB, 4, 130], BF16)
        BT = sb("BT", [P, 4, 256], F32)
        wring = sb("wring", [P, RING, DC, 512], BF16)
        onesD = sb("onesD", [P, P], BF16)
        ident_f = sb("ident_f", [P, P], F32)
        ident_b = sb("ident_b", [P, P], BF16)
        Jt = sb("Jt", [P, P], F32)
        tri = sb("tri", [P, P], F32)
        stage = sb("stage", [P, P], F32)
        cols = sb("cols", [P, 88], F32)
        lnsc = sb("lnsc", [P, 32], F32)
        bsb = sb("bsb", [P, 512], F32)
        sublng = sb("sublng", [P, P], F32)
        lamraw = sb("lamraw", [P, 256], F32)
        lamt = sb("lamt", [P, 8], F32)
        tab = sb("tab", [32, 4], F32)
        oh = sb("oh", [32, 384], F32)
        line = sb("line", [4, 384], F32)
        Wh = sb("Wh", [P, 256], F32)
        spw = sb("spw", [P, 4, P], F32)
        WsT = sb("WsT", [P, 4, P], BF16)
        poolw = sb("poolw", [P, 4, P], BF16)
        invcnt = sb("invcnt", [P, 4, 16], F32)
        phalo = sb("phalo", [P, 4, 16], F32)
        zhalo = sb("zhalo", [P, 4, 2], F32)
        small = sb("small", [P, 16], F32)

        PS = [st.enter_context(nc.psum_tensor("ps%d" % b, [P, 512], F32)) for b in range(8)]
        PSB = PS[7][:].bitcast(BF16)

        def uview(off, nbytes, dt):
            assert off % 4 == 0 and off + nbytes <= UB
            v = U[:, off // 2:(off + nbytes) // 2]
            if dt == F32:
                v = v.bitcast(F32)
            return v

        r4 = lambda v: v.rearrange("p (a b) -> p a b", a=4)
        uT = r4(uview(0, 8192, F32))
        vtm = [uview(8192 + 2048 * i, 2048, F32) for i in range(2)]
        vn = r4(uview(12288, 4096, BF16))
        pT = r4(uview(16384, 8448, F32))
        wsA = uview(24832, 2112, F32)
        wsB = uview(26944, 2112, F32)
        pooled = r4(uview(29056, 4096, BF16))
        hid = uview(0, 32768, BF16).rearrange("p (a b) -> p a b", a=32)
        QT = r4(uview(0, 4096, BF16))
        cgT = r4(uview(4096, 8192, F32))
        zT = r4(uview(12288, 8224, F32))
        Et = [[uview(20512 + 1024 * (2 * m + b), 1024, BF16) for b in range(2)] for m in range(2)]
        accsb = uview(24608, 4680, F32).rearrange("p (a b) -> p a b", a=9)
        oA = r4(uview(29288, 2048, F32))
        oB = r4(uview(31336, 2048, F32))
        oNb = r4(uview(33384, 1024, BF16))
        rcs = sb("rcs", [P, 16], F32)
        sm = lambda i: small[:, i:i + 1]
        ALLY = ["yst%d" % c for c in range(DC)]

        def cload(dst, src, tok):
            S.dma("sp", "c", DMA(dst, src), writes=[tok])

        cload(ident_f[:], c_ident[:, :], "ident_f")
        cload(Jt[:], c_J[:, :], "Jt")
        cload(tri[:], c_tri[:, :], "tri")
        cload(oh[:, 0:383], c_oh[:, :], "oh")
        cload(invcnt[:].rearrange("p a b -> p (a b)"), c_invcnt[:, :], "invcnt")
        cload(stage[0:64, :], norm_g[:, :], "stage")
        cload(stage[64:68, :], pool_s[:, :], "stage")
        cload(stage[68:80, :], conv_w[:, :], "stage")
        cload(stage[80:84, :], ln_g.rearrange("o (g c) -> (o g) c", g=4), "stage")
        cload(stage[84:88, :], ln_b.rearrange("o (g c) -> (o g) c", g=4), "stage")
        cload(bsb[:], sp_b.partition_broadcast(P), "bsb")
        cload(sublng[:], subln.partition_broadcast(P), "sublng")
        cload(lamraw[:], lam_in.partition_broadcast(P), "lamraw")
        cload(tab[:], rel_tab[:, :], "tab")
        cload(spw[:], sp_w.rearrange("g t s -> t g s"), "spw")
        S.fence(["ident_f", "Jt", "tri", "oh", "invcnt", "stage", "bsb", "sublng", "lamraw", "tab", "spw"], "c")
        S.dma("pool", "c2", DMA(poolw[:], pool_w.rearrange("g c d -> c g d")), writes=["poolw"])

        tile_blocks = []
        for c0 in (1024, 512, 0):
            tile_blocks.append((w_in0[:, c0:c0 + 512], s_in0[:, c0:c0 + 512]))
        for c0 in (0, 512):
            tile_blocks.append((w_out0[:, c0:c0 + 512], s_out0[:, c0:c0 + 512]))
        for b in range(8):
            tile_blocks.append((w1[0][:, b * 512:(b + 1) * 512], s_w1[0][:, b * 512:(b + 1) * 512]))
        for half in range(2):
            for kg in range(4):
                tile_blocks.append((w2[0][kg * 1024:(kg + 1) * 1024, half * 512:(half + 1) * 512],
                                    s_w2[0][kg * 1024:(kg + 1) * 1024, half * 512:(half + 1) * 512]))
        for c0 in (2048, 2560, 0, 512, 1536, 1024):
            tile_blocks.append((w_in1[:, c0:c0 + 512], s_in1[:, c0:c0 + 512]))
        for c0 in (0, 512):
            tile_blocks.append((w_out1[:, c0:c0 + 512], s_out1[:, c0:c0 + 512]))
        for b in range(8):
            tile_blocks.append((w1[1][:, b * 512:(b + 1) * 512], s_w1[1][:, b * 512:(b + 1) * 512]))
        for half in range(2):
            for kg in range(4):
                tile_blocks.append((w2[1][kg * 1024:(kg + 1) * 1024, half * 512:(half + 1) * 512],
                                    s_w2[1][kg * 1024:(kg + 1) * 1024, half * 512:(half + 1) * 512]))
        NBLK = len(tile_blocks)
        S.op("pool", MS(onesD[:], 1.0 / D), writes=["onesD"])
        S.op("pool", MS(V[:, :, :, 128:130], 1.0), writes=["Vones"])
        S.op("pool", MS(phalo[:], 0.0), writes=["phalo"])
        S.op("pool", MS(zhalo[:], 0.0), writes=["zhalo"])
        S.op("pool", MS(line[:], NEG), writes=["line"])
        S.op("pool", MS(small[:, 4:5], EPS), writes=["epsc"])
        cstate = {"n": 0}

        def cast_issue_upto(n):
            while cstate["n"] < min(n, NBLK):
                bi = cstate["n"]
                src, dst = tile_blocks[bi]
                S.dma("pool", "k%d" % bi, DMA(dst, src), writes=["S%d" % bi])
                cstate["n"] += 1


        S.op("dve", CP(ident_b[:], ident_f[:]), reads=["ident_f"], writes=["ident_b"])
        S.op("pe", TR(PS[7][:, 0:88], stage[0:88, :], ident_f[0:88, 0:88]), reads=["stage", "ident_f"], writes=["ps7"])
        S.op("dve", CP(cols[:], PS[7][:, 0:88]), reads=["ps7"], writes=["cols"])
        gcol = lambda l, n, c: cols[:, (l * 4 + n) * 8 + c:(l * 4 + n) * 8 + c + 1]
        pscol = lambda j: cols[:, 64 + j:65 + j]
        cwcol = lambda k, j: cols[:, 68 + k * 4 + j:69 + k * 4 + j]
        lngcol = lambda g: cols[:, 80 + g:81 + g]
        for g in range(4):
            S.op("dve", TT(spw[:, g, :], spw[:, g, :], tri[:], ALU.mult), reads=["spw", "tri"], writes=["spw"])
        for g in range(4):
            S.op("pe", TR(PS[6][:, g * P:(g + 1) * P], spw[:, g, :], ident_f[:]), reads=["spw", "ident_f"], writes=["ps6"])
        S.op("dve", CP(WsT[:].rearrange("p a b -> p (a b)"), PS[6][:]), reads=["ps6"], writes=["WsT"])
        S.op("dve", TS(lnsc[:, 24:28], cols[:, 84:88], float(D), None, ALU.mult), reads=["cols"], writes=["lnb1k"])
        for g in range(4):
            S.op("pe", MM(PS[6][:, g * P:(g + 1) * P], onesD[:], WsT[:, g, :]), reads=["onesD", "WsT"], writes=["ps6"], inc=(g == 3))
        for g in range(4):
            S.op("dve", STT(bsb[:, g * P:(g + 1) * P], PS[6][:, g * P:(g + 1) * P], lnsc[:, 24 + g:25 + g], bsb[:, g * P:(g + 1) * P], ALU.mult, ALU.add),
                 reads=["ps6", "lnb1k", "bsb"], writes=["bsb"])
        S.op("pe", MM(PS[5][0:4, 0:383], tab[:, :], oh[:, 0:383]), reads=["tab", "oh"], writes=["ps5"])
        S.op("dve", CP(small[0:4, 0:1], PS[5][0:4, 382:383]), reads=["ps5"], writes=["small0"])
        S.op("dve", TS(line[:, 127:383], PS[5][0:4, 127:383], small[0:4, 0:1], 8.0, ALU.subtract, ALU.mult),
             reads=["ps5", "small0", "line"], writes=["line"])
        S.dma("sp", "c3", DMA(rline[:, :], line[:]), reads=["line"], writes=["rline"])
        for h in range(4):
            src = bass.AP(rline.tensor, h * 384, [[1, P], [1, 256]])
            S.dma("sp", "c4", DMA(BT[:, h, :], src), reads=["rline"], writes=["BTraw"])
        S.fence(["BTraw"], "c4")

        def finish_bias_tiles():
            for h in range(4):
                S.op("pe", MM(PS[7][:, 0:256], Jt[:], BT[:, h, :]), reads=["Jt", "BTraw", "BT"], writes=["ps7"])
                S.op("dve", CP(BT[:, h, :], PS[7][:, 0:256]), reads=["ps7"], writes=["BT", "BTraw"])

        S.op("dve", TT(lamraw[:, 0:64], lamraw[:, 0:64], lamraw[:, 64:128], ALU.mult), reads=["lamraw"], writes=["lamraw"])
        S.op("dve", TT(lamraw[:, 128:192], lamraw[:, 128:192], lamraw[:, 192:256], ALU.mult), reads=["lamraw"], writes=["lamraw"])
        S.op("dve", RSUM(lamt[:, 0:1], lamraw[:, 0:64]), reads=["lamraw"], writes=["lamt"])
        S.op("dve", RSUM(lamt[:, 1:2], lamraw[:, 128:192]), reads=["lamraw", "lamt"], writes=["lamt"])
        S.op("act", ACT(lamt[:, 2:4], lamt[:, 0:2], AF.Exp), reads=["lamt"], writes=["lamt"])
        S.op("dve", TT(lamt[:, 4:5], lamt[:, 3:4], lamt[:, 2:3], ALU.subtract), reads=["lamt"], writes=["lamt"])
        S.op("dve", TS(small[:, 2:3], lamt[:, 4:5], -LAMBDA_INIT, None, ALU.add), reads=["lamt"], writes=["nlam"])
        S.op("dve", TS(sublng[:], sublng[:], 1.0 - LAMBDA_INIT, None, ALU.mult), reads=["sublng"], writes=["sublng"])
        nlam = small[:, 2:3]

        blocks = []
        for i in range(NT):
            for bi, (src, dst) in enumerate(tile_blocks):
                blocks.append((dst, "S%d" % bi))
        wstate = {"issued": 0, "used": 0}

        def w_issue_upto(n):
            while wstate["issued"] < min(n, len(blocks)):
                k = wstate["issued"]
                slot = k % RING
                src, tok = blocks[k]
                if k < NBLK:
                    f32src, scr = tile_blocks[k]
                    S.dma("pool", "wc%d" % slot, DMA(wring[:, slot], f32src.rearrange("(c p) n -> p c n", p=P)), writes=["w%d" % slot])
                    S.dma("sp", "k%d" % k, DMA(scr.rearrange("(c p) n -> p c n", p=P), wring[:, slot]), reads=["w%d" % slot], writes=["S%d" % k])
                else:
                    sv = src.rearrange("(c p) n -> p c n", p=P)
                    S.dma("sp", "w%d" % slot, DMA(wring[:, slot], sv), reads=[tok], writes=["w%d" % slot])
                wstate["issued"] += 1

        def w_next():
            k = wstate["used"]
            w_issue_upto(k + RING)
            wstate["used"] += 1
            slot = k % RING
            return wring[:, slot], "w%d" % slot

        rot = {"i": 0}

        def nextbank(n=6):
            b = rot["i"] % n
            rot["i"] += 1
            return b

        def stats_and_rstd():
            for c in range(DC):
                S.op("pe", MM(PS[6][:], onesD[:], hn[:, c, :], start=(c == 0), stop=(c == DC - 1)),
                     reads=["onesD", "hn%d" % c], writes=["ps6"], inc=(c == DC - 1))
            S.op("act", ACT(sqb[:], PS[6][:], AF.Ln, bias=EPS, scale=1.0), reads=["ps6"], writes=["sqb"])
            S.op("act", ACT(rstd[:], sqb[:], AF.Exp, scale=-0.5), reads=["sqb"], writes=["rstd"])

        def prenorm(l, n, have_squares=False):
            for c in range(DC):
                if not have_squares:
                    S.op("act", ACT(hn[:, c, :], hT[:, c, :], AF.Square), reads=["hT%d" % c], writes=["hn%d" % c])
            stats_and_rstd()
            for c in range(DC):
                S.op("dve", STT(hn[:, c, :], hT[:, c, :], gcol(l, n, c), rstd[:], ALU.mult, ALU.mult),
                     reads=["hT%d" % c, "cols", "rstd"], writes=["hn%d" % c])

        def evac_post(b, c, l, n):
            S.op("act", ACT(hn[:, c, :], PS[b][:], AF.Square), reads=["ps%d" % b], writes=["hn%d" % c])
            S.op("act", ACT(yst[:, c, :], PS[b][:], AF.Copy, scale=gcol(l, n, c)), reads=["ps%d" % b, "cols"], writes=["yst%d" % c])

        def postnorm(l, n):
            stats_and_rstd()
            for c in range(DC):
                S.op("dve", TT(yst[:, c, :], yst[:, c, :], rstd[:], ALU.mult), reads=["yst%d" % c, "rstd"], writes=["yst%d" % c])
                S.op("dve", TT(hT[:, c, :], hT[:, c, :], yst[:, c, :], ALU.add),
                     reads=["yst%d" % c, "hT%d" % c], writes=["hT%d" % c])

        def mm_fm(wblk, wtok, src, src_tok, j, b, nk=DC, start=True, stop=True, koff=0):
            for k in range(nk):
                S.op("pe", MM(PS[b][:], wblk[:, k, j * P:(j + 1) * P], src[:, koff + k, :],
                              start=(start and k == 0), stop=(stop and k == nk - 1)),
                     reads=[wtok, "%s%d" % (src_tok, koff + k)], writes=["ps%d" % b], inc=(k == nk - 1))

        def mm_fm_kouter(wblk, wtok, src, src_tok, banks, korder=None, before_last=None):
            korder = list(range(DC)) if korder is None else korder
            for i, k in enumerate(korder):
                if i == DC - 1 and before_last is not None:
                    before_last()
                for j, b in enumerate(banks):
                    S.op("pe", MM(PS[b][:], wblk[:, k, j * P:(j + 1) * P], src[:, k, :], start=(i == 0), stop=(i == DC - 1)),
                         reads=[wtok, "%s%d" % (src_tok, k)], writes=["ps%d" % b], inc=(i == DC - 1))

        def mm_tm(wblk, wtok, tc, b):
            for k in range(DC):
                S.op("pe", MM(PS[b][:], hn[:, k, tc * P:(tc + 1) * P], wblk[:, k, :], start=(k == 0), stop=(k == DC - 1)),
                     reads=[wtok, "hn%d" % k], writes=["ps%d" % b], inc=(k == DC - 1))

        def wout_and_post(l, korder, before_last=None):
            for half in range(2):
                wblk, wtok = w_next()
                if half == 0:
                    banks = [nextbank() for j in range(4)]
                    mm_fm_kouter(wblk, wtok, mix, "mix", banks, korder, before_last)
                    for j in range(4):
                        evac_post(banks[j], j, l, 1)
                else:
                    for j in range(4):
                        b = nextbank()
                        mm_fm(wblk, wtok, mix, "mix", j, b)
                        evac_post(b, 4 + j, l, 1)
            postnorm(l, 1)

        def ffn(l, after_w2=None):
            for c in range(DC):
                S.op("act", ACT(hn[:, c, :], hT[:, c, :], AF.Copy, scale=gcol(l, 2, c)), reads=["hT%d" % c, "cols"], writes=["hn%d" % c])
                S.op("act", ACT(mix[:, c, :], hT[:, c, :], AF.Square), reads=["hT%d" % c], writes=["mix%d" % c])
            S.alias(["hid%d" % j for j in range(32)])
            for blk in range(8):
                wblk, wtok = w_next()
                banks = [nextbank() for j in range(4)]
                if blk == 0:
                    mm_fm_kouter(wblk, wtok, hn, "hn", banks)
                for j in range(4):
                    hc = blk * 4 + j
                    b = banks[j]
                    if blk != 0:
                        mm_fm(wblk, wtok, hn, "hn", j, b)
                    r = hc % DC
                    S.op("act", ACT(yst[:, r, :], PS[b][:], AF.Relu), reads=["ps%d" % b], writes=["yst%d" % r])
                    eng = "pool" if hc % 4 == 3 else "dve"
                    S.op(eng, TT(hid[:, hc, :], yst[:, r, :], yst[:, r, :], ALU.mult), reads=["yst%d" % r], writes=["hid%d" % hc])
                if blk == 1:
                    for c in range(DC):
                        S.op("pe", MM(PS[6][:], onesD[:], mix[:, c, :], start=(c == 0), stop=(c == DC - 1)),
                             reads=["onesD", "mix%d" % c], writes=["ps6"], inc=(c == DC - 1))
                    S.op("act", ACT(rstd[:], PS[6][:], AF.Square, bias=1e-3 * EPS, scale=1e-3), reads=["ps6"], writes=["rstd"])
            for half in range(2):
                banks = [half * 4 + j for j in range(4)]
                for kg in range(4):
                    wblk, wtok = w_next()
                    for j in range(4):
                        mm_fm(wblk, wtok, hid, "hid", j, banks[j], nk=8, start=(kg == 0), stop=(kg == 3), koff=kg * 8)
                if half == 1 and after_w2 is not None:
                    after_w2()
                for j in range(4):
                    evac_post(banks[j], half * 4 + j, l, 3)
            for c in range(DC):
                S.op("pe", MM(PS[6][:], onesD[:], hn[:, c, :], start=(c == 0), stop=(c == DC - 1)),
                     reads=["onesD", "hn%d" % c], writes=["ps6"], inc=(c == DC - 1))
            S.op("dve", TT(sqb[:], PS[6][:], rstd[:], ALU.add), reads=["ps6", "rstd"], writes=["sqb"])
            S.op("act", ACT(sqb[:], sqb[:], AF.Ln), reads=["sqb"], writes=["sqb"])
            S.op("act", ACT(rstd[:], sqb[:], AF.Exp, scale=-0.5), reads=["sqb"], writes=["rstd"])
            for c in range(DC):
                S.op("dve", TT(yst[:, c, :], yst[:, c, :], rstd[:], ALU.mult), reads=["yst%d" % c, "rstd"], writes=["yst%d" % c])
                S.op("dve", TT(hT[:, c, :], hT[:, c, :], yst[:, c, :], ALU.add),
                     reads=["yst%d" % c, "hT%d" % c], writes=["hT%d" % c])

        ALLHM = ["hn%d" % c for c in range(DC)] + ["mix%d" % c for c in range(DC)]
        xv = U[:, 0:8192].bitcast(F32).rearrange("p (a d) -> p a d", a=4)
        ov = hm[:].rearrange("p a c t -> p (a c t)").bitcast(F32).rearrange("p (a d) -> p a d", a=4)

        def xload(it):
            t0 = it * T
            S.alias(["xbuf"])
            S.dma("sp", "x", DMA(xv, x[t0:t0 + T, :].rearrange("(a p) d -> p a d", p=P)), writes=["xbuf"])

        S.tag = 'xload'
        xload(0)
        for it in range(NT):
            t0 = it * T
            S.tag = 'xload'
            for c in range(DC):
                b = nextbank()
                for a in range(4):
                    S.op("pe", TR(PS[b][:, a * P:(a + 1) * P], xv[:, a, c * P:(c + 1) * P], ident_f[:]),
                         reads=["xbuf", "ident_f"], writes=["ps%d" % b], inc=(a == 3))
                if c % 2 == 0:
                    S.op("dve", CP(hT[:, c, :], PS[b][:]), reads=["ps%d" % b], writes=["hT%d" % c])
                else:
                    S.op("act", ACT(hT[:, c, :], PS[b][:], AF.Copy), reads=["ps%d" % b], writes=["hT%d" % c])

            S.tag = 'L0.pre'
            prenorm(0, 0)
            S.tag = 'L0.mix'
            S.alias(["uT", "vtm0", "vtm1", "vn0", "vn1", "vn2", "vn3", "pT", "wsA", "wsB", "pooled"])
            wblk, wtok = w_next()
            S.op("pool", CP(pT[:, :, 0:16], phalo[:]), reads=["phalo"], writes=["pT"])
            banks = [nextbank() for j in range(4)]
            mm_fm_kouter(wblk, wtok, hn, "hn", banks)
            for j in range(4):
                S.op("act", ACT(pT[:, j, 16:16 + T], PS[banks[j]][:], AF.Copy), reads=["ps%d" % banks[j]], writes=["pT"])
            S.op("pool", CP(phalo[:], pT[:, :, T:T + 16]), reads=["pT"], writes=["phalo"])
            for j in range(4):
                W_ = 2 ** (j + 1)
                cur, curtok = pT[:, j, :], "pT"
                bufs = [(wsA, "wsA"), (wsB, "wsB")]
                sh, lvl = 1, 0
                while sh < W_:
                    dst, dtok = bufs[lvl % 2]
                    lo = 2 * sh - 1
                    S.op("pool", TT(dst[:, lo:16 + T], cur[:, lo:16 + T], cur[:, lo - sh:16 + T - sh], ALU.add), reads=[curtok], writes=[dtok])
                    cur, curtok = dst, dtok
                    sh *= 2
                    lvl += 1
                if it == 0:
                    S.op("pool", TT(cur[:, 16:32], cur[:, 16:32], invcnt[:, j, :], ALU.mult), reads=[curtok, "invcnt"], writes=[curtok])
                if j < 3:
                    S.op("dve", STT(pooled[:, j, :], cur[:, 16:16 + T], 1.0 / W_, pT[:, j, 16:16 + T], ALU.mult, ALU.subtract), reads=[curtok, "pT"], writes=["pooled"])
                else:
                    pool_last = (cur, curtok, W_)
            wblk, wtok = w_next()
            vbanks = [nextbank() for tc in range(TC)]
            for k in range(DC):
                for tc in range(TC):
                    S.op("pe", MM(PS[vbanks[tc]][:], hn[:, k, tc * P:(tc + 1) * P], wblk[:, k, :], start=(k == 0), stop=(k == DC - 1)),
                         reads=[wtok, "hn%d" % k], writes=["ps%d" % vbanks[tc]], inc=(k == DC - 1))
            for tc in range(TC):
                b = vbanks[tc]
                S.op("act", ACT(yst[:, tc, :], PS[b][:], AF.Gelu_apprx_tanh, accum_out=lnsc[:, tc:tc + 1]), reads=["ps%d" % b], writes=["yst%d" % tc, "lns1_%d" % tc])
                S.op("act", ACT(vn[:, tc, :], yst[:, tc, :], AF.Square, accum_out=lnsc[:, 4 + tc:5 + tc]), reads=["yst%d" % tc], writes=["vn%d" % tc, "lns2_%d" % tc])
            LNS = ["lns1_%d" % t for t in range(4)] + ["lns2_%d" % t for t in range(4)]
            S.op("dve", TS(lnsc[:, 8:12], lnsc[:, 0:4], 1.0 / 512, None, ALU.mult), reads=LNS, writes=["lnmu"])
            S.op("dve", TT(lnsc[:, 12:16], lnsc[:, 8:12], lnsc[:, 8:12], ALU.mult), reads=["lnmu"], writes=["lnvar"])
            S.op("dve", STT(lnsc[:, 12:16], lnsc[:, 4:8], 1.0 / 512, lnsc[:, 12:16], ALU.mult, ALU.subtract), reads=LNS + ["lnvar"], writes=["lnvar"])
            S.op("act", ACT(lnsc[:, 20:24], lnsc[:, 12:16], AF.Ln, bias=EPS, scale=1.0), reads=["lnvar"], writes=["lnln"])
            S.op("act", ACT(lnsc[:, 12:16], lnsc[:, 20:24], AF.Exp, scale=-0.5), reads=["lnln"], writes=["lnvar"])
            S.op("dve", STT(lnsc[:, 16:20], lnsc[:, 8:12], -1.0, lnsc[:, 12:16], ALU.mult, ALU.mult), reads=["lnmu", "lnvar"], writes=["lnnmr"])
            for tc in range(TC):
                S.op("dve", TS(vn[:, tc, :], yst[:, tc, :], lnsc[:, 12 + tc:13 + tc], lnsc[:, 16 + tc:17 + tc], ALU.mult, ALU.add),
                     reads=["yst%d" % tc, "lnvar", "lnnmr"], writes=["vn%d" % tc])
            wblk, wtok = w_next()
            for j in range(4):
                b = nextbank()
                mm_fm(wblk, wtok, hn, "hn", j, b)
                S.op("act", ACT(uT[:, j, :], PS[b][:], AF.Gelu_apprx_tanh), reads=["ps%d" % b], writes=["uT"])
            for tc in range(TC):
                gb = 6 + (tc % 2)
                for g in range(4):
                    S.op("pe", MM(PS[gb][:, g * P:(g + 1) * P], vn[:, tc, g * P:(g + 1) * P], WsT[:, g, :]),
                         reads=["vn%d" % tc, "WsT"], writes=["ps%d" % gb], inc=(g == 3))
                vt = vtm[tc % 2]
                for g in range(4):
                    S.op("dve", STT(vt[:, g * P:(g + 1) * P], PS[gb][:, g * P:(g + 1) * P], lngcol(g), bsb[:, g * P:(g + 1) * P], ALU.mult, ALU.add),
                         reads=["ps%d" % gb, "bsb", "cols"], writes=["vtm%d" % (tc % 2)])
                S.op("dve", TT(mix[:, 0:4, tc * P:(tc + 1) * P], uT[:, :, tc * P:(tc + 1) * P], r4(vt), ALU.mult),
                     reads=["vtm%d" % (tc % 2), "uT"], writes=["mix%d" % c for c in range(4)])
            cur, curtok, W_ = pool_last
            S.op("dve", STT(pooled[:, 3, :], cur[:, 16:16 + T], 1.0 / W_, pT[:, 3, 16:16 + T], ALU.mult, ALU.subtract), reads=[curtok, "pT"], writes=["pooled"])
            for j in range(4):
                b = nextbank()
                S.op("pe", MM(PS[b][:], poolw[:, j, :], pooled[:, j, :]), reads=["poolw", "pooled"], writes=["ps%d" % b])
                S.op("act", ACT(mix[:, 4 + j, :], PS[b][:], AF.Copy, scale=pscol(j)), reads=["ps%d" % b, "cols"], writes=["mix%d" % (4 + j)])
            S.tag = 'L0.wout'
            wout_and_post(0, [0, 1, 2, 3, 4, 5, 6, 7])
            S.tag = 'F0'
            if stop_after >= 2:
                ffn(0)
            else:
                [w_next() for _ in range(16)]

            if stop_after >= 3:
                S.tag = 'L1.pre'
                prenorm(1, 0)
                S.tag = 'L1.proj'
                S.alias(["QT%d" % h for h in range(4)] + ["cgT", "zT", "E00", "E01", "E10", "E11", "accsb", "oA", "oB", "oNb"])
                wblk, wtok = w_next()
                banks = [nextbank(4) for j in range(4)]
                mm_fm_kouter(wblk, wtok, hn, "hn", banks)
                for j in range(4):
                    S.op("act", ACT(cgT[:, j, :], PS[banks[j]][:], AF.Copy), reads=["ps%d" % banks[j]], writes=["cgT"])
                wblk, wtok = w_next()
                S.op("pool", CP(zT[:, :, 0:2], zhalo[:]), reads=["zhalo"], writes=["zT"])
                for j in range(4):
                    b = nextbank(4)
                    mm_fm(wblk, wtok, hn, "hn", j, b)
                    S.op("dve", TT(zT[:, j, 2:2 + T], PS[b][:], cgT[:, j, :], ALU.mult), reads=["ps%d" % b, "cgT"], writes=["zT"])
                S.op("pool", CP(zhalo[:], zT[:, :, T:T + 2]), reads=["zT"], writes=["zhalo"])
                for j in range(4):
                    S.op("dve", TS(cgT[:, j, :], zT[:, j, 2:2 + T], cwcol(2, j), None, ALU.mult), reads=["zT", "cols"], writes=["cgT"])
                    S.op("dve", STT(cgT[:, j, :], zT[:, j, 1:1 + T], cwcol(1, j), cgT[:, j, :], ALU.mult, ALU.add), reads=["zT", "cols", "cgT"], writes=["cgT"])
                    S.op("dve", STT(cgT[:, j, :], zT[:, j, 0:T], cwcol(0, j), cgT[:, j, :], ALU.mult, ALU.add), reads=["zT", "cols", "cgT"], writes=["cgT"])
                wblk, wtok = w_next()
                for h in range(4):
                    b = nextbank(4)
                    mm_fm(wblk, wtok, hn, "hn", h, b)
                    S.op("act", ACT(QT[:, h, :], PS[b][:], AF.Copy), reads=["ps%d" % b], writes=["QT%d" % h])
                wblk, wtok = w_next()
                for h in range(4):
                    b = nextbank(4)
                    mm_fm(wblk, wtok, hn, "hn", h, b)
                    S.op("act", ACT(KT[:, h, t0:t0 + T], PS[b][:], AF.Copy), reads=["ps%d" % b], writes=["KT%d_%d" % (it, h)])
                wblk_bg, wtok_bg = w_next()
                for j in range(4):
                    b = nextbank(4)
                    mm_fm(wblk_bg, wtok_bg, hn, "hn", j, b)
                    S.op("dve", TT(mix[:, 4 + j, :], PS[b][:], cgT[:, j, :], ALU.mult), reads=["ps%d" % b, "cgT"], writes=["mix%d" % (4 + j)])
                wblk, wtok = w_next()
                for tc in range(TC):
                    b = nextbank(4)
                    mm_tm(wblk, wtok, tc, b)
                    kb = it * TC + tc
                    S.op("act", ACT(V[:, kb, :, 0:128], r4(PS[b][:]), AF.Copy), reads=["ps%d" % b, "Vones"], writes=["V%d" % kb])

                S.tag = 'L1.attn'
                if it == 0:
                    finish_bias_tiles()
                nkb = (it + 1) * TC

                def accv(m, qi):
                    a_ = m * 4 + qi
                    return PS[4 + a_ // 3][:, (a_ % 3) * 130:(a_ % 3) * 130 + 129]

                acctok = lambda m, qi: "ps%d" % (4 + (m * 4 + qi) // 3)
                est = {"n": 0}

                def scores(h, kb):
                    r = kb - it * TC
                    q0 = max(r, 0)
                    nq = TC - q0
                    ncol = nq * P
                    eb = est["n"] % 2
                    est["n"] += 1
                    for m in range(2):
                        sbk = 2 * m + eb
                        S.op("pe", MM(PS[sbk][:, 0:ncol], KT[m * 64:(m + 1) * 64, h, kb * P:(kb + 1) * P],
                                      QT[m * 64:(m + 1) * 64, h, q0 * P:q0 * P + ncol]),
                             reads=["KT%d_%d" % (kb // TC, h), "QT%d" % h], writes=["ps%d" % sbk])
                        for a in range(nq):
                            rel = it * TC + q0 + a - kb
                            if rel in (0, 1):
                                S.op("dve", TT(PS[sbk][:, a * P:(a + 1) * P], PS[sbk][:, a * P:(a + 1) * P], BT[:, h, rel * P:(rel + 1) * P], ALU.add),
                                     reads=["BT", "ps%d" % sbk], writes=["ps%d" % sbk])
                        S.op("act", ACT(Et[m][eb][:, 0:ncol], PS[sbk][:, 0:ncol], AF.Exp, scale=0.125), reads=["ps%d" % sbk], writes=["E%d%d" % (m, eb)])
                    return eb, q0, nq

                def av(h, kb, eb, q0, nq):
                    for m in range(2):
                        for a in range(nq):
                            qi = q0 + a
                            last = (m == 1 and a == nq - 1)
                            S.op("pe", MM(accv(m, qi), Et[m][eb][:, a * P:(a + 1) * P], V[:, kb, h, 0:129],
                                          start=(kb == 0 and (m * 4 + qi) % 3 == 0), stop=False, skip_group_check=True),
                                 reads=["E%d%d" % (m, eb), "V%d" % kb, "Vones"], writes=[acctok(m, qi)], inc=last)
                    for _ in range(NFILL):
                        S.op("pe", MM(PS[7][:, 256:512], onesD[:], hn[:, 0, 0:256], start=True, stop=True, skip_group_check=True),
                             reads=["onesD", "hn0"], writes=["ps7"], inc=False)

                def fin_math(h):
                    rcv = rcs[:, 0:8].rearrange("p (a b) -> p a b", b=1)
                    ssv = rcs[:, 8:12].rearrange("p (a b) -> p a b", b=1)
                    S.op("dve", RCP(rcv, accsb[:, 0:8, 128:129]), reads=["accsb"], writes=["rcs"])
                    S.op("dve", TS(rcs[:, 4:8], rcs[:, 4:8], nlam, None, ALU.mult), reads=["rcs", "nlam"], writes=["rcs"])
                    S.op("dve", TT(oA[:], accsb[:, 0:4, 0:128], rcv[:, 0:4, :].to_broadcast([P, 4, P]), ALU.mult), reads=["accsb", "rcs"], writes=["oA"])
                    S.op("dve", TT(oB[:], accsb[:, 4:8, 0:128], rcv[:, 4:8, :].to_broadcast([P, 4, P]), ALU.mult), reads=["accsb", "rcs"], writes=["oB"])
                    S.op("dve", TT(oA[:], oA[:], oB[:], ALU.add), reads=["oA", "oB"], writes=["oA"])
                    S.op("dve", TT(oB[:], oA[:], oA[:], ALU.mult), reads=["oA"], writes=["oB"])
                    S.op("dve", RSUM(rcs[:, 8:12], oB[:]), reads=["oB"], writes=["rcs2"])

                def fin_math_b(h):
                    ssv = rcs[:, 8:12].rearrange("p (a b) -> p a b", b=1)
                    S.op("act", ACT(rcs[:, 12:16], rcs[:, 8:12], AF.Ln, bias=EPS, scale=1.0 / 128), reads=["rcs2"], writes=["rcs3"])
                    S.op("act", ACT(rcs[:, 8:12], rcs[:, 12:16], AF.Exp, scale=-0.5), reads=["rcs3"], writes=["rcs2"])
                    S.op("dve", TT(oA[:], oA[:], ssv.to_broadcast([P, 4, P]), ALU.mult), reads=["oA", "rcs2"], writes=["oA"])
                    S.op("dve", TT(oNb[:], oA[:], sublng[:].unsqueeze(1).to_broadcast([P, 4, P]), ALU.mult), reads=["oA", "sublng"], writes=["oNb"])

                def fin_pe(h):
                    for qi in range(TC):
                        S.op("pe", TR(PSB[:, qi * P:(qi + 1) * P], oNb[:, qi, :], ident_b[:]), reads=["oNb", "ident_b"], writes=["ps7"], inc=(qi == TC - 1))
                    S.op("act", ACT(mix[:, h, :], PSB[:, 0:512], AF.Copy), reads=["ps7"], writes=["mix%d" % h])

                for h in range(4):
                    if h > 0:
                        fin_math(h - 1)
                    cur = scores(h, 0)
                    for kb in range(nkb):
                        nxt = scores(h, kb + 1) if kb + 1 < nkb else None
                        av(h, kb, *cur)
                        cur = nxt
                        if h > 0 and kb == min(1, nkb - 1):
                            fin_math_b(h - 1)
                        if h > 0 and kb == min(3, nkb - 1):
                            fin_pe(h - 1)
                    S.op("dve", CP(accsb[:, 0:3, :].rearrange("p a b -> p (a b)"), PS[4][:, 0:390]), reads=["ps4"], writes=["accsb"])
                    S.op("act", ACT(accsb[:, 3:6, :].rearrange("p a b -> p (a b)"), PS[5][:, 0:390], AF.Copy), reads=["ps5"], writes=["accsb"])
                    S.op("dve", CP(accsb[:, 6:9, :].rearrange("p a b -> p (a b)"), PS[6][:, 0:390]), reads=["ps6"], writes=["accsb"])
                fin_math(3)
                fin_math_b(3)
                S.tag = 'L1.wout'
                wout_and_post(1, [4, 5, 6, 7, 0, 1, 2, 3], before_last=lambda: fin_pe(3))
            else:
                [w_next() for _ in range(8)]
            S.tag = 'F1'
            if stop_after >= 4:
                ffn(1, after_w2=(lambda: xload(it + 1)) if it + 1 < NT else None)
            else:
                [w_next() for _ in range(16)]

            S.tag = 'store'
            for c in range(DC):
                b = nextbank()
                for a in range(4):
                    S.op("pe", TR(PS[b][:, a * P:(a + 1) * P], hT[:, c, a * P:(a + 1) * P], ident_f[:]),
                         reads=["hT%d" % c, "ident_f"], writes=["ps%d" % b], inc=(a == 3))
                if c % 2 == 0:
                    S.op("dve", CP(ov[:, :, c * P:(c + 1) * P], r4(PS[b][:])), reads=["ps%d" % b], writes=ALLHM)
                else:
                    S.op("act", ACT(ov[:, :, c * P:(c + 1) * P], r4(PS[b][:]), AF.Copy), reads=["ps%d" % b], writes=ALLHM)
            S.dma("sp", "o", DMA(out[t0:t0 + T, :].rearrange("(a p) d -> p a d", p=P), ov), reads=ALLHM)

        S.wait_all("sp")
        S.emit()
    return nc, S


_CACHE = {}


def kernel(**inputs):
    n = 8
    x = np.ascontiguousarray(inputs["x"], dtype=np.float32)
    common = dict(host_consts())
    f = lambda k: np.ascontiguousarray(inputs[k], dtype=np.float32)
    common["even_w_in"] = f("even_w_in")[0]
    common["even_w_out"] = f("even_w_out")[0]
    common["odd_w_in"] = f("odd_w_in")[0]
    common["odd_w_out"] = f("odd_w_out")[0]
    for l in range(2):
        common["ffn_w1_%d" % l] = f("ffn_w1")[l]
        common["ffn_w2_%d" % l] = f("ffn_w2")[l]
    common["norm_g"] = f("norm_g").reshape(64, 128)
    common["even_ln_g"] = f("even_ln_g").reshape(1, 512)
    common["even_ln_b"] = f("even_ln_b").reshape(1, 512)
    common["even_spatial_w"] = f("even_spatial_w")[0]
    common["even_spatial_b"] = f("even_spatial_b").reshape(1, 512)
    common["even_pool_w"] = f("even_pool_w")[0]
    common["even_pool_scale"] = f("even_pool_scale").reshape(4, 128)
    common["odd_lambda"] = f("odd_lambda").reshape(1, 256)
    common["odd_subln_g"] = f("odd_subln_g").reshape(1, 128)
    common["odd_conv_w"] = f("odd_conv_w").reshape(12, 128)
    common["rel_bias_table"] = f("rel_bias_table")
    if "nc" not in _CACHE:
        _CACHE["nc"] = build()[0]
    nc = _CACHE["nc"]
    in_maps = []
    for b in range(n):
        m = dict(common)
        m["x"] = x[b]
        in_maps.append(m)
    res = run_bass_kernel_spmd(nc, in_maps, core_ids=list(range(n)))
    return np.stack([np.asarray(r["out"], dtype=np.float32) for r in res.results], axis=0)
```

```python
import math
import contextlib
import numpy as np
import concourse.bass as bass
import concourse.mybir as mybir
from concourse.bass_utils import run_bass_kernel_spmd

F32 = mybir.dt.float32
BF16 = mybir.dt.bfloat16
AF = mybir.ActivationFunctionType
ALU = mybir.AluOpType
AX = mybir.AxisListType

D = 1024
SEQ = 4096
T = 512
P = 128
DC = 8
TC = 4
DFF = 4096
EPS = 1e-6
NEG = -30000.0
LAMBDA_INIT = 0.8 - 0.6 * math.exp(-0.3 * 1)
RING = 4

ENGS = ("pe", "act", "dve", "pool", "sp")
STRICT_OWN = True


class Sched:
    def __init__(self, nc, serial=False):
        self.nc = nc
        self.serial = serial
        self.ops = {e: [] for e in ENGS}
        self.cnt = {e: 0 for e in ENGS}
        self.dcnt = {}
        self.known = {e: {} for e in ENGS}
        self.tok_w = {}
        self.tok_r = {}
        self.last = None
        self.tag = ''
        self.tags = {e: [] for e in ENGS}

    def _collect(self, reads, writes):
        deps = {}

        def add(k, v):
            if deps.get(k, -1) < v:
                deps[k] = v
        for t in reads:
            for k, v in self.tok_w.get(t, {}).items():
                add(k, v)
        for t in writes:
            for k, v in self.tok_w.get(t, {}).items():
                add(k, v)
            for k, v in self.tok_r.get(t, {}).items():
                add(k, v)
        if self.serial and self.last is not None:
            add(*self.last)
        return deps

    def _commit(self, me, reads, writes):
        for t in writes:
            self.tok_w[t] = {me[0]: me[1]}
            self.tok_r[t] = {}
        for t in reads:
            r = self.tok_r.setdefault(t, {})
            if r.get(me[0], -1) < me[1]:
                r[me[0]] = me[1]
        self.last = me

    def _waits(self, eng, deps):
        waits = []
        kn = self.known[eng]
        for k, v in deps.items():
            if k == ("e", eng):
                if eng == "pe" or (v < self.cnt[eng] and not STRICT_OWN):
                    continue
            if kn.get(k, -1) >= v:
                continue
            kn[k] = v
            waits.append((k, v))
        return waits

    def op(self, eng, fn, reads=(), writes=(), inc=True):
        psr = [t for t in reads if t.startswith("ps")]
        if psr:
            reads = [t for t in reads if not t.startswith("ps")]
            writes = list(writes) + [t for t in psr if t not in writes]
        deps = self._collect(reads, writes)
        waits = self._waits(eng, deps)
        idx = self.cnt[eng] + 1
        if inc:
            self.cnt[eng] = idx
        me = (("e", eng), idx)
        self.ops[eng].append((waits, fn, ("e", eng) if inc else None))
        self.tags[eng].append(self.tag)
        self._commit(me, reads, writes)

    def dma(self, eng, sem, fn, reads=(), writes=()):
        deps = self._collect(reads, writes)
        waits = self._waits(eng, deps)
        v = self.dcnt.get(sem, 0) + 16
        self.dcnt[sem] = v
        me = (("d", sem), v)
        self.ops[eng].append((waits, fn, ("d", sem)))
        self._commit(me, reads, writes)

    def fence(self, tokens, sem):
        for t in tokens:
            self.tok_w[t] = {("d", sem): self.dcnt[sem]}

    def alias(self, tokens):
        dep = {("e", e): self.cnt[e] for e in ENGS if e != "sp" and self.cnt[e] > 0}
        for t in tokens:
            self.tok_w[t] = dict(dep)
            self.tok_r[t] = {}

    def wait_all(self, eng):
        deps = {}
        for e in ENGS:
            if e != "sp" and e != eng and self.cnt[e] > 0:
                deps[("e", e)] = self.cnt[e]
        for s, v in self.dcnt.items():
            deps[("d", s)] = v
        self.ops[eng].append((list(deps.items()), None, None))

    def emit(self):
        nc = self.nc
        with contextlib.ExitStack() as st:
            sems = {}
            for e in ENGS:
                if e != "sp":
                    sems[("e", e)] = st.enter_context(nc.semaphore("P_" + e))
            for s in self.dcnt:
                sems[("d", s)] = st.enter_context(nc.semaphore("D_" + s))
            block = st.enter_context(nc.Block())

            def run(engname, engobj):
                for waits, fn, incspec in self.ops[engname]:
                    for k, v in waits:
                        engobj.wait_ge(sems[k], v)
                    if fn is None:
                        continue
                    ins = fn(engobj)
                    if incspec is not None:
                        ins.then_inc(sems[incspec], 16 if incspec[0] == "d" else 1)

            @block.tensor
            def _(e):
                run("pe", e)

            @block.scalar
            def _(e):
                run("act", e)

            @block.vector
            def _(e):
                run("dve", e)

            @block.gpsimd
            def _(e):
                run("pool", e)

            @block.sync
            def _(e):
                run("sp", e)


def _t5_bucket_np(dist):
    max_exact = 16
    nf = np.maximum(dist, 1).astype(np.float32)
    large = max_exact + (np.log(nf / max_exact) / math.log(128 / max_exact) * (32 - max_exact)).astype(np.int32)
    large = np.minimum(large, 31)
    return np.where(dist < max_exact, dist, large)


def host_consts():
    c = {}
    c["c_ident"] = np.eye(128, dtype=np.float32)
    c["c_J"] = np.ascontiguousarray(np.eye(128, dtype=np.float32)[::-1])
    c["c_tri"] = np.tril(np.ones((128, 128), dtype=np.float32))
    u = np.arange(383)
    dist = u - 127
    bucket = _t5_bucket_np(np.maximum(dist, 0))
    oh = np.zeros((32, 383), dtype=np.float32)
    for j in range(383):
        if dist[j] >= 0:
            oh[bucket[j], j] = 1.0
    c["c_oh"] = oh
    ic = np.zeros((128, 4, 16), dtype=np.float32)
    for g, w in enumerate((2, 4, 8, 16)):
        ic[:, g, :] = float(w) / np.minimum(np.arange(16) + 1, w).astype(np.float32)
    c["c_invcnt"] = ic.reshape(128, 64)
    return c


def MM(out, lhsT, rhs, start=True, stop=True, **kw):
    return lambda e: e.matmul(out, lhsT=lhsT, rhs=rhs, start=start, stop=stop, **kw)


def TR(out, in_, ident):
    return lambda e: e.transpose(out, in_, ident)


def ACT(out, in_, func, **kw):
    return lambda e: e.activation(out=out, in_=in_, func=func, **kw)


def TT(out, in0, in1, op):
    return lambda e: e.tensor_tensor(out=out, in0=in0, in1=in1, op=op)


def TS(out, in0, s1, s2, op0, op1=None):
    if op1 is None:
        return lambda e: e.tensor_scalar(out=out, in0=in0, scalar1=s1, scalar2=None, op0=op0)
    return lambda e: e.tensor_scalar(out=out, in0=in0, scalar1=s1, scalar2=s2, op0=op0, op1=op1)


def STT(out, in0, scalar, in1, op0, op1):
    return lambda e: e.scalar_tensor_tensor(out=out, in0=in0, scalar=scalar, in1=in1, op0=op0, op1=op1)


def CP(out, in_):
    return lambda e: e.tensor_copy(out=out, in_=in_)


def MS(ap, val):
    return lambda e: e.memset(ap, val)


def RCP(out, in_):
    return lambda e: e.reciprocal(out=out, in_=in_)


def RSUM(out, in_):
    return lambda e: e.reduce_sum(out=out, in_=in_, axis=AX.X)


def DMA(out, in_):
    return lambda e: e.dma_start(out=out, in_=in_)


def build(NT=SEQ // T, serial=False, stop_after=4):
    nc = bass.Bass("TRN2", target_bir_lowering=False)
    SEQL = NT * T
    NKB = NT * TC

    def din(name, shape, dt=F32):
        return nc.dram_tensor(name, shape, dt, kind="ExternalInput").ap()

    x = din("x", [SEQL, D])
    out = nc.dram_tensor("out", [SEQL, D], F32, kind="ExternalOutput").ap()
    w_in0 = din("even_w_in", [D, 1536])
    w_out0 = din("even_w_out", [D, D])
    w_in1 = din("odd_w_in", [D, 3072])
    w_out1 = din("odd_w_out", [D, D])
    w1 = [din("ffn_w1_%d" % l, [D, DFF]) for l in range(2)]
    w2 = [din("ffn_w2_%d" % l, [DFF, D]) for l in range(2)]
    norm_g = din("norm_g", [64, 128])
    ln_g = din("even_ln_g", [1, 512])
    ln_b = din("even_ln_b", [1, 512])
    sp_w = din("even_spatial_w", [4, 128, 128])
    sp_b = din("even_spatial_b", [1, 512])
    pool_w = din("even_pool_w", [4, 128, 128])
    pool_s = din("even_pool_scale", [4, 128])
    lam_in = din("odd_lambda", [1, 256])
    subln = din("odd_subln_g", [1, 128])
    conv_w = din("odd_conv_w", [12, 128])
    rel_tab = din("rel_bias_table", [32, 4])
    c_ident = din("c_ident", [128, 128])
    c_J = din("c_J", [128, 128])
    c_tri = din("c_tri", [128, 128])
    c_oh = din("c_oh", [32, 383])
    c_invcnt = din("c_invcnt", [128, 64])

    def dscr(name, shape, dt=BF16):
        return nc.dram_tensor(name, shape, dt).ap()

    s_in0 = dscr("s_in0", [D, 1536])
    s_out0 = dscr("s_out0", [D, D])
    s_in1 = dscr("s_in1", [D, 3072])
    s_out1 = dscr("s_out1", [D, D])
    s_w1 = [dscr("s_w1_%d" % l, [D, DFF]) for l in range(2)]
    s_w2 = [dscr("s_w2_%d" % l, [DFF, D]) for l in range(2)]
    rline = dscr("rline", [4, 384], F32)

    S = Sched(nc, serial=serial)

    with contextlib.ExitStack() as st:
        def sb(name, shape, dt):
            return st.enter_context(nc.sbuf_tensor(name, shape, dt))

        hT = sb("hT", [P, DC, T], F32)
        hm = sb("hm", [P, 2, DC, T], BF16)
        hn = hm[:, 0]
        yst = sb("yst", [P, DC, T], F32)
        mix = hm[:, 1]
        rstd = sb("rstd", [P, T], F32)
        sqb = sb("sqb", [P, T], F32)
        UB = 35328
        U = sb("U", [P, UB // 2], BF16)
        KT = sb("KT", [P, 4, SEQL], BF16)
        V = sb("V", [P, NKB, 4, 130], BF16)
        BT = sb("BT", [P, 4, 256], F32)
        wring = sb("wring", [P, RING, DC, 512], BF16)
        onesD = sb("onesD", [P, P], BF16)
        ident_f = sb("ident_f", [P, P], F32)
        ident_b = sb("ident_b", [P, P], BF16)
        Jt = sb("Jt", [P, P], F32)
        tri = sb("tri", [P, P], F32)
        stage = sb("stage", [P, P], F32)
        cols = sb("cols", [P, 88], F32)
        lnsc = sb("lnsc", [P, 32], F32)
        bsb = sb("bsb", [P, 512], F32)
        sublng = sb("sublng", [P, P], F32)
        lamraw = sb("lamraw", [P, 256], F32)
        lamt = sb("lamt", [P, 8], F32)
        tab = sb("tab", [32, 4], F32)
        oh = sb("oh", [32, 384], F32)
        line = sb("line", [4, 384], F32)
        Wh = sb("Wh", [P, 256], F32)
        spw = sb("spw", [P, 4, P], F32)
        WsT = sb("WsT", [P, 4, P], BF16)
        poolw = sb("poolw", [P, 4, P], BF16)
        invcnt = sb("invcnt", [P, 4, 16], F32)
        phalo = sb("phalo", [P, 4, 16], F32)
        zhalo = sb("zhalo", [P, 4, 2], F32)
        small = sb("small", [P, 16], F32)

        PS = [st.enter_context(nc.psum_tensor("ps%d" % b, [P, 512], F32)) for b in range(8)]
        PSB = PS[7][:].bitcast(BF16)

        def uview(off, nbytes, dt):
            assert off % 4 == 0 and off + nbytes <= UB
            v = U[:, off // 2:(off + nbytes) // 2]
            if dt == F32:
                v = v.bitcast(F32)
            return v

        r4 = lambda v: v.rearrange("p (a b) -> p a b", a=4)
        uT = r4(uview(0, 8192, F32))
        vtm = [uview(8192 + 2048 * i, 2048, F32) for i in range(2)]
        vn = r4(uview(12288, 4096, BF16))
        pT = r4(uview(16384, 8448, F32))
        wsA = uview(24832, 2112, F32)
        wsB = uview(26944, 2112, F32)
        pooled = r4(uview(29056, 4096, BF16))
        hid = uview(0, 32768, BF16).rearrange("p (a b) -> p a b", a=32)
        QT = r4(uview(0, 4096, BF16))
        cgT = r4(uview(4096, 8192, F32))
        zT = r4(uview(12288, 8224, F32))
        Et = [[uview(20512 + 1024 * (2 * m + b), 1024, BF16) for b in range(2)] for m in range(2)]
        accsb = uview(24608, 4680, F32).rearrange("p (a b) -> p a b", a=9)
        oA = r4(uview(29288, 2048, F32))
        oB = r4(uview(31336, 2048, F32))
        oNb = r4(uview(33384, 1024, BF16))
        rcs = sb("rcs", [P, 16], F32)
        sm = lambda i: small[:, i:i + 1]
        ALLY = ["yst%d" % c for c in range(DC)]

        def cload(dst, src, tok):
            S.dma("sp", "c", DMA(dst, src), writes=[tok])

        cload(ident_f[:], c_ident[:, :], "ident_f")
        cload(Jt[:], c_J[:, :], "Jt")
        cload(tri[:], c_tri[:, :], "tri")
        cload(oh[:, 0:383], c_oh[:, :], "oh")
        cload(invcnt[:].rearrange("p a b -> p (a b)"), c_invcnt[:, :], "invcnt")
        cload(stage[0:64, :], norm_g[:, :], "stage")
        cload(stage[64:68, :], pool_s[:, :], "stage")
        cload(stage[68:80, :], conv_w[:, :], "stage")
        cload(stage[80:84, :], ln_g.rearrange("o (g c) -> (o g) c", g=4), "stage")
        cload(stage[84:88, :], ln_b.rearrange("o (g c) -> (o g) c", g=4), "stage")
        cload(bsb[:], sp_b.partition_broadcast(P), "bsb")
        cload(sublng[:], subln.partition_broadcast(P), "sublng")
        cload(lamraw[:], lam_in.partition_broadcast(P), "lamraw")
        cload(tab[:], rel_tab[:, :], "tab")
        cload(spw[:], sp_w.rearrange("g t s -> t g s"), "spw")
        S.fence(["ident_f", "Jt", "tri", "oh", "invcnt", "stage", "bsb", "sublng", "lamraw", "tab", "spw"], "c")
        S.dma("pool", "c2", DMA(poolw[:], pool_w.rearrange("g c d -> c g d")), writes=["poolw"])

        tile_blocks = []
        for c0 in (1024, 512, 0):
            tile_blocks.append((w_in0[:, c0:c0 + 512], s_in0[:, c0:c0 + 512]))
        for c0 in (0, 512):
            tile_blocks.append((w_out0[:, c0:c0 + 512], s_out0[:, c0:c0 + 512]))
        for b in range(8):
            tile_blocks.append((w1[0][:, b * 512:(b + 1) * 512], s_w1[0][:, b * 512:(b + 1) * 512]))
        for half in range(2):
            for kg in range(4):
                tile_blocks.append((w2[0][kg * 1024:(kg + 1) * 1024, half * 512:(half + 1) * 512],
                                    s_w2[0][kg * 1024:(kg + 1) * 1024, half * 512:(half + 1) * 512]))
        for c0 in (2048, 2560, 0, 512, 1536, 1024):
            tile_blocks.append((w_in1[:, c0:c0 + 512], s_in1[:, c0:c0 + 512]))
        for c0 in (0, 512):
            tile_blocks.append((w_out1[:, c0:c0 + 512], s_out1[:, c0:c0 + 512]))
        for b in range(8):
            tile_blocks.append((w1[1][:, b * 512:(b + 1) * 512], s_w1[1][:, b * 512:(b + 1) * 512]))
        for half in range(2):
            for kg in range(4):
                tile_blocks.append((w2[1][kg * 1024:(kg + 1) * 1024, half * 512:(half + 1) * 512],
                                    s_w2[1][kg * 1024:(kg + 1) * 1024, half * 512:(half + 1) * 512]))
        NBLK = len(tile_blocks)
        S.op("pool", MS(onesD[:], 1.0 / D), writes=["onesD"])
        S.op("pool", MS(V[:, :, :, 128:130], 1.0), writes=["Vones"])
        S.op("pool", MS(phalo[:], 0.0), writes=["phalo"])
        S.op("pool", MS(zhalo[:], 0.0), writes=["zhalo"])
        S.op("pool", MS(line[:], NEG), writes=["line"])
        S.op("pool", MS(small[:, 4:5], EPS), writes=["epsc"])
        cstate = {"n": 0}

        def cast_issue_upto(n):
            while cstate["n"] < min(n, NBLK):
                bi = cstate["n"]
                src, dst = tile_blocks[bi]
                S.dma("pool", "k%d" % bi, DMA(dst, src), writes=["S%d" % bi])
                cstate["n"] += 1


        S.op("dve", CP(ident_b[:], ident_f[:]), reads=["ident_f"], writes=["ident_b"])
        S.op("pe", TR(PS[7][:, 0:88], stage[0:88, :], ident_f[0:88, 0:88]), reads=["stage", "ident_f"], writes=["ps7"])
        S.op("dve", CP(cols[:], PS[7][:, 0:88]), reads=["ps7"], writes=["cols"])
        gcol = lambda l, n, c: cols[:, (l * 4 + n) * 8 + c:(l * 4 + n) * 8 + c + 1]
        pscol = lambda j: cols[:, 64 + j:65 + j]
        cwcol = lambda k, j: cols[:, 68 + k * 4 + j:69 + k * 4 + j]
        lngcol = lambda g: cols[:, 80 + g:81 + g]
        for g in range(4):
            S.op("dve", TT(spw[:, g, :], spw[:, g, :], tri[:], ALU.mult), reads=["spw", "tri"], writes=["spw"])
        for g in range(4):
            S.op("pe", TR(PS[6][:, g * P:(g + 1) * P], spw[:, g, :], ident_f[:]), reads=["spw", "ident_f"], writes=["ps6"])
        S.op("dve", CP(WsT[:].rearrange("p a b -> p (a b)"), PS[6][:]), reads=["ps6"], writes=["WsT"])
        S.op("dve", TS(lnsc[:, 24:28], cols[:, 84:88], float(D), None, ALU.mult), reads=["cols"], writes=["lnb1k"])
        for g in range(4):
            S.op("pe", MM(PS[6][:, g * P:(g + 1) * P], onesD[:], WsT[:, g, :]), reads=["onesD", "WsT"], writes=["ps6"], inc=(g == 3))
        for g in range(4):
            S.op("dve", STT(bsb[:, g * P:(g + 1) * P], PS[6][:, g * P:(g + 1) * P], lnsc[:, 24 + g:25 + g], bsb[:, g * P:(g + 1) * P], ALU.mult, ALU.add),
                 reads=["ps6", "lnb1k", "bsb"], writes=["bsb"])
        S.op("pe", MM(PS[5][0:4, 0:383], tab[:, :], oh[:, 0:383]), reads=["tab", "oh"], writes=["ps5"])
        S.op("dve", CP(small[0:4, 0:1], PS[5][0:4, 382:383]), reads=["ps5"], writes=["small0"])
        S.op("dve", TS(line[:, 127:383], PS[5][0:4, 127:383], small[0:4, 0:1], 8.0, ALU.subtract, ALU.mult),
             reads=["ps5", "small0", "line"], writes=["line"])
        S.dma("sp", "c3", DMA(rline[:, :], line[:]), reads=["line"], writes=["rline"])
        for h in range(4):
            src = bass.AP(rline.tensor, h * 384, [[1, P], [1, 256]])
            S.dma("sp", "c4", DMA(BT[:, h, :], src), reads=["rline"], writes=["BTraw"])
        S.fence(["BTraw"], "c4")

        def finish_bias_tiles():
            for h in range(4):
                S.op("pe", MM(PS[7][:, 0:256], Jt[:], BT[:, h, :]), reads=["Jt", "BTraw", "BT"], writes=["ps7"])
                S.op("dve", CP(BT[:, h, :], PS[7][:, 0:256]), reads=["ps7"], writes=["BT", "BTraw"])

        S.op("dve", TT(lamraw[:, 0:64], lamraw[:, 0:64], lamraw[:, 64:128], ALU.mult), reads=["lamraw"], writes=["lamraw"])
        S.op("dve", TT(lamraw[:, 128:192], lamraw[:, 128:192], lamraw[:, 192:256], ALU.mult), reads=["lamraw"], writes=["lamraw"])
        S.op("dve", RSUM(lamt[:, 0:1], lamraw[:, 0:64]), reads=["lamraw"], writes=["lamt"])
        S.op("dve", RSUM(lamt[:, 1:2], lamraw[:, 128:192]), reads=["lamraw", "lamt"], writes=["lamt"])
        S.op("act", ACT(lamt[:, 2:4], lamt[:, 0:2], AF.Exp), reads=["lamt"], writes=["lamt"])
        S.op("dve", TT(lamt[:, 4:5], lamt[:, 3:4], lamt[:, 2:3], ALU.subtract), reads=["lamt"], writes=["lamt"])
        S.op("dve", TS(small[:, 2:3], lamt[:, 4:5], -LAMBDA_INIT, None, ALU.add), reads=["lamt"], writes=["nlam"])
        S.op("dve", TS(sublng[:], sublng[:], 1.0 - LAMBDA_INIT, None, ALU.mult), reads=["sublng"], writes=["sublng"])
        nlam = small[:, 2:3]

        blocks = []
        for i in range(NT):
            for bi, (src, dst) in enumerate(tile_blocks):
                blocks.append((dst, "S%d" % bi))
        wstate = {"issued": 0, "used": 0}

        def w_issue_upto(n):
            while wstate["issued"] < min(n, len(blocks)):
                k = wstate["issued"]
                slot = k % RING
                src, tok = blocks[k]
                if k < NBLK:
                    f32src, scr = tile_blocks[k]
                    S.dma("pool", "wc%d" % slot, DMA(wring[:, slot], f32src.rearrange("(c p) n -> p c n", p=P)), writes=["w%d" % slot])
                    S.dma("sp", "k%d" % k, DMA(scr.rearrange("(c p) n -> p c n", p=P), wring[:, slot]), reads=["w%d" % slot], writes=["S%d" % k])
                else:
                    sv = src.rearrange("(c p) n -> p c n", p=P)
                    S.dma("sp", "w%d" % slot, DMA(wring[:, slot], sv), reads=[tok], writes=["w%d" % slot])
                wstate["issued"] += 1

        def w_next():
            k = wstate["used"]
            w_issue_upto(k + RING)
            wstate["used"] += 1
            slot = k % RING
            return wring[:, slot], "w%d" % slot

        rot = {"i": 0}

        def nextbank(n=6):
            b = rot["i"] % n
            rot["i"] += 1
            return b

        def stats_and_rstd():
            for c in range(DC):
                S.op("pe", MM(PS[6][:], onesD[:], hn[:, c, :], start=(c == 0), stop=(c == DC - 1)),
                     reads=["onesD", "hn%d" % c], writes=["ps6"], inc=(c == DC - 1))
            S.op("act", ACT(sqb[:], PS[6][:], AF.Ln, bias=EPS, scale=1.0), reads=["ps6"], writes=["sqb"])
            S.op("act", ACT(rstd[:], sqb[:], AF.Exp, scale=-0.5), reads=["sqb"], writes=["rstd"])

        def prenorm(l, n, have_squares=False):
            for c in range(DC):
                if not have_squares:
                    S.op("act", ACT(hn[:, c, :], hT[:, c, :], AF.Square), reads=["hT%d" % c], writes=["hn%d" % c])
            stats_and_rstd()
            for c in range(DC):
                S.op("dve", STT(hn[:, c, :], hT[:, c, :], gcol(l, n, c), rstd[:], ALU.mult, ALU.mult),
                     reads=["hT%d" % c, "cols", "rstd"], writes=["hn%d" % c])

        def evac_post(b, c, l, n):
            S.op("act", ACT(hn[:, c, :], PS[b][:], AF.Square), reads=["ps%d" % b], writes=["hn%d" % c])
            S.op("act", ACT(yst[:, c, :], PS[b][:], AF.Copy, scale=gcol(l, n, c)), reads=["ps%d" % b, "cols"], writes=["yst%d" % c])

        def postnorm(l, n):
            stats_and_rstd()
            for c in range(DC):
                S.op("dve", TT(yst[:, c, :], yst[:, c, :], rstd[:], ALU.mult), reads=["yst%d" % c, "rstd"], writes=["yst%d" % c])
                S.op("dve", TT(hT[:, c, :], hT[:, c, :], yst[:, c, :], ALU.add),
                     reads=["yst%d" % c, "hT%d" % c], writes=["hT%d" % c])

        def mm_fm(wblk, wtok, src, src_tok, j, b, nk=DC, start=True, stop=True, koff=0):
            for k in range(nk):
                S.op("pe", MM(PS[b][:], wblk[:, k, j * P:(j + 1) * P], src[:, koff + k, :],
                              start=(start and k == 0), stop=(stop and k == nk - 1)),
                     reads=[wtok, "%s%d" % (src_tok, koff + k)], writes=["ps%d" % b], inc=(k == nk - 1))

        def mm_fm_kouter(wblk, wtok, src, src_tok, banks, korder=None, before_last=None):
            korder = list(range(DC)) if korder is None else korder
            for i, k in enumerate(korder):
                if i == DC - 1 and before_last is not None:
                    before_last()
                for j, b in enumerate(banks):
                    S.op("pe", MM(PS[b][:], wblk[:, k, j * P:(j + 1) * P], src[:, k, :], start=(i == 0), stop=(i == DC - 1)),
                         reads=[wtok, "%s%d" % (src_tok, k)], writes=["ps%d" % b], inc=(i == DC - 1))

        def mm_tm(wblk, wtok, tc, b):
            for k in range(DC):
                S.op("pe", MM(PS[b][:], hn[:, k, tc * P:(tc + 1) * P], wblk[:, k, :], start=(k == 0), stop=(k == DC - 1)),
                     reads=[wtok, "hn%d" % k], writes=["ps%d" % b], inc=(k == DC - 1))

        def wout_and_post(l, korder, before_last=None):
            for half in range(2):
                wblk, wtok = w_next()
                if half == 0:
                    banks = [nextbank() for j in range(4)]
                    mm_fm_kouter(wblk, wtok, mix, "mix", banks, korder, before_last)
                    for j in range(4):
                        evac_post(banks[j], j, l, 1)
                else:
                    for j in range(4):
                        b = nextbank()
                        mm_fm(wblk, wtok, mix, "mix", j, b)
                        evac_post(b, 4 + j, l, 1)
            postnorm(l, 1)

        def ffn(l, after_w2=None):
            for c in range(DC):
                S.op("act", ACT(hn[:, c, :], hT[:, c, :], AF.Copy, scale=gcol(l, 2, c)), reads=["hT%d" % c, "cols"], writes=["hn%d" % c])
                S.op("act", ACT(mix[:, c, :], hT[:, c, :], AF.Square), reads=["hT%d" % c], writes=["mix%d" % c])
            S.alias(["hid%d" % j for j in range(32)])
            for blk in range(8):
                wblk, wtok = w_next()
                banks = [nextbank() for j in range(4)]
                if blk == 0:
                    mm_fm_kouter(wblk, wtok, hn, "hn", banks)
                for j in range(4):
                    hc = blk * 4 + j
                    b = banks[j]
                    if blk != 0:
                        mm_fm(wblk, wtok, hn, "hn", j, b)
                    r = hc % DC
                    S.op("act", ACT(yst[:, r, :], PS[b][:], AF.Relu), reads=["ps%d" % b], writes=["yst%d" % r])
                    eng = "dve"
                    S.op(eng, TT(hid[:, hc, :], yst[:, r, :], yst[:, r, :], ALU.mult), reads=["yst%d" % r], writes=["hid%d" % hc])
                if blk == 1:
                    for c in range(DC):
                        S.op("pe", MM(PS[6][:], onesD[:], mix[:, c, :], start=(c == 0), stop=(c == DC - 1)),
                             reads=["onesD", "mix%d" % c], writes=["ps6"], inc=(c == DC - 1))
                    S.op("act", ACT(rstd[:], PS[6][:], AF.Square, bias=1e-3 * EPS, scale=1e-3), reads=["ps6"], writes=["rstd"])
            for half in range(2):
                banks = [half * 4 + j for j in range(4)]
                for kg in range(4):
                    wblk, wtok = w_next()
                    for j in range(4):
                        mm_fm(wblk, wtok, hid, "hid", j, banks[j], nk=8, start=(kg == 0), stop=(kg == 3), koff=kg * 8)
                if half == 1 and after_w2 is not None:
                    after_w2()
                for j in range(4):
                    evac_post(banks[j], half * 4 + j, l, 3)
            for c in range(DC):
                S.op("pe", MM(PS[6][:], onesD[:], hn[:, c, :], start=(c == 0), stop=(c == DC - 1)),
                     reads=["onesD", "hn%d" % c], writes=["ps6"], inc=(c == DC - 1))
            S.op("dve", TT(sqb[:], PS[6][:], rstd[:], ALU.add), reads=["ps6", "rstd"], writes=["sqb"])
            S.op("act", ACT(sqb[:], sqb[:], AF.Ln), reads=["sqb"], writes=["sqb"])
            S.op("act", ACT(rstd[:], sqb[:], AF.Exp, scale=-0.5), reads=["sqb"], writes=["rstd"])
            for c in range(DC):
                S.op("dve", TT(yst[:, c, :], yst[:, c, :], rstd[:], ALU.mult), reads=["yst%d" % c, "rstd"], writes=["yst%d" % c])
                S.op("dve", TT(hT[:, c, :], hT[:, c, :], yst[:, c, :], ALU.add),
                     reads=["yst%d" % c, "hT%d" % c], writes=["hT%d" % c])

        ALLHM = ["hn%d" % c for c in range(DC)] + ["mix%d" % c for c in range(DC)]
        xv = U[:, 0:8192].bitcast(F32).rearrange("p (a d) -> p a d", a=4)
        ov = hm[:].rearrange("p a c t -> p (a c t)").bitcast(F32).rearrange("p (a d) -> p a d", a=4)

        def xload(it):
            t0 = it * T
            S.alias(["xbuf"])
            S.dma("sp", "x", DMA(xv, x[t0:t0 + T, :].rearrange("(a p) d -> p a d", p=P)), writes=["xbuf"])

        S.tag = 'xload'
        xload(0)
        for it in range(NT):
            t0 = it * T
            S.tag = 'xload'
            for c in range(DC):
                b = nextbank()
                for a in range(4):
                    S.op("pe", TR(PS[b][:, a * P:(a + 1) * P], xv[:, a, c * P:(c + 1) * P], ident_f[:]),
                         reads=["xbuf", "ident_f"], writes=["ps%d" % b], inc=(a == 3))
                if c % 2 == 0:
                    S.op("dve", CP(hT[:, c, :], PS[b][:]), reads=["ps%d" % b], writes=["hT%d" % c])
                    S.op("dve", TT(hn[:, c, :], hT[:, c, :], hT[:, c, :], ALU.mult), reads=["hT%d" % c], writes=["hn%d" % c])
                else:
                    S.op("act", ACT(hn[:, c, :], PS[b][:], AF.Square), reads=["ps%d" % b], writes=["hn%d" % c])
                    S.op("act", ACT(hT[:, c, :], PS[b][:], AF.Copy), reads=["ps%d" % b], writes=["hT%d" % c])

            S.tag = 'L0.pre'
            prenorm(0, 0, have_squares=True)
            S.tag = 'L0.mix'
            S.alias(["uT", "vtm0", "vtm1", "vn0", "vn1", "vn2", "vn3", "pT", "wsA", "wsB", "pooled"])
            wblk, wtok = w_next()
            S.op("pool", CP(pT[:, :, 0:16], phalo[:]), reads=["phalo"], writes=["pT"])
            banks = [nextbank() for j in range(4)]
            mm_fm_kouter(wblk, wtok, hn, "hn", banks)
            for j in range(4):
                S.op("act", ACT(pT[:, j, 16:16 + T], PS[banks[j]][:], AF.Copy), reads=["ps%d" % banks[j]], writes=["pT"])
            S.op("pool", CP(phalo[:], pT[:, :, T:T + 16]), reads=["pT"], writes=["phalo"])
            for j in range(4):
                W_ = 2 ** (j + 1)
                cur, curtok = pT[:, j, :], "pT"
                bufs = [(wsA, "wsA"), (wsB, "wsB")]
                sh, lvl = 1, 0
                while sh < W_:
                    dst, dtok = bufs[lvl % 2]
                    lo = 2 * sh - 1
                    S.op("pool", TT(dst[:, lo:16 + T], cur[:, lo:16 + T], cur[:, lo - sh:16 + T - sh], ALU.add), reads=[curtok], writes=[dtok])
                    cur, curtok = dst, dtok
                    sh *= 2
                    lvl += 1
                if it == 0:
                    S.op("pool", TT(cur[:, 16:32], cur[:, 16:32], invcnt[:, j, :], ALU.mult), reads=[curtok, "invcnt"], writes=[curtok])
                if j < 3:
                    S.op("dve", STT(pooled[:, j, :], cur[:, 16:16 + T], 1.0 / W_, pT[:, j, 16:16 + T], ALU.mult, ALU.subtract), reads=[curtok, "pT"], writes=["pooled"])
                else:
                    pool_last = (cur, curtok, W_)
            wblk, wtok = w_next()
            vbanks = [nextbank() for tc in range(TC)]
            for k in range(DC):
                for tc in range(TC):
                    S.op("pe", MM(PS[vbanks[tc]][:], hn[:, k, tc * P:(tc + 1) * P], wblk[:, k, :], start=(k == 0), stop=(k == DC - 1)),
                         reads=[wtok, "hn%d" % k], writes=["ps%d" % vbanks[tc]], inc=(k == DC - 1))
            for tc in range(TC):
                b = vbanks[tc]
                S.op("act", ACT(yst[:, tc, :], PS[b][:], AF.Gelu_apprx_tanh, accum_out=lnsc[:, tc:tc + 1]), reads=["ps%d" % b], writes=["yst%d" % tc, "lns1_%d" % tc])
                S.op("act", ACT(vn[:, tc, :], yst[:, tc, :], AF.Square, accum_out=lnsc[:, 4 + tc:5 + tc]), reads=["yst%d" % tc], writes=["vn%d" % tc, "lns2_%d" % tc])
            LNS = ["lns1_%d" % t for t in range(4)] + ["lns2_%d" % t for t in range(4)]
            S.op("dve", TS(lnsc[:, 8:12], lnsc[:, 0:4], 1.0 / 512, None, ALU.mult), reads=LNS, writes=["lnmu"])
            S.op("dve", TT(lnsc[:, 12:16], lnsc[:, 8:12], lnsc[:, 8:12], ALU.mult), reads=["lnmu"], writes=["lnvar"])
            S.op("dve", STT(lnsc[:, 12:16], lnsc[:, 4:8], 1.0 / 512, lnsc[:, 12:16], ALU.mult, ALU.subtract), reads=LNS + ["lnvar"], writes=["lnvar"])
            S.op("act", ACT(lnsc[:, 20:24], lnsc[:, 12:16], AF.Ln, bias=EPS, scale=1.0), reads=["lnvar"], writes=["lnln"])
            S.op("act", ACT(lnsc[:, 12:16], lnsc[:, 20:24], AF.Exp, scale=-0.5), reads=["lnln"], writes=["lnvar"])
            S.op("dve", STT(lnsc[:, 16:20], lnsc[:, 8:12], -1.0, lnsc[:, 12:16], ALU.mult, ALU.mult), reads=["lnmu", "lnvar"], writes=["lnnmr"])
            for tc in range(TC):
                S.op("dve", TS(vn[:, tc, :], yst[:, tc, :], lnsc[:, 12 + tc:13 + tc], lnsc[:, 16 + tc:17 + tc], ALU.mult, ALU.add),
                     reads=["yst%d" % tc, "lnvar", "lnnmr"], writes=["vn%d" % tc])
            wblk, wtok = w_next()
            for j in range(4):
                b = nextbank()
                mm_fm(wblk, wtok, hn, "hn", j, b)
                S.op("act", ACT(uT[:, j, :], PS[b][:], AF.Gelu_apprx_tanh), reads=["ps%d" % b], writes=["uT"])
            for tc in range(TC):
                gb = 6 + (tc % 2)
                for g in range(4):
                    S.op("pe", MM(PS[gb][:, g * P:(g + 1) * P], vn[:, tc, g * P:(g + 1) * P], WsT[:, g, :]),
                         reads=["vn%d" % tc, "WsT"], writes=["ps%d" % gb], inc=(g == 3))
                vt = vtm[tc % 2]
                for g in range(4):
                    S.op("dve", STT(vt[:, g * P:(g + 1) * P], PS[gb][:, g * P:(g + 1) * P], lngcol(g), bsb[:, g * P:(g + 1) * P], ALU.mult, ALU.add),
                         reads=["ps%d" % gb, "bsb", "cols"], writes=["vtm%d" % (tc % 2)])
                S.op("dve", TT(mix[:, 0:4, tc * P:(tc + 1) * P], uT[:, :, tc * P:(tc + 1) * P], r4(vt), ALU.mult),
                     reads=["vtm%d" % (tc % 2), "uT"], writes=["mix%d" % c for c in range(4)])
            cur, curtok, W_ = pool_last
            S.op("dve", STT(pooled[:, 3, :], cur[:, 16:16 + T], 1.0 / W_, pT[:, 3, 16:16 + T], ALU.mult, ALU.subtract), reads=[curtok, "pT"], writes=["pooled"])
            for j in range(4):
                b = nextbank()
                S.op("pe", MM(PS[b][:], poolw[:, j, :], pooled[:, j, :]), reads=["poolw", "pooled"], writes=["ps%d" % b])
                S.op("act", ACT(mix[:, 4 + j, :], PS[b][:], AF.Copy, scale=pscol(j)), reads=["ps%d" % b, "cols"], writes=["mix%d" % (4 + j)])
            S.tag = 'L0.wout'
            wout_and_post(0, [0, 1, 2, 3, 4, 5, 6, 7])
            S.tag = 'F0'
            if stop_after >= 2:
                ffn(0)
            else:
                [w_next() for _ in range(16)]

            if stop_after >= 3:
                S.tag = 'L1.pre'
                prenorm(1, 0)
                S.tag = 'L1.proj'
                S.alias(["QT%d" % h for h in range(4)] + ["cgT", "zT", "E00", "E01", "E10", "E11", "accsb", "oA", "oB", "oNb"])
                wblk, wtok = w_next()
                banks = [nextbank(4) for j in range(4)]
                mm_fm_kouter(wblk, wtok, hn, "hn", banks)
                for j in range(4):
                    S.op("act", ACT(cgT[:, j, :], PS[banks[j]][:], AF.Copy), reads=["ps%d" % banks[j]], writes=["cgT"])
                wblk, wtok = w_next()
                S.op("pool", CP(zT[:, :, 0:2], zhalo[:]), reads=["zhalo"], writes=["zT"])
                for j in range(4):
                    b = nextbank(4)
                    mm_fm(wblk, wtok, hn, "hn", j, b)
                    S.op("dve", TT(zT[:, j, 2:2 + T], PS[b][:], cgT[:, j, :], ALU.mult), reads=["ps%d" % b, "cgT"], writes=["zT"])
                S.op("pool", CP(zhalo[:], zT[:, :, T:T + 2]), reads=["zT"], writes=["zhalo"])
                for j in range(4):
                    S.op("dve", TS(cgT[:, j, :], zT[:, j, 2:2 + T], cwcol(2, j), None, ALU.mult), reads=["zT", "cols"], writes=["cgT"])
                    S.op("dve", STT(cgT[:, j, :], zT[:, j, 1:1 + T], cwcol(1, j), cgT[:, j, :], ALU.mult, ALU.add), reads=["zT", "cols", "cgT"], writes=["cgT"])
                    S.op("dve", STT(cgT[:, j, :], zT[:, j, 0:T], cwcol(0, j), cgT[:, j, :], ALU.mult, ALU.add), reads=["zT", "cols", "cgT"], writes=["cgT"])
                wblk, wtok = w_next()
                for h in range(4):
                    b = nextbank(4)
                    mm_fm(wblk, wtok, hn, "hn", h, b)
                    S.op("act", ACT(QT[:, h, :], PS[b][:], AF.Copy), reads=["ps%d" % b], writes=["QT%d" % h])
                wblk, wtok = w_next()
                for h in range(4):
                    b = nextbank(4)
                    mm_fm(wblk, wtok, hn, "hn", h, b)
                    S.op("act", ACT(KT[:, h, t0:t0 + T], PS[b][:], AF.Copy), reads=["ps%d" % b], writes=["KT%d_%d" % (it, h)])
                wblk_bg, wtok_bg = w_next()
                for j in range(4):
                    b = nextbank(4)
                    mm_fm(wblk_bg, wtok_bg, hn, "hn", j, b)
                    S.op("dve", TT(mix[:, 4 + j, :], PS[b][:], cgT[:, j, :], ALU.mult), reads=["ps%d" % b, "cgT"], writes=["mix%d" % (4 + j)])
                wblk, wtok = w_next()
                for tc in range(TC):
                    b = nextbank(4)
                    mm_tm(wblk, wtok, tc, b)
                    kb = it * TC + tc
                    S.op("act", ACT(V[:, kb, :, 0:128], r4(PS[b][:]), AF.Copy), reads=["ps%d" % b, "Vones"], writes=["V%d" % kb])

                S.tag = 'L1.attn'
                if it == 0:
                    finish_bias_tiles()
                nkb = (it + 1) * TC

                def accv(m, qi):
                    a_ = m * 4 + qi
                    return PS[4 + a_ // 3][:, (a_ % 3) * 130:(a_ % 3) * 130 + 129]

                acctok = lambda m, qi: "ps%d" % (4 + (m * 4 + qi) // 3)
                est = {"n": 0}

                def scores(h, kb):
                    r = kb - it * TC
                    q0 = max(r, 0)
                    nq = TC - q0
                    ncol = nq * P
                    eb = est["n"] % 2
                    est["n"] += 1
                    for m in range(2):
                        sbk = 2 * m + eb
                        S.op("pe", MM(PS[sbk][:, 0:ncol], KT[m * 64:(m + 1) * 64, h, kb * P:(kb + 1) * P],
                                      QT[m * 64:(m + 1) * 64, h, q0 * P:q0 * P + ncol]),
                             reads=["KT%d_%d" % (kb // TC, h), "QT%d" % h], writes=["ps%d" % sbk])
                        for a in range(nq):
                            rel = it * TC + q0 + a - kb
                            if rel in (0, 1):
                                S.op("dve", TT(PS[sbk][:, a * P:(a + 1) * P], PS[sbk][:, a * P:(a + 1) * P], BT[:, h, rel * P:(rel + 1) * P], ALU.add),
                                     reads=["BT", "ps%d" % sbk], writes=["ps%d" % sbk])
                        S.op("act", ACT(Et[m][eb][:, 0:ncol], PS[sbk][:, 0:ncol], AF.Exp, scale=0.125), reads=["ps%d" % sbk], writes=["E%d%d" % (m, eb)])
                    return eb, q0, nq

                def av(h, kb, eb, q0, nq):
                    for m in range(2):
                        for a in range(nq):
                            qi = q0 + a
                            last = (m == 1 and a == nq - 1)
                            S.op("pe", MM(accv(m, qi), Et[m][eb][:, a * P:(a + 1) * P], V[:, kb, h, 0:129],
                                          start=(kb == 0 and (m * 4 + qi) % 3 == 0), stop=False, skip_group_check=True),
                                 reads=["E%d%d" % (m, eb), "V%d" % kb, "Vones"], writes=[acctok(m, qi)], inc=last)

                def fin_math(h):
                    rcv = rcs[:, 0:8].rearrange("p (a b) -> p a b", b=1)
                    ssv = rcs[:, 8:12].rearrange("p (a b) -> p a b", b=1)
                    S.op("dve", RCP(rcv, accsb[:, 0:8, 128:129]), reads=["accsb"], writes=["rcs"])
                    S.op("dve", TS(rcs[:, 4:8], rcs[:, 4:8], nlam, None, ALU.mult), reads=["rcs", "nlam"], writes=["rcs"])
                    S.op("dve", TT(oA[:], accsb[:, 0:4, 0:128], rcv[:, 0:4, :].to_broadcast([P, 4, P]), ALU.mult), reads=["accsb", "rcs"], writes=["oA"])
                    S.op("dve", TT(oB[:], accsb[:, 4:8, 0:128], rcv[:, 4:8, :].to_broadcast([P, 4, P]), ALU.mult), reads=["accsb", "rcs"], writes=["oB"])
                    S.op("dve", TT(oA[:], oA[:], oB[:], ALU.add), reads=["oA", "oB"], writes=["oA"])
                    S.op("dve", TT(oB[:], oA[:], oA[:], ALU.mult), reads=["oA"], writes=["oB"])
                    S.op("dve", RSUM(rcs[:, 8:12], oB[:]), reads=["oB"], writes=["rcs2"])

                def fin_math_b(h):
                    ssv = rcs[:, 8:12].rearrange("p (a b) -> p a b", b=1)
                    S.op("act", ACT(rcs[:, 12:16], rcs[:, 8:12], AF.Ln, bias=EPS, scale=1.0 / 128), reads=["rcs2"], writes=["rcs3"])
                    S.op("act", ACT(rcs[:, 8:12], rcs[:, 12:16], AF.Exp, scale=-0.5), reads=["rcs3"], writes=["rcs2"])
                    S.op("dve", TT(oA[:], oA[:], ssv.to_broadcast([P, 4, P]), ALU.mult), reads=["oA", "rcs2"], writes=["oA"])
                    S.op("dve", TT(oNb[:], oA[:], sublng[:].unsqueeze(1).to_broadcast([P, 4, P]), ALU.mult), reads=["oA", "sublng"], writes=["oNb"])

                def fin_pe(h):
                    for qi in range(TC):
                        S.op("pe", TR(PSB[:, qi * P:(qi + 1) * P], oNb[:, qi, :], ident_b[:]), reads=["oNb", "ident_b"], writes=["ps7"], inc=(qi == TC - 1))
                    S.op("act", ACT(mix[:, h, :], PSB[:, 0:512], AF.Copy), reads=["ps7"], writes=["mix%d" % h])

                for h in range(4):
                    if h > 0:
                        fin_math(h - 1)
                    cur = scores(h, 0)
                    for kb in range(nkb):
                        nxt = scores(h, kb + 1) if kb + 1 < nkb else None
                        av(h, kb, *cur)
                        cur = nxt
                        if h > 0 and kb == min(1, nkb - 1):
                            fin_math_b(h - 1)
                        if h > 0 and kb == min(3, nkb - 1):
                            fin_pe(h - 1)
                    S.op("dve", CP(accsb[:, 0:3, :].rearrange("p a b -> p (a b)"), PS[4][:, 0:390]), reads=["ps4"], writes=["accsb"])
                    S.op("act", ACT(accsb[:, 3:6, :].rearrange("p a b -> p (a b)"), PS[5][:, 0:390], AF.Copy), reads=["ps5"], writes=["accsb"])
                    S.op("dve", CP(accsb[:, 6:9, :].rearrange("p a b -> p (a b)"), PS[6][:, 0:390]), reads=["ps6"], writes=["accsb"])
                fin_math(3)
                fin_math_b(3)
                S.tag = 'L1.wout'
                wout_and_post(1, [4, 5, 6, 7, 0, 1, 2, 3], before_last=lambda: fin_pe(3))
            else:
                [w_next() for _ in range(8)]
            S.tag = 'F1'
            if stop_after >= 4:
                ffn(1, after_w2=(lambda: xload(it + 1)) if it + 1 < NT else None)
            else:
                [w_next() for _ in range(16)]

            S.tag = 'store'
            for c in range(DC):
                b = nextbank()
                for a in range(4):
                    S.op("pe", TR(PS[b][:, a * P:(a + 1) * P], hT[:, c, a * P:(a + 1) * P], ident_f[:]),
                         reads=["hT%d" % c, "ident_f"], writes=["ps%d" % b], inc=(a == 3))
                if c % 2 == 0:
                    S.op("dve", CP(ov[:, :, c * P:(c + 1) * P], r4(PS[b][:])), reads=["ps%d" % b], writes=ALLHM)
                else:
                    S.op("act", ACT(ov[:, :, c * P:(c + 1) * P], r4(PS[b][:]), AF.Copy), reads=["ps%d" % b], writes=ALLHM)
            S.dma("sp", "o", DMA(out[t0:t0 + T, :].rearrange("(a p) d -> p a d", p=P), ov), reads=ALLHM)

        S.wait_all("sp")
        S.emit()
    return nc, S


_CACHE = {}


def kernel(**inputs):
    n = 8
    x = np.ascontiguousarray(inputs["x"], dtype=np.float32)
    common = dict(host_consts())
    f = lambda k: np.ascontiguousarray(inputs[k], dtype=np.float32)
    common["even_w_in"] = f("even_w_in")[0]
    common["even_w_out"] = f("even_w_out")[0]
    common["odd_w_in"] = f("odd_w_in")[0]
    common["odd_w_out"] = f("odd_w_out")[0]
    for l in range(2):
        common["ffn_w1_%d" % l] = f("ffn_w1")[l]
        common["ffn_w2_%d" % l] = f("ffn_w2")[l]
    common["norm_g"] = f("norm_g").reshape(64, 128)
    common["even_ln_g"] = f("even_ln_g").reshape(1, 512)
    common["even_ln_b"] = f("even_ln_b").reshape(1, 512)
    common["even_spatial_w"] = f("even_spatial_w")[0]
    common["even_spatial_b"] = f("even_spatial_b").reshape(1, 512)
    common["even_pool_w"] = f("even_pool_w")[0]
    common["even_pool_scale"] = f("even_pool_scale").reshape(4, 128)
    common["odd_lambda"] = f("odd_lambda").reshape(1, 256)
    common["odd_subln_g"] = f("odd_subln_g").reshape(1, 128)
    common["odd_conv_w"] = f("odd_conv_w").reshape(12, 128)
    common["rel_bias_table"] = f("rel_bias_table")
    if "nc" not in _CACHE:
        _CACHE["nc"] = build()[0]
    nc = _CACHE["nc"]
    in_maps = []
    for b in range(n):
        m = dict(common)
        m["x"] = x[b]
        in_maps.append(m)
    res = run_bass_kernel_spmd(nc, in_maps, core_ids=list(range(n)))
    return np.stack([np.asarray(r["out"], dtype=np.float32) for r in res.results], axis=0)
```

```python
import math
import contextlib
import numpy as np
import concourse.bass as bass
import concourse.mybir as mybir
from concourse.bass_utils import run_bass_kernel_spmd

F32 = mybir.dt.float32
BF16 = mybir.dt.bfloat16
AF = mybir.ActivationFunctionType
ALU = mybir.AluOpType
AX = mybir.AxisListType

D = 1024
SEQ = 4096
T = 512
P = 128
DC = 8
TC = 4
DFF = 4096
EPS = 1e-6
NEG = -30000.0
LAMBDA_INIT = 0.8 - 0.6 * math.exp(-0.3 * 1)
RING = 4

ENGS = ("pe", "act", "dve", "pool", "sp")
STRICT_OWN = True


class Sched:
    def __init__(self, nc, serial=False):
        self.nc = nc
        self.serial = serial
        self.ops = {e: [] for e in ENGS}
        self.cnt = {e: 0 for e in ENGS}
        self.dcnt = {}
        self.known = {e: {} for e in ENGS}
        self.tok_w = {}
        self.tok_r = {}
        self.last = None
        self.tag = ''
        self.tags = {e: [] for e in ENGS}

    def _collect(self, reads, writes):
        deps = {}

        def add(k, v):
            if deps.get(k, -1) < v:
                deps[k] = v
        for t in reads:
            for k, v in self.tok_w.get(t, {}).items():
                add(k, v)
        for t in writes:
            for k, v in self.tok_w.get(t, {}).items():
                add(k, v)
            for k, v in self.tok_r.get(t, {}).items():
                add(k, v)
        if self.serial and self.last is not None:
            add(*self.last)
        return deps

    def _commit(self, me, reads, writes):
        for t in writes:
            self.tok_w[t] = {me[0]: me[1]}
            self.tok_r[t] = {}
        for t in reads:
            r = self.tok_r.setdefault(t, {})
            if r.get(me[0], -1) < me[1]:
                r[me[0]] = me[1]
        self.last = me

    def _waits(self, eng, deps):
        waits = []
        kn = self.known[eng]
        for k, v in deps.items():
            if k == ("e", eng):
                if eng == "pe" or (v < self.cnt[eng] and not STRICT_OWN):
                    continue
            if kn.get(k, -1) >= v:
                continue
            kn[k] = v
            waits.append((k, v))
        return waits

    def op(self, eng, fn, reads=(), writes=(), inc=True):
        psr = [t for t in reads if t.startswith("ps")]
        if psr:
            reads = [t for t in reads if not t.startswith("ps")]
            writes = list(writes) + [t for t in psr if t not in writes]
        deps = self._collect(reads, writes)
        waits = self._waits(eng, deps)
        idx = self.cnt[eng] + 1
        if inc:
            self.cnt[eng] = idx
        me = (("e", eng), idx)
        self.ops[eng].append((waits, fn, ("e", eng) if inc else None))
        self.tags[eng].append(self.tag)
        self._commit(me, reads, writes)

    def dma(self, eng, sem, fn, reads=(), writes=()):
        deps = self._collect(reads, writes)
        waits = self._waits(eng, deps)
        v = self.dcnt.get(sem, 0) + 16
        self.dcnt[sem] = v
        me = (("d", sem), v)
        self.ops[eng].append((waits, fn, ("d", sem)))
        self._commit(me, reads, writes)

    def fence(self, tokens, sem):
        for t in tokens:
            self.tok_w[t] = {("d", sem): self.dcnt[sem]}

    def alias(self, tokens):
        dep = {("e", e): self.cnt[e] for e in ENGS if e != "sp" and self.cnt[e] > 0}
        for t in tokens:
            self.tok_w[t] = dict(dep)
            self.tok_r[t] = {}

    def wait_all(self, eng):
        deps = {}
        for e in ENGS:
            if e != "sp" and e != eng and self.cnt[e] > 0:
                deps[("e", e)] = self.cnt[e]
        for s, v in self.dcnt.items():
            deps[("d", s)] = v
        self.ops[eng].append((list(deps.items()), None, None))

    def emit(self):
        nc = self.nc
        with contextlib.ExitStack() as st:
            sems = {}
            for e in ENGS:
                if e != "sp":
                    sems[("e", e)] = st.enter_context(nc.semaphore("P_" + e))
            for s in self.dcnt:
                sems[("d", s)] = st.enter_context(nc.semaphore("D_" + s))
            block = st.enter_context(nc.Block())

            def run(engname, engobj):
                for waits, fn, incspec in self.ops[engname]:
                    for k, v in waits:
                        engobj.wait_ge(sems[k], v)
                    if fn is None:
                        continue
                    ins = fn(engobj)
                    if incspec is not None:
                        ins.then_inc(sems[incspec], 16 if incspec[0] == "d" else 1)

            @block.tensor
            def _(e):
                run("pe", e)

            @block.scalar
            def _(e):
                run("act", e)

            @block.vector
            def _(e):
                run("dve", e)

            @block.gpsimd
            def _(e):
                run("pool", e)

            @block.sync
            def _(e):
                run("sp", e)


def _t5_bucket_np(dist):
    max_exact = 16
    nf = np.maximum(dist, 1).astype(np.float32)
    large = max_exact + (np.log(nf / max_exact) / math.log(128 / max_exact) * (32 - max_exact)).astype(np.int32)
    large = np.minimum(large, 31)
    return np.where(dist < max_exact, dist, large)


def host_consts():
    c = {}
    c["c_ident"] = np.eye(128, dtype=np.float32)
    c["c_J"] = np.ascontiguousarray(np.eye(128, dtype=np.float32)[::-1])
    c["c_tri"] = np.tril(np.ones((128, 128), dtype=np.float32))
    u = np.arange(383)
    dist = u - 127
    bucket = _t5_bucket_np(np.maximum(dist, 0))
    oh = np.zeros((32, 383), dtype=np.float32)
    for j in range(383):
        if dist[j] >= 0:
            oh[bucket[j], j] = 1.0
    c["c_oh"] = oh
    ic = np.zeros((128, 4, 16), dtype=np.float32)
    for g, w in enumerate((2, 4, 8, 16)):
        ic[:, g, :] = float(w) / np.minimum(np.arange(16) + 1, w).astype(np.float32)
    c["c_invcnt"] = ic.reshape(128, 64)
    return c


def MM(out, lhsT, rhs, start=True, stop=True, **kw):
    return lambda e: e.matmul(out, lhsT=lhsT, rhs=rhs, start=start, stop=stop, **kw)


def TR(out, in_, ident):
    return lambda e: e.transpose(out, in_, ident)


def ACT(out, in_, func, **kw):
    return lambda e: e.activation(out=out, in_=in_, func=func, **kw)


def TT(out, in0, in1, op):
    return lambda e: e.tensor_tensor(out=out, in0=in0, in1=in1, op=op)


def TS(out, in0, s1, s2, op0, op1=None):
    if op1 is None:
        return lambda e: e.tensor_scalar(out=out, in0=in0, scalar1=s1, scalar2=None, op0=op0)
    return lambda e: e.tensor_scalar(out=out, in0=in0, scalar1=s1, scalar2=s2, op0=op0, op1=op1)


def STT(out, in0, scalar, in1, op0, op1):
    return lambda e: e.scalar_tensor_tensor(out=out, in0=in0, scalar=scalar, in1=in1, op0=op0, op1=op1)


def CP(out, in_):
    return lambda e: e.tensor_copy(out=out, in_=in_)


def MS(ap, val):
    return lambda e: e.memset(ap, val)


def RCP(out, in_):
    return lambda e: e.reciprocal(out=out, in_=in_)


def RSUM(out, in_):
    return lambda e: e.reduce_sum(out=out, in_=in_, axis=AX.X)


def DMA(out, in_):
    return lambda e: e.dma_start(out=out, in_=in_)


def build(NT=SEQ // T, serial=False, stop_after=4):
    nc = bass.Bass("TRN2", target_bir_lowering=False)
    SEQL = NT * T
    NKB = NT * TC

    def din(name, shape, dt=F32):
        return nc.dram_tensor(name, shape, dt, kind="ExternalInput").ap()

    x = din("x", [SEQL, D])
    out = nc.dram_tensor("out", [SEQL, D], F32, kind="ExternalOutput").ap()
    w_in0 = din("even_w_in", [D, 1536])
    w_out0 = din("even_w_out", [D, D])
    w_in1 = din("odd_w_in", [D, 3072])
    w_out1 = din("odd_w_out", [D, D])
    w1 = [din("ffn_w1_%d" % l, [D, DFF]) for l in range(2)]
    w2 = [din("ffn_w2_%d" % l, [DFF, D]) for l in range(2)]
    norm_g = din("norm_g", [64, 128])
    ln_g = din("even_ln_g", [1, 512])
    ln_b = din("even_ln_b", [1, 512])
    sp_w = din("even_spatial_w", [4, 128, 128])
    sp_b = din("even_spatial_b", [1, 512])
    pool_w = din("even_pool_w", [4, 128, 128])
    pool_s = din("even_pool_scale", [4, 128])
    lam_in = din("odd_lambda", [1, 256])
    subln = din("odd_subln_g", [1, 128])
    conv_w = din("odd_conv_w", [12, 128])
    rel_tab = din("rel_bias_table", [32, 4])
    c_ident = din("c_ident", [128, 128])
    c_J = din("c_J", [128, 128])
    c_tri = din("c_tri", [128, 128])
    c_oh = din("c_oh", [32, 383])
    c_invcnt = din("c_invcnt", [128, 64])

    def dscr(name, shape, dt=BF16):
        return nc.dram_tensor(name, shape, dt).ap()

    s_in0 = dscr("s_in0", [D, 1536])
    s_out0 = dscr("s_out0", [D, D])
    s_in1 = dscr("s_in1", [D, 3072])
    s_out1 = dscr("s_out1", [D, D])
    s_w1 = [dscr("s_w1_%d" % l, [D, DFF]) for l in range(2)]
    s_w2 = [dscr("s_w2_%d" % l, [DFF, D]) for l in range(2)]
    rline = dscr("rline", [4, 384], F32)

    S = Sched(nc, serial=serial)

    with contextlib.ExitStack() as st:
        def sb(name, shape, dt):
            return st.enter_context(nc.sbuf_tensor(name, shape, dt))

        hT = sb("hT", [P, DC, T], F32)
        hm = sb("hm", [P, 2, DC, T], BF16)
        hn = hm[:, 0]
        yst = sb("yst", [P, DC, T], F32)
        mix = hm[:, 1]
        rstd = sb("rstd", [P, T], F32)
        sqb = sb("sqb", [P, T], F32)
        UB = 35328
        U = sb("U", [P, UB // 2], BF16)
        KT = sb("KT", [P, 4, SEQL], BF16)
        V = sb("V", [P, NKB, 4, 130], BF16)
        BT = sb("BT", [P, 4, 256], F32)
        wring = sb("wring", [P, RING, DC, 512], BF16)
        onesD = sb("onesD", [P, P], BF16)
        ident_f = sb("ident_f", [P, P], F32)
        ident_b = sb("ident_b", [P, P], BF16)
        Jt = sb("Jt", [P, P], F32)
        tri = sb("tri", [P, P], F32)
        stage = sb("stage", [P, P], F32)
        cols = sb("cols", [P, 88], F32)
        lnsc = sb("lnsc", [P, 32], F32)
        bsb = sb("bsb", [P, 512], F32)
        sublng = sb("sublng", [P, P], F32)
        lamraw = sb("lamraw", [P, 256], F32)
        lamt = sb("lamt", [P, 8], F32)
        tab = sb("tab", [32, 4], F32)
        oh = sb("oh", [32, 384], F32)
        line = sb("line", [4, 384], F32)
        Wh = sb("Wh", [P, 256], F32)
        spw = sb("spw", [P, 4, P], F32)
        WsT = sb("WsT", [P, 4, P], BF16)
        poolw = sb("poolw", [P, 4, P], BF16)
        invcnt = sb("invcnt", [P, 4, 16], F32)
        phalo = sb("phalo", [P, 4, 16], F32)
        zhalo = sb("zhalo", [P, 4, 2], F32)
        small = sb("small", [P, 16], F32)

        PS = [st.enter_context(nc.psum_tensor("ps%d" % b, [P, 512], F32)) for b in range(8)]
        PSB = PS[7][:].bitcast(BF16)

        def uview(off, nbytes, dt):
            assert off % 4 == 0 and off + nbytes <= UB
            v = U[:, off // 2:(off + nbytes) // 2]
            if dt == F32:
                v = v.bitcast(F32)
            return v

        r4 = lambda v: v.rearrange("p (a b) -> p a b", a=4)
        uT = r4(uview(0, 8192, F32))
        vtm = [uview(8192 + 2048 * i, 2048, F32) for i in range(2)]
        vn = r4(uview(12288, 4096, BF16))
        pT = r4(uview(16384, 8448, F32))
        wsA = uview(24832, 2112, F32)
        wsB = uview(26944, 2112, F32)
        pooled = r4(uview(29056, 4096, BF16))
        hid = uview(0, 32768, BF16).rearrange("p (a b) -> p a b", a=32)
        QT = r4(uview(0, 4096, BF16))
        cgT = r4(uview(4096, 8192, F32))
        zT = r4(uview(12288, 8224, F32))
        Et = [[uview(20512 + 1024 * (2 * m + b), 1024, BF16) for b in range(2)] for m in range(2)]
        accsb = uview(24608, 4680, F32).rearrange("p (a b) -> p a b", a=9)
        oA = r4(uview(29288, 2048, F32))
        oB = r4(uview(31336, 2048, F32))
        oNb = r4(uview(33384, 1024, BF16))
        rcs = sb("rcs", [P, 16], F32)
        sm = lambda i: small[:, i:i + 1]
        ALLY = ["yst%d" % c for c in range(DC)]

        def cload(dst, src, tok):
            S.dma("sp", "c", DMA(dst, src), writes=[tok])

        cload(ident_f[:], c_ident[:, :], "ident_f")
        cload(Jt[:], c_J[:, :], "Jt")
        cload(tri[:], c_tri[:, :], "tri")
        cload(oh[:, 0:383], c_oh[:, :], "oh")
        cload(invcnt[:].rearrange("p a b -> p (a b)"), c_invcnt[:, :], "invcnt")
        cload(stage[0:64, :], norm_g[:, :], "stage")
        cload(stage[64:68, :], pool_s[:, :], "stage")
        cload(stage[68:80, :], conv_w[:, :], "stage")
        cload(stage[80:84, :], ln_g.rearrange("o (g c) -> (o g) c", g=4), "stage")
        cload(stage[84:88, :], ln_b.rearrange("o (g c) -> (o g) c", g=4), "stage")
        cload(bsb[:], sp_b.partition_broadcast(P), "bsb")
        cload(sublng[:], subln.partition_broadcast(P), "sublng")
        cload(lamraw[:], lam_in.partition_broadcast(P), "lamraw")
        cload(tab[:], rel_tab[:, :], "tab")
        cload(spw[:], sp_w.rearrange("g t s -> t g s"), "spw")
        S.fence(["ident_f", "Jt", "tri", "oh", "invcnt", "stage", "bsb", "sublng", "lamraw", "tab", "spw"], "c")
        S.dma("pool", "c2", DMA(poolw[:], pool_w.rearrange("g c d -> c g d")), writes=["poolw"])

        tile_blocks = []
        for c0 in (1024, 512, 0):
            tile_blocks.append((w_in0[:, c0:c0 + 512], s_in0[:, c0:c0 + 512]))
        for c0 in (0, 512):
            tile_blocks.append((w_out0[:, c0:c0 + 512], s_out0[:, c0:c0 + 512]))
        for b in range(8):
            tile_blocks.append((w1[0][:, b * 512:(b + 1) * 512], s_w1[0][:, b * 512:(b + 1) * 512]))
        for half in range(2):
            for kg in range(4):
                tile_blocks.append((w2[0][kg * 1024:(kg + 1) * 1024, half * 512:(half + 1) * 512],
                                    s_w2[0][kg * 1024:(kg + 1) * 1024, half * 512:(half + 1) * 512]))
        for c0 in (2048, 2560, 0, 512, 1536, 1024):
            tile_blocks.append((w_in1[:, c0:c0 + 512], s_in1[:, c0:c0 + 512]))
        for c0 in (0, 512):
            tile_blocks.append((w_out1[:, c0:c0 + 512], s_out1[:, c0:c0 + 512]))
        for b in range(8):
            tile_blocks.append((w1[1][:, b * 512:(b + 1) * 512], s_w1[1][:, b * 512:(b + 1) * 512]))
        for half in range(2):
            for kg in range(4):
                tile_blocks.append((w2[1][kg * 1024:(kg + 1) * 1024, half * 512:(half + 1) * 512],
                                    s_w2[1][kg * 1024:(kg + 1) * 1024, half * 512:(half + 1) * 512]))
        NBLK = len(tile_blocks)
        S.op("pool", MS(onesD[:], 1.0 / D), writes=["onesD"])
        S.op("pool", MS(V[:, :, :, 128:130], 1.0), writes=["Vones"])
        S.op("pool", MS(phalo[:], 0.0), writes=["phalo"])
        S.op("pool", MS(zhalo[:], 0.0), writes=["zhalo"])
        S.op("pool", MS(line[:], NEG), writes=["line"])
        S.op("pool", MS(small[:, 4:5], EPS), writes=["epsc"])
        cstate = {"n": 0}

        def cast_issue_upto(n):
            while cstate["n"] < min(n, NBLK):
                bi = cstate["n"]
                src, dst = tile_blocks[bi]
                S.dma("pool", "k%d" % bi, DMA(dst, src), writes=["S%d" % bi])
                cstate["n"] += 1


        S.op("dve", CP(ident_b[:], ident_f[:]), reads=["ident_f"], writes=["ident_b"])
        S.op("pe", TR(PS[7][:, 0:88], stage[0:88, :], ident_f[0:88, 0:88]), reads=["stage", "ident_f"], writes=["ps7"])
        S.op("dve", CP(cols[:], PS[7][:, 0:88]), reads=["ps7"], writes=["cols"])
        gcol = lambda l, n, c: cols[:, (l * 4 + n) * 8 + c:(l * 4 + n) * 8 + c + 1]
        pscol = lambda j: cols[:, 64 + j:65 + j]
        cwcol = lambda k, j: cols[:, 68 + k * 4 + j:69 + k * 4 + j]
        lngcol = lambda g: cols[:, 80 + g:81 + g]
        for g in range(4):
            S.op("dve", TT(spw[:, g, :], spw[:, g, :], tri[:], ALU.mult), reads=["spw", "tri"], writes=["spw"])
        for g in range(4):
            S.op("pe", TR(PS[6][:, g * P:(g + 1) * P], spw[:, g, :], ident_f[:]), reads=["spw", "ident_f"], writes=["ps6"])
        S.op("dve", CP(WsT[:].rearrange("p a b -> p (a b)"), PS[6][:]), reads=["ps6"], writes=["WsT"])
        S.op("dve", TS(lnsc[:, 24:28], cols[:, 84:88], float(D), None, ALU.mult), reads=["cols"], writes=["lnb1k"])
        for g in range(4):
            S.op("pe", MM(PS[6][:, g * P:(g + 1) * P], onesD[:], WsT[:, g, :]), reads=["onesD", "WsT"], writes=["ps6"], inc=(g == 3))
        for g in range(4):
            S.op("dve", STT(bsb[:, g * P:(g + 1) * P], PS[6][:, g * P:(g + 1) * P], lnsc[:, 24 + g:25 + g], bsb[:, g * P:(g + 1) * P], ALU.mult, ALU.add),
                 reads=["ps6", "lnb1k", "bsb"], writes=["bsb"])
        S.op("pe", MM(PS[5][0:4, 0:383], tab[:, :], oh[:, 0:383]), reads=["tab", "oh"], writes=["ps5"])
        S.op("dve", CP(small[0:4, 0:1], PS[5][0:4, 382:383]), reads=["ps5"], writes=["small0"])
        S.op("dve", TS(line[:, 127:383], PS[5][0:4, 127:383], small[0:4, 0:1], 8.0, ALU.subtract, ALU.mult),
             reads=["ps5", "small0", "line"], writes=["line"])
        S.dma("sp", "c3", DMA(rline[:, :], line[:]), reads=["line"], writes=["rline"])
        for h in range(4):
            src = bass.AP(rline.tensor, h * 384, [[1, P], [1, 256]])
            S.dma("sp", "c4", DMA(BT[:, h, :], src), reads=["rline"], writes=["BTraw"])
        S.fence(["BTraw"], "c4")

        def finish_bias_tiles():
            for h in range(4):
                S.op("pe", MM(PS[7][:, 0:256], Jt[:], BT[:, h, :]), reads=["Jt", "BTraw", "BT"], writes=["ps7"])
                S.op("dve", CP(BT[:, h, :], PS[7][:, 0:256]), reads=["ps7"], writes=["BT", "BTraw"])

        S.op("dve", TT(lamraw[:, 0:64], lamraw[:, 0:64], lamraw[:, 64:128], ALU.mult), reads=["lamraw"], writes=["lamraw"])
        S.op("dve", TT(lamraw[:, 128:192], lamraw[:, 128:192], lamraw[:, 192:256], ALU.mult), reads=["lamraw"], writes=["lamraw"])
        S.op("dve", RSUM(lamt[:, 0:1], lamraw[:, 0:64]), reads=["lamraw"], writes=["lamt"])
        S.op("dve", RSUM(lamt[:, 1:2], lamraw[:, 128:192]), reads=["lamraw", "lamt"], writes=["lamt"])
        S.op("act", ACT(lamt[:, 2:4], lamt[:, 0:2], AF.Exp), reads=["lamt"], writes=["lamt"])
        S.op("dve", TT(lamt[:, 4:5], lamt[:, 3:4], lamt[:, 2:3], ALU.subtract), reads=["lamt"], writes=["lamt"])
        S.op("dve", TS(small[:, 2:3], lamt[:, 4:5], -LAMBDA_INIT, None, ALU.add), reads=["lamt"], writes=["nlam"])
        S.op("dve", TS(sublng[:], sublng[:], 1.0 - LAMBDA_INIT, None, ALU.mult), reads=["sublng"], writes=["sublng"])
        nlam = small[:, 2:3]

        blocks = []
        for i in range(NT):
            for bi, (src, dst) in enumerate(tile_blocks):
                blocks.append((dst, "S%d" % bi))
        wstate = {"issued": 0, "used": 0}

        def w_issue_upto(n):
            while wstate["issued"] < min(n, len(blocks)):
                k = wstate["issued"]
                slot = k % RING
                src, tok = blocks[k]
                if k < NBLK:
                    f32src, scr = tile_blocks[k]
                    S.dma("pool", "wc%d" % slot, DMA(wring[:, slot], f32src.rearrange("(c p) n -> p c n", p=P)), writes=["w%d" % slot])
                    S.dma("sp", "k%d" % k, DMA(scr.rearrange("(c p) n -> p c n", p=P), wring[:, slot]), reads=["w%d" % slot], writes=["S%d" % k])
                else:
                    sv = src.rearrange("(c p) n -> p c n", p=P)
                    S.dma("sp", "w%d" % slot, DMA(wring[:, slot], sv), reads=[tok], writes=["w%d" % slot])
                wstate["issued"] += 1

        def w_next():
            k = wstate["used"]
            w_issue_upto(k + RING)
            wstate["used"] += 1
            slot = k % RING
            return wring[:, slot], "w%d" % slot

        rot = {"i": 0}

        def nextbank(n=6):
            b = rot["i"] % n
            rot["i"] += 1
            return b

        def stats_and_rstd():
            for c in range(DC):
                S.op("pe", MM(PS[6][:], onesD[:], hn[:, c, :], start=(c == 0), stop=(c == DC - 1)),
                     reads=["onesD", "hn%d" % c], writes=["ps6"], inc=(c == DC - 1))
            S.op("act", ACT(sqb[:], PS[6][:], AF.Ln, bias=EPS, scale=1.0), reads=["ps6"], writes=["sqb"])
            S.op("act", ACT(rstd[:], sqb[:], AF.Exp, scale=-0.5), reads=["sqb"], writes=["rstd"])

        def prenorm(l, n, have_squares=False):
            for c in range(DC):
                if not have_squares:
                    S.op("act", ACT(hn[:, c, :], hT[:, c, :], AF.Square), reads=["hT%d" % c], writes=["hn%d" % c])
            stats_and_rstd()
            for c in range(DC):
                S.op("dve", STT(hn[:, c, :], hT[:, c, :], gcol(l, n, c), rstd[:], ALU.mult, ALU.mult),
                     reads=["hT%d" % c, "cols", "rstd"], writes=["hn%d" % c])

        def evac_post(b, c, l, n):
            S.op("act", ACT(hn[:, c, :], PS[b][:], AF.Square), reads=["ps%d" % b], writes=["hn%d" % c])
            S.op("act", ACT(yst[:, c, :], PS[b][:], AF.Copy, scale=gcol(l, n, c)), reads=["ps%d" % b, "cols"], writes=["yst%d" % c])

        def postnorm(l, n):
            stats_and_rstd()
            for c in range(DC):
                S.op("dve", TT(yst[:, c, :], yst[:, c, :], rstd[:], ALU.mult), reads=["yst%d" % c, "rstd"], writes=["yst%d" % c])
                S.op("dve", TT(hT[:, c, :], hT[:, c, :], yst[:, c, :], ALU.add),
                     reads=["yst%d" % c, "hT%d" % c], writes=["hT%d" % c])

        def mm_fm(wblk, wtok, src, src_tok, j, b, nk=DC, start=True, stop=True, koff=0):
            for k in range(nk):
                S.op("pe", MM(PS[b][:], wblk[:, k, j * P:(j + 1) * P], src[:, koff + k, :],
                              start=(start and k == 0), stop=(stop and k == nk - 1)),
                     reads=[wtok, "%s%d" % (src_tok, koff + k)], writes=["ps%d" % b], inc=(k == nk - 1))

        def mm_fm_kouter(wblk, wtok, src, src_tok, banks, korder=None, before_last=None):
            korder = list(range(DC)) if korder is None else korder
            for i, k in enumerate(korder):
                if i == DC - 1 and before_last is not None:
                    before_last()
                for j, b in enumerate(banks):
                    S.op("pe", MM(PS[b][:], wblk[:, k, j * P:(j + 1) * P], src[:, k, :], start=(i == 0), stop=(i == DC - 1)),
                         reads=[wtok, "%s%d" % (src_tok, k)], writes=["ps%d" % b], inc=(i == DC - 1))

        def mm_tm(wblk, wtok, tc, b):
            for k in range(DC):
                S.op("pe", MM(PS[b][:], hn[:, k, tc * P:(tc + 1) * P], wblk[:, k, :], start=(k == 0), stop=(k == DC - 1)),
                     reads=[wtok, "hn%d" % k], writes=["ps%d" % b], inc=(k == DC - 1))

        def wout_and_post(l, korder, before_last=None):
            for half in range(2):
                wblk, wtok = w_next()
                if half == 0:
                    banks = [nextbank() for j in range(4)]
                    mm_fm_kouter(wblk, wtok, mix, "mix", banks, korder, before_last)
                    for j in range(4):
                        evac_post(banks[j], j, l, 1)
                else:
                    for j in range(4):
                        b = nextbank()
                        mm_fm(wblk, wtok, mix, "mix", j, b)
                        evac_post(b, 4 + j, l, 1)
            postnorm(l, 1)

        def ffn(l, after_w2=None):
            for c in range(DC):
                S.op("act", ACT(hn[:, c, :], hT[:, c, :], AF.Copy, scale=gcol(l, 2, c)), reads=["hT%d" % c, "cols"], writes=["hn%d" % c])
                S.op("act", ACT(mix[:, c, :], hT[:, c, :], AF.Square), reads=["hT%d" % c], writes=["mix%d" % c])
            S.alias(["hid%d" % j for j in range(32)])
            for blk in range(8):
                wblk, wtok = w_next()
                banks = [nextbank() for j in range(4)]
                if blk == 0:
                    mm_fm_kouter(wblk, wtok, hn, "hn", banks)
                for j in range(4):
                    hc = blk * 4 + j
                    b = banks[j]
                    if blk != 0:
                        mm_fm(wblk, wtok, hn, "hn", j, b)
                    r = hc % DC
                    S.op("act", ACT(yst[:, r, :], PS[b][:], AF.Relu), reads=["ps%d" % b], writes=["yst%d" % r])
                    eng = "pool" if hc % 4 == 3 else "dve"
                    S.op(eng, TT(hid[:, hc, :], yst[:, r, :], yst[:, r, :], ALU.mult), reads=["yst%d" % r], writes=["hid%d" % hc])
                if blk == 1:
                    for c in range(DC):
                        S.op("pe", MM(PS[6][:], onesD[:], mix[:, c, :], start=(c == 0), stop=(c == DC - 1)),
                             reads=["onesD", "mix%d" % c], writes=["ps6"], inc=(c == DC - 1))
                    S.op("act", ACT(rstd[:], PS[6][:], AF.Square, bias=1e-3 * EPS, scale=1e-3), reads=["ps6"], writes=["rstd"])
            for half in range(2):
                banks = [half * 4 + j for j in range(4)]
                for kg in range(4):
                    wblk, wtok = w_next()
                    for j in range(4):
                        mm_fm(wblk, wtok, hid, "hid", j, banks[j], nk=8, start=(kg == 0), stop=(kg == 3), koff=kg * 8)
                if half == 1 and after_w2 is not None:
                    after_w2()
                for j in range(4):
                    evac_post(banks[j], half * 4 + j, l, 3)
            for c in range(DC):
                S.op("pe", MM(PS[6][:], onesD[:], hn[:, c, :], start=(c == 0), stop=(c == DC - 1)),
                     reads=["onesD", "hn%d" % c], writes=["ps6"], inc=(c == DC - 1))
            S.op("dve", TT(sqb[:], PS[6][:], rstd[:], ALU.add), reads=["ps6", "rstd"], writes=["sqb"])
            S.op("act", ACT(sqb[:], sqb[:], AF.Ln), reads=["sqb"], writes=["sqb"])
            S.op("act", ACT(rstd[:], sqb[:], AF.Exp, scale=-0.5), reads=["sqb"], writes=["rstd"])
            for c in range(DC):
                S.op("dve", TT(yst[:, c, :], yst[:, c, :], rstd[:], ALU.mult), reads=["yst%d" % c, "rstd"], writes=["yst%d" % c])
                S.op("dve", TT(hT[:, c, :], hT[:, c, :], yst[:, c, :], ALU.add),
                     reads=["yst%d" % c, "hT%d" % c], writes=["hT%d" % c])

        ALLHM = ["hn%d" % c for c in range(DC)] + ["mix%d" % c for c in range(DC)]
        xv = U[:, 0:8192].bitcast(F32).rearrange("p (a d) -> p a d", a=4)
        ov = hm[:].rearrange("p a c t -> p (a c t)").bitcast(F32).rearrange("p (a d) -> p a d", a=4)

        def xload(it):
            t0 = it * T
            S.alias(["xbuf"])
            S.dma("sp", "x", DMA(xv, x[t0:t0 + T, :].rearrange("(a p) d -> p a d", p=P)), writes=["xbuf"])

        S.tag = 'xload'
        xload(0)
        for it in range(NT):
            t0 = it * T
            S.tag = 'xload'
            for c in range(DC):
                b = nextbank()
                for a in range(4):
                    S.op("pe", TR(PS[b][:, a * P:(a + 1) * P], xv[:, a, c * P:(c + 1) * P], ident_f[:]),
                         reads=["xbuf", "ident_f"], writes=["ps%d" % b], inc=(a == 3))
                if c % 2 == 0:
                    S.op("dve", CP(hT[:, c, :], PS[b][:]), reads=["ps%d" % b], writes=["hT%d" % c])
                else:
                    S.op("act", ACT(hT[:, c, :], PS[b][:], AF.Copy), reads=["ps%d" % b], writes=["hT%d" % c])

            S.tag = 'L0.pre'
            prenorm(0, 0)
            S.tag = 'L0.mix'
            S.alias(["uT", "vtm0", "vtm1", "vn0", "vn1", "vn2", "vn3", "pT", "wsA", "wsB", "pooled"])
            wblk, wtok = w_next()
            S.op("pool", CP(pT[:, :, 0:16], phalo[:]), reads=["phalo"], writes=["pT"])
            banks = [nextbank() for j in range(4)]
            mm_fm_kouter(wblk, wtok, hn, "hn", banks)
            for j in range(4):
                S.op("act", ACT(pT[:, j, 16:16 + T], PS[banks[j]][:], AF.Copy), reads=["ps%d" % banks[j]], writes=["pT"])
            S.op("pool", CP(phalo[:], pT[:, :, T:T + 16]), reads=["pT"], writes=["phalo"])
            for j in range(4):
                W_ = 2 ** (j + 1)
                cur, curtok = pT[:, j, :], "pT"
                bufs = [(wsA, "wsA"), (wsB, "wsB")]
                sh, lvl = 1, 0
                while sh < W_:
                    dst, dtok = bufs[lvl % 2]
                    lo = 2 * sh - 1
                    S.op("dve", TT(dst[:, lo:16 + T], cur[:, lo:16 + T], cur[:, lo - sh:16 + T - sh], ALU.add), reads=[curtok], writes=[dtok])
                    cur, curtok = dst, dtok
                    sh *= 2
                    lvl += 1
                if it == 0:
                    S.op("dve", TT(cur[:, 16:32], cur[:, 16:32], invcnt[:, j, :], ALU.mult), reads=[curtok, "invcnt"], writes=[curtok])
                if j < 3:
                    S.op("dve", STT(pooled[:, j, :], cur[:, 16:16 + T], 1.0 / W_, pT[:, j, 16:16 + T], ALU.mult, ALU.subtract), reads=[curtok, "pT"], writes=["pooled"])
                else:
                    pool_last = (cur, curtok, W_)
            wblk, wtok = w_next()
            vbanks = [nextbank() for tc in range(TC)]
            for k in range(DC):
                for tc in range(TC):
                    S.op("pe", MM(PS[vbanks[tc]][:], hn[:, k, tc * P:(tc + 1) * P], wblk[:, k, :], start=(k == 0), stop=(k == DC - 1)),
                         reads=[wtok, "hn%d" % k], writes=["ps%d" % vbanks[tc]], inc=(k == DC - 1))
            for tc in range(TC):
                b = vbanks[tc]
                S.op("act", ACT(yst[:, tc, :], PS[b][:], AF.Gelu_apprx_tanh, accum_out=lnsc[:, tc:tc + 1]), reads=["ps%d" % b], writes=["yst%d" % tc, "lns1_%d" % tc])
                S.op("act", ACT(vn[:, tc, :], yst[:, tc, :], AF.Square, accum_out=lnsc[:, 4 + tc:5 + tc]), reads=["yst%d" % tc], writes=["vn%d" % tc, "lns2_%d" % tc])
            LNS = ["lns1_%d" % t for t in range(4)] + ["lns2_%d" % t for t in range(4)]
            S.op("dve", TS(lnsc[:, 8:12], lnsc[:, 0:4], 1.0 / 512, None, ALU.mult), reads=LNS, writes=["lnmu"])
            S.op("dve", TT(lnsc[:, 12:16], lnsc[:, 8:12], lnsc[:, 8:12], ALU.mult), reads=["lnmu"], writes=["lnvar"])
            S.op("dve", STT(lnsc[:, 12:16], lnsc[:, 4:8], 1.0 / 512, lnsc[:, 12:16], ALU.mult, ALU.subtract), reads=LNS + ["lnvar"], writes=["lnvar"])
            S.op("act", ACT(lnsc[:, 20:24], lnsc[:, 12:16], AF.Ln, bias=EPS, scale=1.0), reads=["lnvar"], writes=["lnln"])
            S.op("act", ACT(lnsc[:, 12:16], lnsc[:, 20:24], AF.Exp, scale=-0.5), reads=["lnln"], writes=["lnvar"])
            S.op("dve", STT(lnsc[:, 16:20], lnsc[:, 8:12], -1.0, lnsc[:, 12:16], ALU.mult, ALU.mult), reads=["lnmu", "lnvar"], writes=["lnnmr"])
            for tc in range(TC):
                S.op("dve", TS(vn[:, tc, :], yst[:, tc, :], lnsc[:, 12 + tc:13 + tc], lnsc[:, 16 + tc:17 + tc], ALU.mult, ALU.add),
                     reads=["yst%d" % tc, "lnvar", "lnnmr"], writes=["vn%d" % tc])
            wblk, wtok = w_next()
            for j in range(4):
                b = nextbank()
                mm_fm(wblk, wtok, hn, "hn", j, b)
                S.op("act", ACT(uT[:, j, :], PS[b][:], AF.Gelu_apprx_tanh), reads=["ps%d" % b], writes=["uT"])
            for tc in range(TC):
                gb = 6 + (tc % 2)
                for g in range(4):
                    S.op("pe", MM(PS[gb][:, g * P:(g + 1) * P], vn[:, tc, g * P:(g + 1) * P], WsT[:, g, :]),
                         reads=["vn%d" % tc, "WsT"], writes=["ps%d" % gb], inc=(g == 3))
                vt = vtm[tc % 2]
                for g in range(4):
                    S.op("dve", STT(vt[:, g * P:(g + 1) * P], PS[gb][:, g * P:(g + 1) * P], lngcol(g), bsb[:, g * P:(g + 1) * P], ALU.mult, ALU.add),
                         reads=["ps%d" % gb, "bsb", "cols"], writes=["vtm%d" % (tc % 2)])
                S.op("dve", TT(mix[:, 0:4, tc * P:(tc + 1) * P], uT[:, :, tc * P:(tc + 1) * P], r4(vt), ALU.mult),
                     reads=["vtm%d" % (tc % 2), "uT"], writes=["mix%d" % c for c in range(4)])
            cur, curtok, W_ = pool_last
            S.op("dve", STT(pooled[:, 3, :], cur[:, 16:16 + T], 1.0 / W_, pT[:, 3, 16:16 + T], ALU.mult, ALU.subtract), reads=[curtok, "pT"], writes=["pooled"])
            for j in range(4):
                b = nextbank()
                S.op("pe", MM(PS[b][:], poolw[:, j, :], pooled[:, j, :]), reads=["poolw", "pooled"], writes=["ps%d" % b])
                S.op("act", ACT(mix[:, 4 + j, :], PS[b][:], AF.Copy, scale=pscol(j)), reads=["ps%d" % b, "cols"], writes=["mix%d" % (4 + j)])
            S.tag = 'L0.wout'
            wout_and_post(0, [0, 1, 2, 3, 4, 5, 6, 7])
            S.tag = 'F0'
            if stop_after >= 2:
                ffn(0)
            else:
                [w_next() for _ in range(16)]

            if stop_after >= 3:
                S.tag = 'L1.pre'
                prenorm(1, 0)
                S.tag = 'L1.proj'
                S.alias(["QT%d" % h for h in range(4)] + ["cgT", "zT", "E00", "E01", "E10", "E11", "accsb", "oA", "oB", "oNb"])
                wblk, wtok = w_next()
                banks = [nextbank(4) for j in range(4)]
                mm_fm_kouter(wblk, wtok, hn, "hn", banks)
                for j in range(4):
                    S.op("act", ACT(cgT[:, j, :], PS[banks[j]][:], AF.Copy), reads=["ps%d" % banks[j]], writes=["cgT"])
                wblk, wtok = w_next()
                S.op("pool", CP(zT[:, :, 0:2], zhalo[:]), reads=["zhalo"], writes=["zT"])
                for j in range(4):
                    b = nextbank(4)
                    mm_fm(wblk, wtok, hn, "hn", j, b)
                    S.op("dve", TT(zT[:, j, 2:2 + T], PS[b][:], cgT[:, j, :], ALU.mult), reads=["ps%d" % b, "cgT"], writes=["zT"])
                S.op("pool", CP(zhalo[:], zT[:, :, T:T + 2]), reads=["zT"], writes=["zhalo"])
                for j in range(4):
                    S.op("dve", TS(cgT[:, j, :], zT[:, j, 2:2 + T], cwcol(2, j), None, ALU.mult), reads=["zT", "cols"], writes=["cgT"])
                    S.op("dve", STT(cgT[:, j, :], zT[:, j, 1:1 + T], cwcol(1, j), cgT[:, j, :], ALU.mult, ALU.add), reads=["zT", "cols", "cgT"], writes=["cgT"])
                    S.op("dve", STT(cgT[:, j, :], zT[:, j, 0:T], cwcol(0, j), cgT[:, j, :], ALU.mult, ALU.add), reads=["zT", "cols", "cgT"], writes=["cgT"])
                wblk, wtok = w_next()
                for h in range(4):
                    b = nextbank(4)
                    mm_fm(wblk, wtok, hn, "hn", h, b)
                    S.op("act", ACT(QT[:, h, :], PS[b][:], AF.Copy), reads=["ps%d" % b], writes=["QT%d" % h])
                wblk, wtok = w_next()
                for h in range(4):
                    b = nextbank(4)
                    mm_fm(wblk, wtok, hn, "hn", h, b)
                    S.op("act", ACT(KT[:, h, t0:t0 + T], PS[b][:], AF.Copy), reads=["ps%d" % b], writes=["KT%d_%d" % (it, h)])
                wblk_bg, wtok_bg = w_next()
                for j in range(4):
                    b = nextbank(4)
                    mm_fm(wblk_bg, wtok_bg, hn, "hn", j, b)
                    S.op("dve", TT(mix[:, 4 + j, :], PS[b][:], cgT[:, j, :], ALU.mult), reads=["ps%d" % b, "cgT"], writes=["mix%d" % (4 + j)])
                wblk, wtok = w_next()
                for tc in range(TC):
                    b = nextbank(4)
                    mm_tm(wblk, wtok, tc, b)
                    kb = it * TC + tc
                    S.op("act", ACT(V[:, kb, :, 0:128], r4(PS[b][:]), AF.Copy), reads=["ps%d" % b, "Vones"], writes=["V%d" % kb])

                S.tag = 'L1.attn'
                if it == 0:
                    finish_bias_tiles()
                nkb = (it + 1) * TC

                def accv(m, qi):
                    a_ = m * 4 + qi
                    return PS[4 + a_ // 3][:, (a_ % 3) * 130:(a_ % 3) * 130 + 129]

                acctok = lambda m, qi: "ps%d" % (4 + (m * 4 + qi) // 3)
                est = {"n": 0}

                def scores(h, kb):
                    r = kb - it * TC
                    q0 = max(r, 0)
                    nq = TC - q0
                    ncol = nq * P
                    eb = est["n"] % 2
                    est["n"] += 1
                    for m in range(2):
                        sbk = 2 * m + eb
                        S.op("pe", MM(PS[sbk][:, 0:ncol], KT[m * 64:(m + 1) * 64, h, kb * P:(kb + 1) * P],
                                      QT[m * 64:(m + 1) * 64, h, q0 * P:q0 * P + ncol]),
                             reads=["KT%d_%d" % (kb // TC, h), "QT%d" % h], writes=["ps%d" % sbk])
                        for a in range(nq):
                            rel = it * TC + q0 + a - kb
                            if rel in (0, 1):
                                S.op("dve", TT(PS[sbk][:, a * P:(a + 1) * P], PS[sbk][:, a * P:(a + 1) * P], BT[:, h, rel * P:(rel + 1) * P], ALU.add),
                                     reads=["BT", "ps%d" % sbk], writes=["ps%d" % sbk])
                        S.op("act", ACT(Et[m][eb][:, 0:ncol], PS[sbk][:, 0:ncol], AF.Exp, scale=0.125), reads=["ps%d" % sbk], writes=["E%d%d" % (m, eb)])
                    return eb, q0, nq

                def av(h, kb, eb, q0, nq):
                    for m in range(2):
                        for a in range(nq):
                            qi = q0 + a
                            last = (m == 1 and a == nq - 1)
                            S.op("pe", MM(accv(m, qi), Et[m][eb][:, a * P:(a + 1) * P], V[:, kb, h, 0:129],
                                          start=(kb == 0 and (m * 4 + qi) % 3 == 0), stop=False, skip_group_check=True),
                                 reads=["E%d%d" % (m, eb), "V%d" % kb, "Vones"], writes=[acctok(m, qi)], inc=last)

                def fin_math(h):
                    rcv = rcs[:, 0:8].rearrange("p (a b) -> p a b", b=1)
                    ssv = rcs[:, 8:12].rearrange("p (a b) -> p a b", b=1)
                    S.op("dve", RCP(rcv, accsb[:, 0:8, 128:129]), reads=["accsb"], writes=["rcs"])
                    S.op("dve", TS(rcs[:, 4:8], rcs[:, 4:8], nlam, None, ALU.mult), reads=["rcs", "nlam"], writes=["rcs"])
                    S.op("dve", TT(oA[:], accsb[:, 0:4, 0:128], rcv[:, 0:4, :].to_broadcast([P, 4, P]), ALU.mult), reads=["accsb", "rcs"], writes=["oA"])
                    S.op("dve", TT(oB[:], accsb[:, 4:8, 0:128], rcv[:, 4:8, :].to_broadcast([P, 4, P]), ALU.mult), reads=["accsb", "rcs"], writes=["oB"])
                    S.op("dve", TT(oA[:], oA[:], oB[:], ALU.add), reads=["oA", "oB"], writes=["oA"])
                    S.op("dve", TT(oB[:], oA[:], oA[:], ALU.mult), reads=["oA"], writes=["oB"])
                    S.op("dve", RSUM(rcs[:, 8:12], oB[:]), reads=["oB"], writes=["rcs2"])

                def fin_math_b(h):
                    ssv = rcs[:, 8:12].rearrange("p (a b) -> p a b", b=1)
                    S.op("act", ACT(rcs[:, 12:16], rcs[:, 8:12], AF.Ln, bias=EPS, scale=1.0 / 128), reads=["rcs2"], writes=["rcs3"])
                    S.op("act", ACT(rcs[:, 8:12], rcs[:, 12:16], AF.Exp, scale=-0.5), reads=["rcs3"], writes=["rcs2"])
                    S.op("dve", TT(oA[:], oA[:], ssv.to_broadcast([P, 4, P]), ALU.mult), reads=["oA", "rcs2"], writes=["oA"])
                    S.op("dve", TT(oNb[:], oA[:], sublng[:].unsqueeze(1).to_broadcast([P, 4, P]), ALU.mult), reads=["oA", "sublng"], writes=["oNb"])

                def fin_pe(h):
                    for qi in range(TC):
                        S.op("pe", TR(PSB[:, qi * P:(qi + 1) * P], oNb[:, qi, :], ident_b[:]), reads=["oNb", "ident_b"], writes=["ps7"], inc=(qi == TC - 1))
                    S.op("act", ACT(mix[:, h, :], PSB[:, 0:512], AF.Copy), reads=["ps7"], writes=["mix%d" % h])

                for h in range(4):
                    if h > 0:
                        fin_math(h - 1)
                    cur = scores(h, 0)
                    for kb in range(nkb):
                        nxt = scores(h, kb + 1) if kb + 1 < nkb else None
                        av(h, kb, *cur)
                        cur = nxt
                        if h > 0 and kb == min(1, nkb - 1):
                            fin_math_b(h - 1)
                        if h > 0 and kb == min(3, nkb - 1):
                            fin_pe(h - 1)
                    S.op("dve", CP(accsb[:, 0:3, :].rearrange("p a b -> p (a b)"), PS[4][:, 0:390]), reads=["ps4"], writes=["accsb"])
                    S.op("act", ACT(accsb[:, 3:6, :].rearrange("p a b -> p (a b)"), PS[5][:, 0:390], AF.Copy), reads=["ps5"], writes=["accsb"])
                    S.op("dve", CP(accsb[:, 6:9, :].rearrange("p a b -> p (a b)"), PS[6][:, 0:390]), reads=["ps6"], writes=["accsb"])
                fin_math(3)
                fin_math_b(3)
                S.tag = 'L1.wout'
                wout_and_post(1, [4, 5, 6, 7, 0, 1, 2, 3], before_last=lambda: fin_pe(3))
            else:
                [w_next() for _ in range(8)]
            S.tag = 'F1'
            if stop_after >= 4:
                ffn(1, after_w2=(lambda: xload(it + 1)) if it + 1 < NT else None)
            else:
                [w_next() for _ in range(16)]

            S.tag = 'store'
            for c in range(DC):
                b = nextbank()
                for a in range(4):
                    S.op("pe", TR(PS[b][:, a * P:(a + 1) * P], hT[:, c, a * P:(a + 1) * P], ident_f[:]),
                         reads=["hT%d" % c, "ident_f"], writes=["ps%d" % b], inc=(a == 3))
                if c % 2 == 0:
                    S.op("dve", CP(ov[:, :, c * P:(c + 1) * P], r4(PS[b][:])), reads=["ps%d" % b], writes=ALLHM)
                else:
                    S.op("act", ACT(ov[:, :, c * P:(c + 1) * P], r4(PS[b][:]), AF.Copy), reads=["ps%d" % b], writes=ALLHM)
            S.dma("sp", "o", DMA(out[t0:t0 + T, :].rearrange("(a p) d -> p a d", p=P), ov), reads=ALLHM)

        S.wait_all("sp")
        S.emit()
    return nc, S


_CACHE = {}


def kernel(**inputs):
    n = 8
    x = np.ascontiguousarray(inputs["x"], dtype=np.float32)
    common = dict(host_consts())
    f = lambda k: np.ascontiguousarray(inputs[k], dtype=np.float32)
    common["even_w_in"] = f("even_w_in")[0]
    common["even_w_out"] = f("even_w_out")[0]
    common["odd_w_in"] = f("odd_w_in")[0]
    common["odd_w_out"] = f("odd_w_out")[0]
    for l in range(2):
        common["ffn_w1_%d" % l] = f("ffn_w1")[l]
        common["ffn_w2_%d" % l] = f("ffn_w2")[l]
    common["norm_g"] = f("norm_g").reshape(64, 128)
    common["even_ln_g"] = f("even_ln_g").reshape(1, 512)
    common["even_ln_b"] = f("even_ln_b").reshape(1, 512)
    common["even_spatial_w"] = f("even_spatial_w")[0]
    common["even_spatial_b"] = f("even_spatial_b").reshape(1, 512)
    common["even_pool_w"] = f("even_pool_w")[0]
    common["even_pool_scale"] = f("even_pool_scale").reshape(4, 128)
    common["odd_lambda"] = f("odd_lambda").reshape(1, 256)
    common["odd_subln_g"] = f("odd_subln_g").reshape(1, 128)
    common["odd_conv_w"] = f("odd_conv_w").reshape(12, 128)
    common["rel_bias_table"] = f("rel_bias_table")
    if "nc" not in _CACHE:
        _CACHE["nc"] = build()[0]
    nc = _CACHE["nc"]
    in_maps = []
    for b in range(n):
        m = dict(common)
        m["x"] = x[b]
        in_maps.append(m)
    res = run_bass_kernel_spmd(nc, in_maps, core_ids=list(range(n)))
    return np.stack([np.asarray(r["out"], dtype=np.float32) for r in res.results], axis=0)
```
